# Optimizing a Trainium2 kernel written in Bass

```python
import math
import jax, jax.numpy as jnp
from jax import lax
import numpy as np

D_MODEL = 2048
BATCH = 4
SEQ = 4096
DEPTH = 4

N_EVEN = (DEPTH + 1) // 2
N_ODD = DEPTH // 2

CONV_WIDTH = 3
CONV_GROUPS = 8
D_CONV = D_MODEL // 2

NSA_HEADS = 8
NSA_KV_GROUPS = 2
NSA_HPG = NSA_HEADS // NSA_KV_GROUPS
HEAD_DIM = 128
D_NSA = NSA_HEADS * HEAD_DIM
D_KV = NSA_KV_GROUPS * HEAD_DIM
CMP_BLOCK = 32
CMP_STRIDE = 16
SEL_BLOCK = 64
N_SELECT = 16
WINDOW = 512
N_BRANCH = 3
Q_BLOCK = 128

EVEN_SPLITS = (D_CONV, D_CONV, D_CONV, D_CONV, D_NSA, D_KV, D_KV, D_KV, D_KV, D_KV, D_KV, N_BRANCH * NSA_HEADS, D_NSA)
EVEN_IN = sum(EVEN_SPLITS)
D_EVEN_MIX = D_CONV + D_NSA

D_SGU = D_MODEL
SGU_GROUPS = 8
SGU_CHUNK = 128
SGU_GROUP_DIM = D_SGU // SGU_GROUPS

REL_BUCKETS = 32
REL_MAX_DIST = 128

DEEPNORM_ALPHA = (2 * DEPTH) ** 0.25
DEEPNORM_BETA = (8 * DEPTH) ** -0.25
LN_EPS = 1e-5
NEG_INF = -1e30
FORCED_SCORE = 1e9

kernel_name = "hybrid_conv_nsa_sgu_deepnorm"


def layer_norm(x, g, b):
    xf = x.astype(jnp.float32)
    mu = jnp.mean(xf, axis=-1, keepdims=True)
    var = jnp.mean(jnp.square(xf - mu), axis=-1, keepdims=True)
    return ((xf - mu) * lax.rsqrt(var + LN_EPS) * g + b).astype(x.dtype)


def t5_bucket(dist):
    n = jnp.maximum(dist, 0)
    max_exact = REL_BUCKETS // 2
    large = max_exact + (jnp.log(jnp.maximum(n, 1).astype(jnp.float32) / max_exact)
                         / math.log(REL_MAX_DIST / max_exact) * (REL_BUCKETS - max_exact)).astype(jnp.int32)
    large = jnp.minimum(large, REL_BUCKETS - 1)
    return jnp.where(n < max_exact, n, large)


def causal_short_conv(h, w):
    s = h.shape[1]
    hp = jnp.pad(h, ((0, 0), (CONV_WIDTH - 1, 0), (0, 0)))
    out = w[0] * hp[:, 0:s]
    for k in range(1, CONV_WIDTH):
        out = out + w[k] * hp[:, k:k + s]
    return out


def compress_blocks(tok, pos, w1, w2):
    s = tok.shape[2]
    n_cmp = (s - CMP_BLOCK) // CMP_STRIDE + 1
    idx = np.arange(n_cmp)[:, None] * CMP_STRIDE + np.arange(CMP_BLOCK)[None, :]
    blocks = tok[:, :, idx] + pos
    flat = blocks.reshape(blocks.shape[0], blocks.shape[1], n_cmp, CMP_BLOCK * HEAD_DIM)
    return jax.nn.silu(flat @ w1) @ w2


def nsa_attention(q, k_c, v_c, k_s, v_s, k_w, v_w, gates, rel_table, cmp_pos, cmp_w1, cmp_w2):
    bsz, s, _ = q.shape
    g_n, hpg, dh = NSA_KV_GROUPS, NSA_HPG, HEAD_DIM
    scale = dh ** -0.5
    q = q.reshape(bsz, s, g_n, hpg, dh).transpose(0, 2, 3, 1, 4)

    def kv(t):
        return t.reshape(bsz, s, g_n, dh).transpose(0, 2, 1, 3)

    table_g = rel_table.reshape(REL_BUCKETS, g_n, hpg)
    table_gt = jnp.transpose(table_g, (1, 0, 2))

    kc = compress_blocks(kv(k_c), cmp_pos[0], cmp_w1[0], cmp_w2[0])
    vc = compress_blocks(kv(v_c), cmp_pos[1], cmp_w1[1], cmp_w2[1])
    n_cmp = kc.shape[2]
    t = jnp.arange(s)
    cmp_end = jnp.arange(n_cmp) * CMP_STRIDE + CMP_BLOCK - 1
    dist_c = t[:, None] - cmp_end[None, :]
    valid_c = dist_c >= 0
    bias_c = jnp.transpose(table_g[t5_bucket(dist_c)], (2, 3, 0, 1))
    s_c = jnp.einsum('bghqd,bgkd->bghqk', q, kc).astype(jnp.float32) * scale + bias_c
    p_c = jax.nn.softmax(jnp.where(valid_c, s_c, NEG_INF), axis=-1)
    p_c = jnp.where(valid_c, p_c, 0.0)
    o_c = jnp.einsum('bghqk,bgkd->bghqd', p_c.astype(vc.dtype), vc)

    n_sel = s // SEL_BLOCK
    cstart = np.arange(n_cmp)[:, None] * CMP_STRIDE
    sstart = np.arange(n_sel)[None, :] * SEL_BLOCK
    overlap = jnp.asarray(((cstart < sstart + SEL_BLOCK) & (cstart + CMP_BLOCK > sstart)).astype(np.float32))
    imp = jnp.einsum('bghqk,kj->bgqj', p_c, overlap)
    blk = jnp.arange(n_sel)[None, :]
    cur = (t // SEL_BLOCK)[:, None]
    forced = (blk == 0) | (blk == cur) | (blk == cur - 1)
    imp = jnp.where(blk > cur, -1.0, jnp.where(forced, FORCED_SCORE, imp))
    n_top = min(N_SELECT, n_sel)
    _, sel_idx = lax.top_k(imp, n_top)
    n_keys = n_top * SEL_BLOCK

    k_sel_blocks = kv(k_s).reshape(bsz, g_n, n_sel, SEL_BLOCK, dh)
    v_sel_blocks = kv(v_s).reshape(bsz, g_n, n_sel, SEL_BLOCK, dh)
    gather = jax.vmap(jax.vmap(lambda blocks, idx: blocks[idx]))
    g_idx = jnp.arange(g_n)[None, :, None, None]

    k_win_pad = jnp.pad(kv(k_w), ((0, 0), (0, 0), (WINDOW, 0), (0, 0)))
    v_win_pad = jnp.pad(kv(v_w), ((0, 0), (0, 0), (WINDOW, 0), (0, 0)))
    qi = jnp.arange(Q_BLOCK)[:, None]
    kj = jnp.arange(WINDOW + Q_BLOCK)[None, :]
    rel_w = WINDOW + qi - kj
    band = (rel_w >= 0) & (rel_w < WINDOW)
    bias_w = jnp.transpose(table_g[t5_bucket(rel_w)], (2, 3, 0, 1))

    def block_step(q0):
        qb = lax.dynamic_slice_in_dim(q, q0, Q_BLOCK, axis=3)
        tq = q0 + jnp.arange(Q_BLOCK)
        idx = lax.dynamic_slice_in_dim(sel_idx, q0, Q_BLOCK, axis=2)
        ks = gather(k_sel_blocks, idx).reshape(bsz, g_n, Q_BLOCK, n_keys, dh)
        vs = gather(v_sel_blocks, idx).reshape(bsz, g_n, Q_BLOCK, n_keys, dh)
        kpos = (idx[..., None] * SEL_BLOCK + jnp.arange(SEL_BLOCK)).reshape(bsz, g_n, Q_BLOCK, n_keys)
        dist = tq[:, None] - kpos
        bias = jnp.moveaxis(table_gt[g_idx, t5_bucket(dist)], -1, 2)
        sc = jnp.einsum('bghqd,bgqkd->bghqk', qb, ks).astype(jnp.float32) * scale + bias
        p = jax.nn.softmax(jnp.where((dist >= 0)[:, :, None], sc, NEG_INF), axis=-1)
        o_s = jnp.einsum('bghqk,bgqkd->bghqd', p.astype(vs.dtype), vs)
        kw = lax.dynamic_slice_in_dim(k_win_pad, q0, WINDOW + Q_BLOCK, axis=2)
        vw = lax.dynamic_slice_in_dim(v_win_pad, q0, WINDOW + Q_BLOCK, axis=2)
        valid = band & (kj >= WINDOW - q0)
        sw = jnp.einsum('bghqd,bgkd->bghqk', qb, kw).astype(jnp.float32) * scale + bias_w
        pw = jax.nn.softmax(jnp.where(valid, sw, NEG_INF), axis=-1)
        o_w = jnp.einsum('bghqk,bgkd->bghqd', pw.astype(vw.dtype), vw)
        return o_s, o_w

    o_s, o_w = lax.map(block_step, jnp.arange(s // Q_BLOCK) * Q_BLOCK)
    o_s = jnp.moveaxis(o_s, 0, 3).reshape(bsz, g_n, hpg, s, dh)
    o_w = jnp.moveaxis(o_w, 0, 3).reshape(bsz, g_n, hpg, s, dh)

    gate = jax.nn.sigmoid(gates.astype(jnp.float32)).reshape(bsz, s, N_BRANCH, g_n, hpg)
    gate = jnp.transpose(gate, (2, 0, 3, 4, 1))[..., None].astype(o_c.dtype)
    o = gate[0] * o_c + gate[1] * o_s + gate[2] * o_w
    return o.transpose(0, 3, 1, 2, 4).reshape(bsz, s, D_NSA)


def even_layer(x, w_in, conv_w, cmp_pos, cmp_w1, cmp_w2, w_out, rel_table):
    h = x @ w_in
    offs = [int(o) for o in np.cumsum(EVEN_SPLITS)[:-1]]
    (a_h, a_b, a_c, a_z, q, k_c, v_c, k_s, v_s, k_w, v_w, gates, b_z) = jnp.split(h, offs, axis=-1)
    y_a = a_b * causal_short_conv(a_c * a_h, conv_w) * jax.nn.silu(a_z)
    y_b = nsa_attention(q, k_c, v_c, k_s, v_s, k_w, v_w, gates, rel_table, cmp_pos, cmp_w1, cmp_w2) * jax.nn.silu(b_z)
    return jnp.concatenate([y_a, y_b], axis=-1) @ w_out


def odd_layer(x, w_in, ln_g, ln_b, sgu_w, sgu_b, w_out):
    bsz, s, _ = x.shape
    h = x @ w_in
    uv = jax.nn.gelu(h[..., :2 * D_SGU])
    z = h[..., 2 * D_SGU:]
    u, v = jnp.split(uv, 2, axis=-1)
    v = layer_norm(v, ln_g, ln_b).reshape(bsz, s // SGU_CHUNK, SGU_CHUNK, SGU_GROUPS, SGU_GROUP_DIM)
    w_causal = sgu_w * jnp.tril(jnp.ones((SGU_CHUNK, SGU_CHUNK), sgu_w.dtype))
    mixed = jnp.einsum('gts,bnsgc->bntgc', w_causal, v) + sgu_b.T[:, :, None]
    y = u * mixed.reshape(bsz, s, D_SGU) * jax.nn.silu(z)
    return y @ w_out


def setup_inputs(seed: int = 0) -> dict:
    key = jax.random.key(seed)
    ks = jax.random.split(key, 16)

    def nrm(k, shape, scale):
        return jax.random.normal(k, shape, jnp.float32) * scale

    return {
        "x": nrm(ks[0], (BATCH, SEQ, D_MODEL), 1.0),
        "rel_bias_table": nrm(ks[1], (REL_BUCKETS, NSA_HEADS), 0.5),
        "ln_g": 1.0 + nrm(ks[2], (DEPTH, D_MODEL), 0.02),
        "ln_b": nrm(ks[3], (DEPTH, D_MODEL), 0.02),
        "ev_w_in": nrm(ks[4], (N_EVEN, D_MODEL, EVEN_IN), D_MODEL ** -0.5),
        "ev_conv_w": nrm(ks[5], (N_EVEN, CONV_WIDTH, D_CONV), CONV_WIDTH ** -0.5),
        "ev_cmp_pos": nrm(ks[6], (N_EVEN, 2, CMP_BLOCK, HEAD_DIM), 0.5),
        "ev_cmp_w1": nrm(ks[7], (N_EVEN, 2, CMP_BLOCK * HEAD_DIM, HEAD_DIM), (CMP_BLOCK * HEAD_DIM) ** -0.5),
        "ev_cmp_w2": nrm(ks[8], (N_EVEN, 2, HEAD_DIM, HEAD_DIM), HEAD_DIM ** -0.5),
        "ev_w_out": nrm(ks[9], (N_EVEN, D_EVEN_MIX, D_MODEL), D_EVEN_MIX ** -0.5 * DEEPNORM_BETA),
        "od_w_in": nrm(ks[10], (N_ODD, D_MODEL, 3 * D_SGU), D_MODEL ** -0.5),
        "od_ln_g": 1.0 + nrm(ks[11], (N_ODD, D_SGU), 0.02),
        "od_ln_b": nrm(ks[12], (N_ODD, D_SGU), 0.02),
        "od_sgu_w": nrm(ks[13], (N_ODD, SGU_GROUPS, SGU_CHUNK, SGU_CHUNK), SGU_CHUNK ** -0.5),
        "od_sgu_b": 1.0 + nrm(ks[14], (N_ODD, SGU_GROUPS, SGU_CHUNK), 0.02),
        "od_w_out": nrm(ks[15], (N_ODD, D_SGU, D_MODEL), D_SGU ** -0.5 * DEEPNORM_BETA),
    }


def reference(x, rel_bias_table, ln_g, ln_b, ev_w_in, ev_conv_w, ev_cmp_pos, ev_cmp_w1, ev_cmp_w2, ev_w_out,
              od_w_in, od_ln_g, od_ln_b, od_sgu_w, od_sgu_b, od_w_out):
    for layer in range(DEPTH):
        i = layer // 2
        if layer % 2 == 0:
            y = even_layer(x, ev_w_in[i], ev_conv_w[i], ev_cmp_pos[i], ev_cmp_w1[i], ev_cmp_w2[i],
                           ev_w_out[i], rel_bias_table)
        else:
            y = odd_layer(x, od_w_in[i], od_ln_g[i], od_ln_b[i], od_sgu_w[i], od_sgu_b[i], od_w_out[i])
        x = layer_norm(DEEPNORM_ALPHA * x + y, ln_g[layer], ln_b[layer])
    return x
```

```python
import math
import numpy as np
from contextlib import ExitStack
import concourse.bass as bass
import concourse.mybir as mybir

F32 = mybir.dt.float32
BF16 = mybir.dt.bfloat16
I32 = mybir.dt.int32
U32 = mybir.dt.uint32
AF = mybir.ActivationFunctionType
ALU = mybir.AluOpType
AX = mybir.AxisListType

ENGS = ("pe", "dve", "act", "pool", "sp")
N_DMA_SEMS = 24


class Tile:
    def __init__(self, name, ap_handle, nsub=1):
        self.name = name
        self.t = ap_handle
        self.nsub = nsub

    def __getitem__(self, idx):
        return self.t[idx]

    def k(self, i):
        assert 0 <= i < self.nsub
        return (self.name, i)

    def all(self):
        return [(self.name, i) for i in range(self.nsub)]


def _keys(lst):
    out = []
    for x in lst:
        if isinstance(x, Tile):
            out.extend(x.all())
        elif isinstance(x, list):
            out.extend(_keys(x))
        else:
            out.append(x)
    return out


def _dma_dbg(e, out, in_, kw):
    o = out() if callable(out) else out
    i = in_() if callable(in_) else in_
    try:
        return e.dma_start(out=o, in_=i, **kw)
    except Exception:
        print("DMA FAILED out=", o, " in=", i)
        raise


class KB:
    def __init__(self, nc, same_engine_sync=True):
        self.nc = nc
        self.ses = same_engine_sync
        self.stack = ExitStack()
        self.semstack = ExitStack()
        self.csem = None
        self.base = {e: 0 for e in ENGS}
        self.cc_uses = 0
        self.cc_last = None
        self.vals = {}
        self.phase_no = 0
        self.tag = None
        self.ops = {e: [] for e in ENGS}
        self.res = {}
        self.seen = {e: {} for e in ENGS}
        self.dma_last = [None] * N_DMA_SEMS
        self.dma_uses = [0] * N_DMA_SEMS
        self.dma_rr = 0
        self.ntiles = 0
        self.out_tokens = []
        self.excl = set()

    def sbuf(self, name, shape, dtype, nsub=1):
        t = self.stack.enter_context(self.nc.sbuf_tensor("sb%d_" % self.phase_no + name, list(shape), dtype))
        return Tile(name, t, nsub)

    def psum(self, name, shape, dtype, nsub=1):
        t = self.stack.enter_context(self.nc.psum_tensor("ps%d_" % self.phase_no + name, list(shape), dtype))
        self.excl.add(name)
        return Tile(name, t, nsub)

    def _need(self, eng, tok, waits):
        if tok is None:
            return
        kind, src, idx = tok
        if kind == "c" and src == eng and not self.ses:
            return
        if kind == "c" and src == eng and eng == "pe":
            return
        key = (kind, src)
        if self.seen[eng].get(key, -1) >= idx:
            return
        self.seen[eng][key] = idx
        waits.append(tok)

    def _deps(self, eng, r, w, tok):
        waits = []
        for key in _keys(r):
            ent = self.res.setdefault(key, [None, {}])
            self._need(eng, ent[0], waits)
            if isinstance(key, tuple) and key[0] in self.excl:
                for t in ent[1].values():
                    if not (t[0] == "c" and t[1] == eng):
                        self._need(eng, t, waits)
        for key in _keys(w):
            ent = self.res.setdefault(key, [None, {}])
            self._need(eng, ent[0], waits)
            for t in ent[1].values():
                self._need(eng, t, waits)
        for key in _keys(r):
            ent = self.res[key]
            ent[1][(tok[0], tok[1])] = tok
        for key in _keys(w):
            ent = self.res[key]
            ent[0] = tok
            ent[1] = {}
        return waits

    def op(self, eng, fn, r=(), w=(), tag=None):
        idx = len(self.ops[eng])
        tok = ("c", eng, idx)
        waits = self._deps(eng, list(r), list(w), tok)
        self.ops[eng].append({"fn": fn, "waits": waits, "tok": tok, "dma": None, "tag": tag or self.tag})
        return tok

    def raw(self, eng, fn):
        self.ops[eng].append({"fn": fn, "waits": [], "tok": None, "dma": "noinc"})

    def dma(self, eng, out, in_, r=(), w=(), is_output=False, **kw):
        s = self.dma_rr
        self.dma_rr = (self.dma_rr + 1) % N_DMA_SEMS
        pre = []
        self._need(eng, self.dma_last[s], pre)
        self.dma_uses[s] += 1
        tok = ("d", s, self.dma_uses[s])
        self.dma_last[s] = tok
        waits = pre + self._deps(eng, list(r), list(w), tok)
        self.ops[eng].append(
            {"fn": (lambda e: _dma_dbg(e, out, in_, kw)), "waits": waits, "tok": tok, "dma": s}
        )
        if is_output:
            self.out_tokens.append(tok)
        return tok

    def collective(self, kind, groups, in_ap, out_ap, r=(), w=(), serialize=True):
        pre = []
        if serialize:
            self._need("pool", self.cc_last, pre)
        self.cc_uses += 1
        tok = ("x", 0, self.cc_uses)
        self.cc_last = tok
        waits = pre + self._deps("pool", list(r), list(w), tok)
        self.ops["pool"].append(
            {"fn": (lambda e: e.collective_compute(kind, ALU.bypass, replica_groups=groups, ins=[in_ap], outs=[out_ap])),
             "waits": waits, "tok": tok, "dma": "cc"})
        return tok

    def _alloc_sems(self):
        if self.csem is None:
            nc = self.nc
            self.csem = {e: self.semstack.enter_context(nc.semaphore("c_" + e)) for e in ENGS}
            self.dsem = [self.semstack.enter_context(nc.semaphore("d_%d" % i)) for i in range(N_DMA_SEMS)]
            self.ccsem = self.semstack.enter_context(nc.semaphore("ccs"))

    def end_phase(self):
        self.emit(final=False)
        self.stack = ExitStack()
        for e in ENGS:
            self.base[e] += self.n_incs[e]
        self.ops = {e: [] for e in ENGS}
        self.res = {}
        self.seen = {e: {} for e in ENGS}
        self.out_tokens = []
        self.excl = set()
        self.phase_no += 1

    def finish(self):
        self.emit(final=True)
        self.semstack.close()

    def emit(self, final=True):
        nc = self.nc
        self._alloc_sems()
        last = {}
        for e in ENGS:
            for o in reversed(self.ops[e]):
                if o["fn"] is not None and o["dma"] is None:
                    last[e] = o["tok"]
                    break
        for e in ENGS:
            fin = []
            for t in self.out_tokens:
                self._need(e, t, fin)
            for s in range(N_DMA_SEMS):
                self._need(e, self.dma_last[s], fin)
            self._need(e, self.cc_last, fin)
            for e2 in ENGS:
                if e2 != e and e2 in last:
                    self._need(e, last[e2], fin)
            self.ops[e].append({"fn": None, "waits": fin, "tok": None, "dma": None})

        waited = {e: set() for e in ENGS}
        for e in ENGS:
            for o in self.ops[e]:
                for (kind, src, idx) in o["waits"]:
                    if kind == "c":
                        waited[src].add(idx)
        rank = {}
        for e in ENGS:
            for i, idx in enumerate(sorted(waited[e])):
                rank[(e, idx)] = self.base[e] + i + 1
        self.n_incs = {e: len(waited[e]) for e in ENGS}
        csem, dsem, ccsem = self.csem, self.dsem, self.ccsem

        def run(ename, e):
            for o in self.ops[ename]:
                for (kind, src, idx) in o["waits"]:
                    if kind == "c":
                        wi = e.wait_ge(csem[src], rank[(src, idx)])
                    elif kind == "x":
                        wi = e.wait_ge(ccsem, idx)
                    else:
                        wi = e.wait_ge(dsem[src], 16 * idx)
                    if o.get("tag") and wi is not None:
                        try:
                            wi.annotate("wait[%s]<-%s" % (o["tag"], src if kind == "c" else kind))
                        except Exception:
                            pass
                if o["fn"] is None:
                    continue
                ins = o["fn"](e)
                if o.get("tag") and ins is not None:
                    try:
                        ins.annotate(o["tag"])
                    except Exception:
                        pass
                if o["dma"] == "cc":
                    ins.then_inc(ccsem, 1)
                elif o["dma"] == "noinc":
                    pass
                elif o["dma"] is not None:
                    ins.then_inc(dsem[o["dma"]], 16)
                else:
                    _, en, idx = o["tok"]
                    if (en, idx) in rank:
                        ins.then_inc(csem[en], 1)

        with nc.Block() as block:
            @block.tensor
            def _(e):
                run("pe", e)

            @block.vector
            def _(e):
                run("dve", e)

            @block.scalar
            def _(e):
                run("act", e)

            @block.gpsimd
            def _(e):
                run("pool", e)

            @block.sync
            def _(e):
                run("sp", e)
        self.stack.close()
        if final:
            pass


NTOK = 2048
D = 2048
EVEN_IN = 7704
SCALE = 128 ** -0.5

GR = 3072
FM_JOBS = []
for j in range(4):
    for g in range(2):
        FM_JOBS.append((j * 1024 + g * 512, 512, [g * GR + j * 512 + c * 128 for c in range(4)], 1.0))
for g in range(2):
    FM_JOBS.append((4096 + g * 512, 512, [g * GR + 2048 + c * 128 for c in range(4)], SCALE))
FM_JOBS.append((5120, 512, [2560, GR + 2560, 2688, GR + 2688], 1.0))
FM_JOBS.append((5632, 256, [2816, GR + 2816], 1.0))
FM_JOBS.append((6144, 256, [2944, GR + 2944], 1.0))
NFM = 6144
GC = 780
TM_JOBS = [(5888, 256, [(0, 128, 0), (128, 128, GC)]), (6400, 256, [(0, 128, 128), (128, 128, GC + 128)]),
           (6656, 24, [(0, 12, 256), (12, 12, GC + 256)]), (6680, 512, [(0, 512, 268)]), (7192, 512, [(0, 512, GC + 268)])]
NTM = 1560


def build_A():
    nc = bass.Bass("TRN2", target_bir_lowering=False)
    x = nc.dram_tensor("x", [NTOK, D], F32, kind="ExternalInput").ap()
    w = nc.dram_tensor("w", [D, EVEN_IN], F32, kind="ExternalInput").ap()
    ident = nc.dram_tensor("ident", [128, 128], F32, kind="ExternalInput").ap()
    hfm = nc.dram_tensor("hfm", [NFM, NTOK], BF16, kind="ExternalOutput").ap()
    htm = nc.dram_tensor("htm", [NTOK, NTM], BF16, kind="ExternalOutput").ap()
    k = KB(nc)
    emit_A(k, x, w, ident, hfm, htm)
    k.finish()
    return nc


def load_xT(k, x, idb, xT, ntt, pT, name="xb"):
    xb = [k.sbuf("%s%d" % (name, i), [128, D], BF16) for i in range(2)]
    n = 0
    for tt in range(ntt):
        b = xb[tt % 2]
        k.dma("pool", b[:], x[tt * 128:(tt + 1) * 128, :], w=[b])
        for g in range(4):
            p = pT[n % len(pT)]
            for j in range(4):
                kc = g * 4 + j
                k.op("pe", lambda e, p=p, b=b, kc=kc, j=j: e.transpose(
                    p[:, j * 128:(j + 1) * 128], b[:, kc * 128:(kc + 1) * 128], idb[:]),
                    r=[b, idb], w=[p])
            eng = "dve" if n % 2 == 0 else "act"
            src = p[:, 0:512].rearrange("p (a b) -> p a b", a=4)
            dst = xT[:, g * 4:(g + 1) * 4, tt * 128:(tt + 1) * 128]
            if eng == "dve":
                k.op("dve", lambda e, dst=dst, src=src: e.tensor_copy(out=dst, in_=src), r=[p], w=[xT.k(tt)])
            else:
                k.op("act", lambda e, dst=dst, src=src: e.copy(out=dst, in_=src), r=[p], w=[xT.k(tt)])
            n += 1


def load_ident(k, ident):
    idf = k.sbuf("idf", [128, 128], F32)
    idb = k.sbuf("idb", [128, 128], BF16)
    k.dma("sp", idf[:], ident[:, :], w=[idf])
    k.op("dve", lambda e: e.tensor_copy(out=idb[:], in_=idf[:]), r=[idf], w=[idb])
    return idf, idb


def emit_A(k, x, w, ident, hfm, htm, gather=None, halo=None):
    idf, idb = load_ident(k, ident)
    xT = k.sbuf("xT", [128, 16, NTOK], BF16, nsub=16)
    pT = [k.psum("pT%d" % i, [128, 1024], BF16) for i in range(2)]
    pm = [k.psum("pm%d" % i, [128, 512], F32) for i in range(4)]
    load_xT(k, x, idb, xT, 16, pT)
    wb = [k.sbuf("wb%d" % i, [128, 16, 512], BF16) for i in range(3)]
    stg = [k.sbuf("stg%d" % i, [128, NTOK], BF16) for i in range(3)]
    wv = w.rearrange("(c p) n -> p c n", p=128)
    nw = 0
    npm = 0
    nst = 0
    nev = 0
    jobs = [("tm",) + j for j in TM_JOBS] + [("fm",) + j for j in FM_JOBS]
    pending = []
    done_rows = {}

    def flush(jidx):
        while pending and pending[0][0] <= jidx:
            pending.pop(0)[1]()

    def cc_fm(c):
        hg_fm, hg_tm, pairs = gather
        k.collective("AllGather", pairs, hfm[c * 512:(c + 1) * 512, :], hg_fm[c * 1024:(c + 1) * 1024, :],
                     r=[("hfm", 4 * c + i) for i in range(4)], w=["hg_fm"])

    def cc_tm(c):
        hg_fm, hg_tm, pairs = gather
        k.collective("AllGather", pairs, htm[c * 512:(c + 1) * 512, :], hg_tm[c * 1024:(c + 1) * 1024, :],
                     r=[("htm", c)], w=["hg_tm"])

    for jidx, job in enumerate(jobs):
        kind = job[0]
        col0, ncols = job[1], job[2]
        wt = wb[nw % 3]
        nw += 1
        k.dma("pool", wt[:, :, 0:ncols], wv[:, :, col0:col0 + ncols], w=[wt])
        flush(jidx)
        if kind == "fm":
            rows0, scale = job[3], job[4]
            for ch in range(ncols // 128):
                st = stg[nst % 3]
                nst += 1
                for tg in range(4):
                    p = pm[npm % 4]
                    npm += 1
                    for kc in range(16):
                        k.op("pe", lambda e, p=p, wt=wt, kc=kc, ch=ch, tg=tg: e.matmul(
                            p[:, :], lhsT=wt[:, kc, ch * 128:(ch + 1) * 128],
                            rhs=xT[:, kc, tg * 512:(tg + 1) * 512], start=(kc == 0), stop=(kc == 15)),
                            r=[wt] + [xT.k(t) for t in range(tg * 4, tg * 4 + 4)], w=[p])
                    dst = st[:, tg * 512:(tg + 1) * 512]
                    if nev % 2 == 0:
                        k.op("act", lambda e, dst=dst, p=p, scale=scale: e.activation(
                            out=dst, in_=p[:, :], func=AF.Copy, scale=float(scale)), r=[p], w=[st])
                    else:
                        k.op("dve", lambda e, dst=dst, p=p, scale=scale: e.tensor_scalar(
                            out=dst, in0=p[:, :], scalar1=float(scale), scalar2=None, op0=ALU.mult), r=[p], w=[st])
                    nev += 1
                r0 = rows0[ch]
                if isinstance(hfm, tuple):
                    dst_t = hfm[r0 // GR]
                    rr = r0 % GR
                    k.dma("sp", dst_t[rr:rr + 128, :], st[:, :], r=[st], w=[("hfm", r0 // 128)], is_output=True)
                else:
                    k.dma("sp", hfm[r0:r0 + 128, :], st[:, :], r=[st], w=[("hfm", r0 // 128)], is_output=True)
                if halo is not None and col0 < 4096 and (col0 // 1024) in (0, 2):
                    c8 = (r0 // GR) * 4 + ch
                    hr = (0 if col0 // 1024 == 0 else 1024) + c8 * 128
                    k.dma("sp", halo[hr:hr + 128, 0:2], st[:, 2046:2048], r=[st], is_output=True)
                c512 = r0 // 512
                done_rows[c512] = done_rows.get(c512, 0) + 1
                if gather is not None and done_rows[c512] == 4:
                    pending.append((jidx + 2, (lambda c=c512: cc_fm(c))))
        else:
            pieces = job[3]
            for tg in range(4):
                st = stg[nst % 3]
                nst += 1
                for t4 in range(4):
                    tt = tg * 4 + t4
                    p = pm[npm % 4]
                    npm += 1
                    for kc in range(16):
                        k.op("pe", lambda e, p=p, wt=wt, kc=kc, tt=tt, ncols=ncols: e.matmul(
                            p[:, 0:ncols], lhsT=xT[:, kc, tt * 128:(tt + 1) * 128],
                            rhs=wt[:, kc, 0:ncols], start=(kc == 0), stop=(kc == 15)),
                            r=[wt, xT.k(tt)], w=[p])
                    dst = st[:, t4 * 512:t4 * 512 + ncols]
                    if nev % 2 == 0:
                        k.op("act", lambda e, dst=dst, p=p, ncols=ncols: e.copy(out=dst, in_=p[:, 0:ncols]), r=[p], w=[st])
                    else:
                        k.op("dve", lambda e, dst=dst, p=p, ncols=ncols: e.tensor_copy(out=dst, in_=p[:, 0:ncols]), r=[p], w=[st])
                    nev += 1
                for (so, n_, ocol) in pieces:
                    src = st[:, :].rearrange("p (a b) -> p a b", a=4)[:, :, so:so + n_]
                    if isinstance(htm, tuple):
                        dst = htm[ocol // GC][tg * 512:(tg + 1) * 512, ocol % GC:ocol % GC + n_].rearrange("(a p) n -> p a n", p=128)
                    else:
                        dst = htm[tg * 512:(tg + 1) * 512, ocol:ocol + n_].rearrange("(a p) n -> p a n", p=128)
                    k.dma("sp", dst, src, r=[st, ("htm", tg)], w=[("htm", tg)], is_output=True)
            if gather is not None and jidx == len(TM_JOBS) - 1:
                for c in range(4):
                    pending.append((jidx + 2, (lambda c=c: cc_tm(c))))
    flush(10 ** 9)


D = 2048
ALPHA = 8 ** 0.25
LN_EPS = 1e-5


def ln_rows(k, z, outt, gB, bB, width, tagres, scr, aff_eng="dve"):
    st, mv = scr["st"], scr["mv"]
    nch = width // 512
    for c in range(nch):
        k.op("dve", lambda e, c=c: e.bn_stats(out=st[:, c, :], in_=z[:, c * 512:(c + 1) * 512]), r=[tagres], w=[st])
    k.op("dve", lambda e: e.bn_aggr(out=mv[:, 0:2], in_=st[:, 0:nch, :]), r=[st], w=[mv])
    k.op("act", lambda e: e.activation(out=mv[:, 2:3], in_=mv[:, 1:2], func=AF.Sqrt, bias=LN_EPS), r=[mv], w=[mv])
    k.op("dve", lambda e: e.reciprocal(out=mv[:, 3:4], in_=mv[:, 2:3]), r=[mv], w=[mv])
    if outt is not None and not isinstance(outt, Tile) and not (outt is z):
        oap, ores = outt
        k.op("dve", lambda e: e.tensor_scalar(out=oap, in0=z[:, 0:width], scalar1=mv[:, 0:1], scalar2=mv[:, 3:4],
                                              op0=ALU.subtract, op1=ALU.mult), r=[mv, tagres], w=[ores])
        return
    k.op("dve", lambda e: e.tensor_scalar(out=z[:, 0:width], in0=z[:, 0:width], scalar1=mv[:, 0:1], scalar2=mv[:, 3:4],
                                          op0=ALU.subtract, op1=ALU.mult), r=[mv, tagres], w=[tagres])
    if gB is not None:
        k.op(aff_eng, lambda e: e.tensor_tensor(out=z[:, 0:width], in0=z[:, 0:width], in1=gB[:, 0:width], op=ALU.mult),
             r=[tagres, gB], w=[tagres])


def emit_C(k, yT_d, x, w_out, g, b, xout, ntok=2048):
    ntt = ntok // 128
    wo = k.sbuf("wo", [128, 16, D], BF16, nsub=4)
    wv = w_out.rearrange("(c p) n -> p c n", p=128)
    for cb in range(4):
        k.dma("pool", wo[:, :, cb * 512:(cb + 1) * 512], wv[:, :, cb * 512:(cb + 1) * 512], w=[wo.k(cb)])
    yT = k.sbuf("yT", [128, 16, ntok], BF16, nsub=16)
    for kc in range(16):
        k.dma("sp", yT[:, kc, :], yT_d[kc * 128:(kc + 1) * 128, :], w=[yT.k(kc)])
    gB = k.sbuf("gB", [128, D], F32)
    bB = k.sbuf("bB", [128, D], F32)
    k.dma("sp", gB[:], g.partition_broadcast(128), w=[gB])
    k.dma("sp", bB[:], b.partition_broadcast(128), w=[bB])
    pm = [k.psum("pmc%d" % i, [128, 512], F32) for i in range(4)]
    xs = [k.sbuf("xsc%d" % i, [128, D], F32) for i in range(2)]
    zs = [k.sbuf("zsc%d" % i, [128, D], F32) for i in range(2)]
    scr = [{"st": k.sbuf("stc%d" % i, [128, 4, 6], F32), "mv": k.sbuf("mvc%d" % i, [128, 4], F32)} for i in range(2)]
    npm = 0
    for tt in range(ntt):
        xt = xs[tt % 2]
        z = zs[tt % 2]
        k.dma("sp", xt[:], x[tt * 128:(tt + 1) * 128, :], w=[xt])
        for cb in range(4):
            p = pm[npm % 4]
            npm += 1
            for kc in range(16):
                k.op("pe", lambda e, p=p, kc=kc, tt=tt, cb=cb: e.matmul(
                    p[:, :], lhsT=yT[:, kc, tt * 128:(tt + 1) * 128], rhs=wo[:, kc, cb * 512:(cb + 1) * 512],
                    start=(kc == 0), stop=(kc == 15)), r=[yT.k(kc), wo.k(cb)], w=[p])
            k.op("dve", lambda e, p=p, z=z, xt=xt, cb=cb: e.scalar_tensor_tensor(
                out=z[:, cb * 512:(cb + 1) * 512], in0=xt[:, cb * 512:(cb + 1) * 512], scalar=float(ALPHA),
                in1=p[:, :], op0=ALU.mult, op1=ALU.add), r=[p, xt], w=[z])
        ln_rows(k, z, z, gB, bB, D, z, scr[tt % 2])
        k.op("dve", lambda e, z=z: e.tensor_tensor(out=z[:, :], in0=z[:, :], in1=bB[:, :], op=ALU.add), r=[z, bB], w=[z])
        k.dma("sp", xout[tt * 128:(tt + 1) * 128, :], z[:, :], r=[z], is_output=True)


def build_C():
    nc = bass.Bass("TRN2", target_bir_lowering=False)
    yT = nc.dram_tensor("yT", [D, 2048], BF16, kind="ExternalInput").ap()
    x = nc.dram_tensor("x", [2048, D], F32, kind="ExternalInput").ap()
    w = nc.dram_tensor("w", [D, D], F32, kind="ExternalInput").ap()
    g = nc.dram_tensor("g", [D], F32, kind="ExternalInput").ap()
    b = nc.dram_tensor("b", [D], F32, kind="ExternalInput").ap()
    xout = nc.dram_tensor("xout", [2048, D], F32, kind="ExternalOutput").ap()
    k = KB(nc)
    emit_C(k, yT, x, w, g, b, xout)
    k.finish()
    return nc


def emit_D(k, x, w_in, og, ob, sgw_t, sgb, triu, w_out, g, b, ident, xout, ntok=2048):
    G = 512
    ngrp = ntok // G
    idf, idb = load_ident(k, ident)
    pT = [k.psum("pT%d" % i, [128, 1024], BF16) for i in range(2)]
    pm = [k.psum("pm%d" % i, [128, 512], F32) for i in range(6)]
    gB = k.sbuf("gB", [128, D], F32)
    bB = k.sbuf("bB", [128, D], F32)
    k.dma("sp", gB[:], g.partition_broadcast(128), w=[gB])
    k.dma("sp", bB[:], b.partition_broadcast(128), w=[bB])
    ogc = k.sbuf("ogc", [128, 16], F32)
    obc = k.sbuf("obc", [128, 16], F32)
    k.dma("sp", ogc[:], og.rearrange("(f p) -> p f", p=128), w=[ogc], allow_slow_non_contiguous=True)
    k.dma("sp", obc[:], ob.rearrange("(f p) -> p f", p=128), w=[obc], allow_slow_non_contiguous=True)
    wcf = k.sbuf("wcf", [128, 8, 128], F32)
    tri = k.sbuf("tri", [128, 128], F32)
    wct = k.sbuf("wct", [128, 8, 128], BF16)
    ones = k.sbuf("ones", [128, 128], BF16)
    k.op("dve", lambda e: e.memset(ones[:], 1.0), w=[ones])
    k.dma("sp", wcf[:], sgw_t.rearrange("g s t -> s g t"), w=[wcf])
    k.dma("sp", tri[:], triu[:, :], w=[tri])
    for gq in range(8):
        k.op("dve", lambda e, gq=gq: e.tensor_tensor(out=wct[:, gq, :], in0=wcf[:, gq, :], in1=tri[:, :], op=ALU.mult),
             r=[wcf, tri], w=[wct])
    sb1 = k.sbuf("sb1", [128, 8, 128], F32)
    k.dma("sp", sb1[:].rearrange("p g t -> p (g t)"), sgb.partition_broadcast(128), w=[sb1])
    bias2 = k.sbuf("bias2", [128, 16, 128], F32)
    for half in range(2):
        p = pm[half]
        k.op("pe", lambda e, p=p, half=half: e.matmul(
            p[:, :], lhsT=ones[:, :], rhs=wct[:, half * 4:(half + 1) * 4, :].rearrange("p a b -> p (a b)"),
            start=True, stop=True), r=[ones, wct], w=[p])
        for f in range(half * 8, half * 8 + 8):
            gq = f // 2
            gl = gq - half * 4
            k.op("dve", lambda e, p=p, f=f, gq=gq, gl=gl: e.scalar_tensor_tensor(
                out=bias2[:, f, :], in0=p[:, gl * 128:(gl + 1) * 128], scalar=obc[:, f:f + 1], in1=sb1[:, gq, :],
                op0=ALU.mult, op1=ALU.add), r=[p, obc, sb1], w=[bias2])

    wb = [k.sbuf("wb%d" % i, [128, 16, 512], BF16) for i in range(3)]
    xT = k.sbuf("xT", [128, 16, G], BF16, nsub=4)
    zs = k.sbuf("zso", [128, 4, D], F32, nsub=4)
    vg = k.sbuf("vgo", [128, 4, D], BF16, nsub=4)
    vn = k.sbuf("vn", [128, 4, D], BF16, nsub=4)
    yT = k.sbuf("yTo", [128, 16, G], BF16, nsub=16)
    xs = [k.sbuf("xso%d" % i, [128, D], F32) for i in range(1)]
    gu = [k.sbuf("gu%d" % i, [128, 512], F32) for i in range(2)]
    sz = [k.sbuf("sz%d" % i, [128, 512], F32) for i in range(2)]
    mm = [k.sbuf("mm%d" % i, [128, 512], F32) for i in range(1)]
    scr = [{"st": k.sbuf("sto%d" % i, [128, 4, 6], F32), "mv": k.sbuf("mvo%d" % i, [128, 4], F32)} for i in range(2)]
    wv = w_in.rearrange("(c p) n -> p c n", p=128)
    wov = w_out.rearrange("(c p) n -> p c n", p=128)
    cnt = {"w": 0, "p": 0, "e": 0, "s": 0, "x": 0}

    def nextw():
        t = wb[cnt["w"] % 3]
        cnt["w"] += 1
        return t

    def nextp():
        t = pm[cnt["p"] % 6]
        cnt["p"] += 1
        return t

    deferred = []
    for grp in range(ngrp):
        t0 = grp * G
        if grp == 0:
            k.tag = "xT"
            _load_xT_grp(k, x, t0, idb, xT, pT)
        k.tag = "p1"
        for cb in range(4):
            wt = nextw()
            c0 = 2048 + cb * 512
            k.dma("pool", wt[:], wv[:, :, c0:c0 + 512], w=[wt])
            for t4 in range(4):
                p = nextp()
                for kc in range(16):
                    k.op("pe", lambda e, p=p, wt=wt, kc=kc, t4=t4: e.matmul(
                        p[:, :], lhsT=xT[:, kc, t4 * 128:(t4 + 1) * 128], rhs=wt[:, kc, :],
                        start=(kc == 0), stop=(kc == 15)), r=[wt, xT.k(t4)], w=[p])
                k.op("act", lambda e, p=p, t4=t4, cb=cb: e.activation(
                    out=vg[:, t4, cb * 512:(cb + 1) * 512], in_=p[:, :], func=AF.Gelu_apprx_tanh), r=[p], w=[vg.k(t4)])
        k.tag = "ln1"
        for t4 in range(4):
            s = scr[cnt["s"] % 2]
            cnt["s"] += 1
            ln_rows(k, vg[:, t4, :], (vn[:, t4, :], vn.k(t4)), None, None, D, vg.k(t4), s)
        k.tag = "p2"
        for cb8 in range(8):
            if cb8 % 2 == 1 and deferred:
                deferred.pop(0)()
                k.tag = "p2"
            wt = nextw()
            k.dma("pool", wt[:, :, 0:256], wv[:, :, cb8 * 256:(cb8 + 1) * 256], w=[wt])
            k.dma("pool", wt[:, :, 256:512], wv[:, :, 4096 + cb8 * 256:4096 + (cb8 + 1) * 256], w=[wt])
            for fc in range(2):
                f = cb8 * 2 + fc
                gq = f // 2
                pu = nextp()
                pz = nextp()
                px = nextp()
                def mm_u():
                    for kc in range(16):
                        k.op("pe", lambda e, p=pu, wt=wt, kc=kc, fc=fc: e.matmul(
                            p[:, :], lhsT=wt[:, kc, fc * 128:(fc + 1) * 128], rhs=xT[:, kc, :],
                            start=(kc == 0), stop=(kc == 15)), r=[wt, xT], w=[pu])

                def mm_z():
                    for kc in range(16):
                        k.op("pe", lambda e, p=pz, wt=wt, kc=kc, fc=fc: e.matmul(
                            p[:, :], lhsT=wt[:, kc, 256 + fc * 128:256 + (fc + 1) * 128], rhs=xT[:, kc, :],
                            start=(kc == 0), stop=(kc == 15)), r=[wt, xT], w=[pz])
                if f % 2 == 0:
                    mm_u()
                    mm_z()
                else:
                    mm_z()
                    mm_u()
                for t4 in range(4):
                    k.op("pe", lambda e, p=px, t4=t4, f=f, gq=gq: e.matmul(
                        p[:, t4 * 128:(t4 + 1) * 128], lhsT=vn[:, t4, f * 128:(f + 1) * 128], rhs=wct[:, gq, :],
                        start=True, stop=True), r=[vn.k(t4), wct], w=[px])
                i = cnt["e"] % 2
                cnt["e"] += 1
                g_, s_, m_ = gu[i], sz[i], mm[0]
                def act_u():
                    k.op("act", lambda e, g_=g_, pu=pu: e.activation(out=g_[:], in_=pu[:, :], func=AF.Gelu_apprx_tanh), r=[pu], w=[g_])

                def act_z():
                    k.op("act", lambda e, s_=s_, pz=pz: e.activation(out=s_[:], in_=pz[:, :], func=AF.Silu), r=[pz], w=[s_])
                if f % 2 == 0:
                    act_u()
                    act_z()
                else:
                    act_z()
                    act_u()
                for t4 in range(4):
                    k.op("dve", lambda e, m_=m_, px=px, f=f, t4=t4: e.scalar_tensor_tensor(
                        out=m_[:, t4 * 128:(t4 + 1) * 128], in0=px[:, t4 * 128:(t4 + 1) * 128], scalar=ogc[:, f:f + 1],
                        in1=bias2[:, f, :], op0=ALU.mult, op1=ALU.add), r=[px, ogc, bias2], w=[m_])
                k.op("dve", lambda e, m_=m_, g_=g_: e.tensor_tensor(out=m_[:], in0=m_[:], in1=g_[:], op=ALU.mult),
                     r=[m_, g_], w=[m_])
                k.op("dve", lambda e, m_=m_, s_=s_, f=f: e.tensor_tensor(out=yT[:, f, :], in0=m_[:], in1=s_[:], op=ALU.mult),
                     r=[m_, s_], w=[yT.k(f)])
        if grp + 1 < ngrp:
            k.tag = "xT"
            _load_xT_grp(k, x, t0 + G, idb, xT, pT)
        k.tag = "p3"
        for cb in range(4):
            wt = nextw()
            k.dma("pool", wt[:], wov[:, :, cb * 512:(cb + 1) * 512], w=[wt])
            for t4 in range(4):
                p = nextp()
                for kc in range(16):
                    k.op("pe", lambda e, p=p, wt=wt, kc=kc, t4=t4: e.matmul(
                        p[:, :], lhsT=yT[:, kc, t4 * 128:(t4 + 1) * 128], rhs=wt[:, kc, :],
                        start=(kc == 0), stop=(kc == 15)), r=[wt, yT.k(kc)], w=[p])
                k.op("act", lambda e, p=p, t4=t4, cb=cb: e.copy(out=zs[:, t4, cb * 512:(cb + 1) * 512], in_=p[:, :]),
                     r=[p], w=[zs.k(t4)])
        def ln3(t4, t0=t0):
            k.tag = "ln3"
            if True:
                xt = xs[0]
                cnt["x"] += 1
                k.dma("sp", xt[:], x[t0 + t4 * 128:t0 + (t4 + 1) * 128, :], w=[xt])
                k.op("dve", lambda e, t4=t4, xt=xt: e.scalar_tensor_tensor(
                    out=zs[:, t4, :], in0=xt[:, :], scalar=float(ALPHA), in1=zs[:, t4, :], op0=ALU.mult, op1=ALU.add),
                    r=[xt, zs.k(t4)], w=[zs.k(t4)])
                s = scr[cnt["s"] % 2]
                cnt["s"] += 1
                ln_rows(k, zs[:, t4, :], None, gB, bB, D, zs.k(t4), s)
                k.op("dve", lambda e, t4=t4: e.tensor_tensor(out=zs[:, t4, :], in0=zs[:, t4, :], in1=bB[:, :], op=ALU.add),
                     r=[zs.k(t4), bB], w=[zs.k(t4)])
                k.dma("sp", xout[t0 + t4 * 128:t0 + (t4 + 1) * 128, :], zs[:, t4, :], r=[zs.k(t4)], is_output=True)
        for t4_ in range(4):
            deferred.append(lambda t4_=t4_, ln3=ln3: ln3(t4_))
    while deferred:
        deferred.pop(0)()


_xb_tiles = {}


def _load_xT_grp(k, x, t0, idb, xT, pT):
    if (id(k), k.phase_no) not in _xb_tiles:
        _xb_tiles[(id(k), k.phase_no)] = [k.sbuf("xbo%d" % i, [128, D], BF16) for i in range(2)]
    xb = _xb_tiles[(id(k), k.phase_no)]
    n = 0
    for t4 in range(4):
        b = xb[t4 % 2]
        k.dma("pool", b[:], x[t0 + t4 * 128:t0 + (t4 + 1) * 128, :], w=[b])
        for g in range(4):
            p = pT[n % len(pT)]
            for j in range(4):
                kc = g * 4 + j
                k.op("pe", lambda e, p=p, b=b, kc=kc, j=j: e.transpose(
                    p[:, j * 128:(j + 1) * 128], b[:, kc * 128:(kc + 1) * 128], idb[:]), r=[b, idb], w=[p])
            src = p[:, 0:512].rearrange("p (a b) -> p a b", a=4)
            dst = xT[:, g * 4:(g + 1) * 4, t4 * 128:(t4 + 1) * 128]
            if n % 2 == 0:
                k.op("dve", lambda e, dst=dst, src=src: e.tensor_copy(out=dst, in_=src), r=[p], w=[xT.k(t4)])
            else:
                k.op("act", lambda e, dst=dst, src=src: e.copy(out=dst, in_=src), r=[p], w=[xT.k(t4)])
            n += 1


def build_D(ntok=2048):
    nc = bass.Bass("TRN2", target_bir_lowering=False)
    x = nc.dram_tensor("x", [ntok, D], F32, kind="ExternalInput").ap()
    w_in = nc.dram_tensor("w_in", [D, 3 * D], F32, kind="ExternalInput").ap()
    og = nc.dram_tensor("og", [D], F32, kind="ExternalInput").ap()
    ob = nc.dram_tensor("ob", [D], F32, kind="ExternalInput").ap()
    sgw_t = nc.dram_tensor("sgw_t", [8, 128, 128], F32, kind="ExternalInput").ap()
    sgb = nc.dram_tensor("sgb", [1024], F32, kind="ExternalInput").ap()
    triu = nc.dram_tensor("triu", [128, 128], F32, kind="ExternalInput").ap()
    w_out = nc.dram_tensor("w_out", [D, D], F32, kind="ExternalInput").ap()
    g = nc.dram_tensor("g", [D], F32, kind="ExternalInput").ap()
    b = nc.dram_tensor("b", [D], F32, kind="ExternalInput").ap()
    ident = nc.dram_tensor("ident", [128, 128], F32, kind="ExternalInput").ap()
    xout = nc.dram_tensor("xout", [ntok, D], F32, kind="ExternalOutput").ap()
    k = KB(nc)
    emit_D(k, x, w_in, og, ob, sgw_t, sgb, triu, w_out, g, b, ident, xout, ntok)
    k.finish()
    return nc

import math

S_LEN = 4096
NQT = 32
NEG = -30000.0
DMIN = -2063
NL = 4608
SIMDBG = False
SUB = 99
DEPTH = 2


def t5_bucket_np(d):
    n = np.maximum(d, 0)
    large = 16 + (np.log(np.maximum(n, 1).astype(np.float32) / 16) / math.log(128 / 16) * 16).astype(np.int32)
    large = np.minimum(large, 31)
    return np.where(n < 16, n, large)


def host_consts_B():
    c = {}
    c["ident"] = np.eye(128, dtype=np.float32)
    c["jflip"] = np.ascontiguousarray(np.eye(128, dtype=np.float32)[::-1])
    d = DMIN + np.arange(NL)
    oh = np.zeros((33, NL), np.float32)
    bk = t5_bucket_np(d)
    oh[bk[d >= 0], np.nonzero(d >= 0)[0]] = 1.0
    oh[32, d < 0] = 1.0
    c["onehot"] = oh
    es = np.zeros((64, 32, 128), np.float32)
    for cc in range(32):
        es[2 * cc, cc, 0:64] = 1.0
        es[2 * cc + 1, cc, 64:128] = 1.0
    c["esel"] = es.reshape(64, 4096)
    n = np.arange(256)[:, None]
    j = np.arange(64)[None, :]
    ov = ((16 * n < 64 * j + 64) & (16 * n + 32 > 64 * j)).astype(np.float32)
    ovx = np.concatenate([ov, np.ones((256, 1), np.float32)], axis=1)
    ovx[255, :] = 0.0
    c["ovx"] = ovx
    keep = np.zeros((32, 128, 64), np.float32)
    fix = np.zeros((32, 128, 64), np.float32)
    for qt in range(32):
        t = qt * 128 + np.arange(128)
        cur = (t // 64)[:, None]
        blk = np.arange(64)[None, :]
        forced = (blk == 0) | (blk == cur) | (blk == cur - 1)
        fut = blk > cur
        keep[qt] = (~forced & ~fut).astype(np.float32)
        f = np.zeros((128, 64), np.float32)
        f = np.where(blk == cur - 1, 1e9, f)
        f = np.where(blk == cur, 2e9, f)
        f = np.where(blk == 0, 3e9, f)
        f = np.where(fut, -1.0 - 0.001 * blk, f)
        fix[qt] = f
    c["keep"] = keep
    c["fix"] = fix
    qi = np.arange(128)[None, :]
    ki = np.arange(128)[:, None]
    bw4 = np.where(qi < ki, 0.0, NEG).astype(np.float32)
    c["bw4"] = np.tile(bw4, (1, 4))
    return c


CONST_SHAPES = {"ident": [128, 128], "jflip": [128, 128], "onehot": [33, NL], "esel": [64, 4096], "ovx": [256, 65],
                "keep": [32, 128, 64], "fix": [32, 128, 64], "bw4": [128, 512]}


def emit_B(k, nc, I, yT_d, stage=99, dbg=None, lut_name="lutd"):
    idf, idb = load_ident(k, I["ident"])
    S = [k.psum("S%d" % i, [128, 512], F32) for i in range(2)]
    OA = [k.psum("OA%d" % i, [128, 512], F32) for i in range(2)]
    OS = [k.psum("OS%d" % i, [128, 512], F32) for i in range(2)]
    PT = k.psum("PT", [128, 1024], BF16)
    PX = k.psum("PX", [128, 512], F32)

    cin = I.get("cin")
    cw = k.sbuf("cw", [128, 4, 3], F32)
    if cin is not None:
        k.dma("sp", cw[:], I["conv_w"].rearrange("(c p) k -> p c k", p=128), w=[cw])
    TP = 512
    cb_ = {n: [k.sbuf("cv_%s%d" % (n, i), [128, TP + 2], BF16) for i in range(2)] for n in ("h", "b", "c", "z")}
    pbuf = [k.sbuf("cv_p%d" % i, [128, TP + 2], F32) for i in range(2)]
    abuf = [k.sbuf("cv_a%d" % i, [128, TP], F32) for i in range(2)]
    zbuf = [k.sbuf("cv_s%d" % i, [128, TP], F32) for i in range(2)]
    ybuf = [k.sbuf("cv_y%d" % i, [128, TP], BF16) for i in range(2)]
    cin = I.get("cin")
    conv_jobs = []
    it = 0
    for cc in range(4):
        for tp in range(S_LEN // TP):
            conv_jobs.append((cc, tp, it % 2))
            it += 1
    ebuf = [k.sbuf("cv_e%d" % i, [128, TP], F32) for i in range(2)]

    def conv_piece(cc, tp, i):
        k.tag = "conv"
        t0 = tp * TP
        bh, bb, bc, bz = cb_["h"][i], cb_["b"][i], cb_["c"][i], cb_["z"][i]
        rows = slice(cc * 128, (cc + 1) * 128)
        if tp == 0:
            k.op("pool", lambda e, bh=bh: e.memset(bh[:, 0:2], 0.0), w=[bh])
            k.op("pool", lambda e, bc=bc: e.memset(bc[:, 0:2], 0.0), w=[bc])
            k.dma("sp", bh[:, 2:], cin[0, rows, 0:TP], w=[bh])
            k.dma("sp", bc[:, 2:], cin[2, rows, 0:TP], w=[bc])
        else:
            k.dma("sp", bh[:, :], cin[0, rows, t0 - 2:t0 + TP], w=[bh])
            k.dma("sp", bc[:, :], cin[2, rows, t0 - 2:t0 + TP], w=[bc])
        k.dma("sp", bb[:, 0:TP], cin[1, rows, t0:t0 + TP], w=[bb])
        k.dma("sp", bz[:, 0:TP], cin[3, rows, t0:t0 + TP], w=[bz])
        p, a, sz, y, ex = pbuf[i], abuf[i], zbuf[i], ybuf[i], ebuf[i]
        k.op("dve", lambda e: e.tensor_tensor(out=p[:], in0=bc[:], in1=bh[:], op=ALU.mult), r=[bc, bh], w=[p])
        k.op("dve", lambda e: e.tensor_scalar(out=a[:], in0=p[:, 2:TP + 2], scalar1=cw[:, cc, 2:3], scalar2=None, op0=ALU.mult), r=[p, cw], w=[a])
        k.op("dve", lambda e: e.scalar_tensor_tensor(out=a[:], in0=p[:, 1:TP + 1], scalar=cw[:, cc, 1:2], in1=a[:], op0=ALU.mult, op1=ALU.add), r=[p, cw, a], w=[a])
        k.op("dve", lambda e: e.scalar_tensor_tensor(out=a[:], in0=p[:, 0:TP], scalar=cw[:, cc, 0:1], in1=a[:], op0=ALU.mult, op1=ALU.add), r=[p, cw, a], w=[a])
        k.op("act", lambda e: e.activation(out=sz[:], in_=bz[:, 0:TP], func=AF.Silu), r=[bz], w=[sz])
        k.op("pool", lambda e: e.tensor_tensor(out=a[:], in0=a[:], in1=bb[:, 0:TP], op=ALU.mult), r=[a, bb], w=[a])
        k.op("pool", lambda e: e.tensor_tensor(out=y[:], in0=a[:], in1=sz[:], op=ALU.mult), r=[a, sz], w=[y])
        k.dma("sp", yT_d[rows, t0:t0 + TP], y[:], r=[y], is_output=True)

    while conv_jobs and cin is not None:
        conv_piece(*conv_jobs.pop(0))
    if stage <= 1:
        return
    qT4 = k.sbuf("qT4", [128, NQT, 4, 128], BF16)
    for h in range(4):
        k.dma("sp", qT4[:, :, h, :], I["qT"][h].rearrange("d (t q) -> d t q", q=128), w=[qT4])
    kin = {}
    for n in ("kc", "vc", "ks", "kw"):
        kin[n] = k.sbuf("in_" + n, [128, S_LEN], BF16)
        k.dma("sp", kin[n][:], I[n + "T"][:, :], w=[kin[n]])
    vs_ext = k.sbuf("vs_ext", [128, 32, 129], BF16)
    vw_ext = k.sbuf("vw_ext", [128, 32, 129], BF16)
    for t, n in ((vs_ext, "vs"), (vw_ext, "vw")):
        k.dma("sp", t[:, :, 0:128], I[n].rearrange("(c p) d -> p c d", p=128), w=[t])
        k.op("pool", lambda e, t=t: e.memset(t[:, :, 128:129], 1.0), w=[t])
    esel = k.sbuf("esel", [64, 4096], BF16)
    k.dma(("sp" if SIMDBG else "pool"), esel[:], I["esel"][:, :], w=[esel])
    jfl = k.sbuf("jfl", [128, 128], BF16)
    k.dma(("sp" if SIMDBG else "pool"), jfl[:], I["jflip"][:, :], w=[jfl])
    bw4 = k.sbuf("bw4", [128, 512], BF16)
    k.dma(("sp" if SIMDBG else "pool"), bw4[:], I["bw4"][:, :], w=[bw4])
    graw = k.sbuf("graw", [128, NQT, 12], BF16)
    k.dma("sp", graw[:], I["gates"].rearrange("(t p) c -> p t c", p=128), w=[graw])
    gs = k.sbuf("gs", [128, NQT, 12], F32)
    k.op("act", lambda e: e.activation(out=gs[:], in_=graw[:], func=AF.Exp, scale=-1.0), r=[graw], w=[gs])
    k.op("dve", lambda e: e.tensor_scalar(out=gs[:], in0=gs[:], scalar1=1.0, scalar2=None, op0=ALU.add), r=[gs], w=[gs])
    k.op("dve", lambda e: e.reciprocal(out=gs[:], in_=gs[:]), r=[gs], w=[gs])

    if stage <= 1.2:
        return
    tab = k.sbuf("tab", [33, 4], F32)
    t31 = k.sbuf("t31", [32, 4], F32)
    tabb = k.sbuf("tabb", [33, 4], BF16)
    k.dma("sp", tab[0:32, :], I["table"][:, :], w=[tab])
    k.dma("sp", t31[:], I["table31"][:, :], w=[t31])
    k.op("dve", lambda e: e.memset(tabb[:], NEG), w=[tabb])
    k.op("dve", lambda e: e.tensor_tensor(out=tabb[0:32, :], in0=tab[0:32, :], in1=t31[:], op=ALU.subtract), r=[tab, t31, tabb], w=[tabb])
    oh = k.sbuf("oh", [33, NL], BF16)
    k.dma(("sp" if SIMDBG else "pool"), oh[:], I["onehot"][:, :], w=[oh])
    lutd = nc.dram_tensor(lut_name, [4, NL], F32).ap()
    lst = [k.sbuf("lst%d" % i, [4, 512], F32) for i in range(2)]
    for i in range(NL // 512):
        k.op("pe", lambda e, i=i: e.matmul(PX[0:4, :], lhsT=tabb[:, :], rhs=oh[:, i * 512:(i + 1) * 512], start=True, stop=True),
             r=[tabb, oh], w=[PX])
        st_ = lst[i % 2]
        k.op("dve", lambda e, st_=st_: e.tensor_copy(out=st_[:, :], in_=PX[0:4, :]), r=[PX], w=[st_])
        k.dma("sp", lutd[:, i * 512:(i + 1) * 512], st_[:, :], r=[st_], w=["lutd"])
    if stage <= 1.5:
        return
    BC = k.sbuf("BC", [128, 17, 512], BF16)
    BD = k.sbuf("BD", [128, 2, 512], BF16)
    hkf = [k.sbuf("hkf%d" % i, [128, 4, 128], F32) for i in range(2)]
    hk = [k.sbuf("hk%d" % i, [128, 4, 128], BF16) for i in range(2)]
    specs = []
    for dl in range(17):
        specs.append((BC[:, dl, :], 128 * dl - 31 - 2032, 16))
    specs.append((BD[:, 0, :], 0 - 127, 1))
    specs.append((BD[:, 1, :], 128 - 127, 1))
    for i, (dst, c0, pstr) in enumerate(specs):
        hh = hk[i % 2]
        src = bass.AP(lutd.tensor, c0 - DMIN, [[pstr, 128], [NL, 4], [1, 128]])
        hf = hkf[i % 2]
        k.dma("sp", hf[:], src, r=["lutd"], w=[hf])
        k.op("pool", lambda e, hh=hh, hf=hf: e.tensor_copy(out=hh[:], in_=hf[:]), r=[hf], w=[hh])
        k.op("pe", lambda e, hh=hh: e.matmul(PX[:, :], lhsT=jfl[:, :], rhs=hh[:].rearrange("p a b -> p (a b)"), start=True, stop=True),
             r=[jfl, hh], w=[PX])
        dres = BC if i < 17 else BD
        k.op("dve", lambda e, dst=dst: e.tensor_copy(out=dst, in_=PX[:, :]), r=[PX], w=[dres])

    if stage <= 2:
        dt = k.sbuf("dbgt", [128, 2048], F32)
        k.op("dve", lambda e: e.tensor_copy(out=dt[:, 0:512], in_=BD[:, 0, :]), r=[BD], w=[dt])
        k.op("dve", lambda e: e.tensor_copy(out=dt[:, 512:1024], in_=BD[:, 1, :]), r=[BD], w=[dt])
        k.op("dve", lambda e: e.tensor_copy(out=dt[:, 1024:1536], in_=BC[:, 0, :]), r=[BC], w=[dt])
        k.op("dve", lambda e: e.tensor_copy(out=dt[:, 1536:2048], in_=BC[:, 16, :]), r=[BC], w=[dt])
        k.dma("sp", dbg[:, 0:2048], dt[:], r=[dt], is_output=True)
        return
    vc_ext = k.sbuf("vc_ext", [128, 2, 193], BF16)
    k.dma(("sp" if SIMDBG else "pool"), vc_ext[:, :, 128:193], I["ovx"].rearrange("(c p) j -> p c j", p=128), w=[vc_ext])
    kcT = k.sbuf("kcT", [128, 256], BF16)
    w1 = k.sbuf("w1", [128, 32, 128], BF16)
    w2 = k.sbuf("w2", [128, 128], BF16)
    posf = k.sbuf("posf", [128, 32], F32)
    posb = k.sbuf("posb", [128, 32], BF16)
    pb = k.sbuf("pb", [128, 1], F32)
    hidT = k.sbuf("hidT", [128, 256], BF16)
    for kv, src in ((0, kin["kc"]), (1, kin["vc"])):
        k.dma(("sp" if SIMDBG else "pool"), w1[:], I["cmp_w1"][kv].rearrange("(l d) h -> d l h", d=128), w=[w1])
        k.dma(("sp" if SIMDBG else "pool"), w2[:], I["cmp_w2"][kv], w=[w2])
        k.dma("sp", posf[:], I["cmp_pos"][kv].rearrange("l d -> d l"), w=[posf], allow_slow_non_contiguous=True)
        k.op("dve", lambda e: e.tensor_copy(out=posb[:], in_=posf[:]), r=[posf], w=[posb])
        for l in range(32):
            k.op("pe", lambda e, l=l: e.matmul(PX[:, 0:1], lhsT=w1[:, l, :], rhs=posb[:, l:l + 1], start=(l == 0), stop=(l == 31)),
                 r=[w1, posb], w=[PX])
        k.op("dve", lambda e: e.tensor_copy(out=pb[:], in_=PX[:, 0:1]), r=[PX], w=[pb])
        for l in range(32):
            k.op("pe", lambda e, l=l, src=src: e.matmul(S[0][:, 0:255], lhsT=w1[:, l, :], rhs=src[:, l:l + 16 * 254 + 1:16],
                                                        start=(l == 0), stop=(l == 31)), r=[w1, src], w=[S[0]])
        k.op("dve", lambda e: e.memset(hidT[:], 0.0), w=[hidT])
        k.op("act", lambda e: e.activation(out=hidT[:, 0:255], in_=S[0][:, 0:255], func=AF.Silu, bias=pb[:, 0:1]), r=[S[0], pb], w=[hidT])
        if kv == 0:
            k.op("pe", lambda e: e.matmul(S[1][:, 0:256], lhsT=w2[:, :], rhs=hidT[:, :], start=True, stop=True), r=[w2, hidT], w=[S[1]])
            k.op("dve", lambda e: e.tensor_copy(out=kcT[:], in_=S[1][:, 0:256]), r=[S[1]], w=[kcT])
        else:
            for c in range(2):
                k.op("pe", lambda e, c=c: e.matmul(S[1][:, c * 128:(c + 1) * 128], lhsT=hidT[:, c * 128:(c + 1) * 128], rhs=w2[:, :],
                                                   start=True, stop=True), r=[w2, hidT], w=[S[1]])
            k.op("dve", lambda e: e.tensor_copy(out=vc_ext[:, :, 0:128], in_=S[1][:, 0:256].rearrange("p (c d) -> p c d", c=2)),
                 r=[S[1]], w=[vc_ext])

    if stage <= 3:
        dt = k.sbuf("dbgt", [128, 2048], F32)
        k.op("dve", lambda e: e.tensor_copy(out=dt[:, 0:256], in_=kcT[:, :]), r=[kcT], w=[dt])
        k.op("dve", lambda e: e.tensor_copy(out=dt[:, 256:256 + 386], in_=vc_ext[:, :, :].rearrange("p a b -> p (a b)")), r=[vc_ext], w=[dt])
        k.dma("sp", dbg[:, 0:1024], dt[:, 0:1024], r=[dt], is_output=True)
        return
    Eb = [k.sbuf("Eb%d" % i, [128, 512], BF16) for i in range(6)]
    oacc = [k.sbuf("oacc%d" % i, [128, 512], F32) for i in range(2)]
    mbT4 = [k.sbuf("mbT%d" % i, [64, 512], BF16) for i in range(2)]
    keepb = [k.sbuf("keep%d" % i, [128, 64], F32) for i in range(2)]
    fixb = [k.sbuf("fix%d" % i, [128, 64], F32) for i in range(2)]
    sm = [k.sbuf("sm%d" % i, [128, 64], F32) for i in range(2)]
    imp = [k.sbuf("imp%d" % i, [128, 64], F32) for i in range(2)]
    wk2 = [k.sbuf("wk2%d" % i, [128, 64], F32) for i in range(2)]
    mx = [k.sbuf("mx%d" % i, [128, 24], F32) for i in range(2)]
    mb = [k.sbuf("mb%d" % i, [128, 64], BF16) for i in range(2)]
    bzb = [k.sbuf("bzb%d" % i, [128, 512], BF16) for i in range(2)]
    sg = [k.sbuf("sg%d" % i, [128, 512], F32) for i in range(2)]
    yb = [k.sbuf("yb%d" % i, [128, 512], BF16) for i in range(2)]
    ybT = [k.sbuf("ybT%d" % i, [128, 512], BF16) for i in range(2)]
    cnt = {"S": 0, "E": 0}

    S3 = [S[0], S[1], PX]

    def nextS():
        t = S3[cnt["S"] % 3]
        cnt["S"] += 1
        return t

    def nextE():
        t = Eb[cnt["E"] % 6]
        cnt["E"] += 1
        return t

    EbC = [k.sbuf("EbC%d" % i, [128, 512], BF16) for i in range(4)]

    def score_tile(lhsT_k, qt, extra, eb=None):
        s = nextS()
        n = len(extra)
        k.op("pe", lambda e: e.matmul(s[:, :], lhsT=lhsT_k[0], rhs=qT4[:, qt, :, :].rearrange("p h q -> p (h q)"),
                                      start=True, stop=(n == 0)), r=[lhsT_k[1], qT4], w=[s], tag=(k.tag or "") + ".qk")
        for i, (l, r_, rd) in enumerate(extra):
            k.op("pe", lambda e, l=l, r_=r_, i=i: e.matmul(s[:, :], lhsT=l, rhs=r_, start=False, stop=(i == n - 1)), r=rd, w=[s], tag=(k.tag or "") + ".x%d" % i)
        if eb is None:
            eb = nextE()
        k.op("act", lambda e: e.activation(out=eb[:], in_=s[:, :], func=AF.Exp), r=[s], w=[eb])
        return eb

    def pv(acc, width, eb, rhs_ap, rd, first=False):
        for h in range(4):
            a = acc[h // 2]
            o0 = (h % 2) * width
            st = bool(first and h % 2 == 0)
            k.op("pe", lambda e, a=a, o0=o0, h=h, st=st: e.matmul(a[:, o0:o0 + rhs_ap.shape[-1]], lhsT=eb[:, h * 128:(h + 1) * 128], rhs=rhs_ap,
                                                                   start=st, stop=True, skip_group_check=True), r=[eb] + rd, w=[a], tag=(k.tag or "") + ".pv%d" % h)

    def finish(acc, width, dcol, qt, branch, first, smt):
        for i in range(2):
            a = acc[i]
            dv = a[:, 0:2 * width].rearrange("p (h c) -> p h c", h=2)[:, :, dcol:dcol + 1]
            k.op("dve", lambda e, dv=dv, i=i: e.tensor_scalar(out=smt[:, 2 * i:2 * i + 2].rearrange("p (h c) -> p h c", c=1), in0=dv,
                                                              scalar1=1e-30, scalar2=None, op0=ALU.max), r=[a], w=[smt])
        k.op("dve", lambda e: e.reciprocal(out=smt[:, 4:8], in_=smt[:, 0:4]), r=[smt], w=[smt])
        k.op("dve", lambda e: e.tensor_tensor(out=smt[:, 8:12], in0=smt[:, 4:8], in1=gs[:, qt, branch * 4:branch * 4 + 4], op=ALU.mult),
             r=[smt, gs], w=[smt])
        oa = oacc[qt % 2]
        for h in range(4):
            a = acc[h // 2]
            o0 = (h % 2) * width
            if first:
                k.op("act", lambda e, a=a, o0=o0, h=h: e.activation(out=oa[:, h * 128:(h + 1) * 128], in_=a[:, o0:o0 + 128], func=AF.Copy,
                                                                     scale=smt[:, 8 + h:9 + h]), r=[a, smt], w=[oa])
            else:
                k.op("dve", lambda e, a=a, o0=o0, h=h: e.scalar_tensor_tensor(out=oa[:, h * 128:(h + 1) * 128], in0=a[:, o0:o0 + 128],
                                                                             scalar=smt[:, 8 + h:9 + h], in1=oa[:, h * 128:(h + 1) * 128],
                                                                             op0=ALU.mult, op1=ALU.add), r=[a, smt, oa], w=[oa])

    cstate = {}

    def Cqk_tile(qt):
        k.tag = "C"
        i = qt % 2
        k.dma("sp", keepb[i][:], I["keep"][qt], w=[keepb[i]])
        k.dma("sp", fixb[i][:], I["fix"][qt], w=[fixb[i]])
        lst = []
        for c in range(1 if qt < 16 else 2):
            dl = qt - 16 * c
            extra = []
            if dl <= 16:
                extra.append((idb[:, :], BC[:, dl, :], [idb, BC]))
            eb = score_tile((kcT[:, c * 128:(c + 1) * 128], kcT), qt, extra, eb=EbC[(qt % 2) * 2 + c])
            lst.append((OA, 193, eb, vc_ext[:, c, :], [vc_ext], c == 0))
        cstate[qt] = lst

    def C_tile(qt):
        k.tag = "C"
        i = qt % 2
        for args in cstate.pop(qt):
            pv(*args)
        if SUB <= 1:
            return
        smt = sm[i]
        finish(OA, 193, 192, qt, 0, True, smt)
        if SUB <= 2:
            return
        im = imp[i]
        for h in range(4):
            if SUB < 2.5 and h >= round((SUB - 2) * 10):
                return
            a = OA[h // 2]
            o0 = (h % 2) * 193 + 128
            if h == 0:
                k.op("dve", lambda e, a=a, o0=o0: e.tensor_scalar(out=im[:], in0=a[:, o0:o0 + 64], scalar1=smt[:, 4:5], scalar2=None, op0=ALU.mult),
                     r=[a, smt], w=[im])
            else:
                k.op("dve", lambda e, a=a, o0=o0, h=h: e.scalar_tensor_tensor(out=im[:], in0=a[:, o0:o0 + 64], scalar=smt[:, 4 + h:5 + h], in1=im[:],
                                                                             op0=ALU.mult, op1=ALU.add), r=[a, smt, im], w=[im])
        if SUB <= 2.5:
            return
        k.op("dve", lambda e: e.tensor_tensor(out=im[:], in0=im[:], in1=keepb[i][:], op=ALU.mult), r=[im, keepb[i]], w=[im])
        k.op("dve", lambda e: e.tensor_tensor(out=im[:], in0=im[:], in1=fixb[i][:], op=ALU.add), r=[im, fixb[i]], w=[im])
        if SUB <= 3:
            return
        m_, w2_ = mx[i], wk2[i]
        k.op("dve", lambda e: e.max(out=m_[:, 0:8], in_=im[:]), r=[im], w=[m_])
        k.op("dve", lambda e: e.match_replace(out=w2_[:], in_to_replace=m_[:, 0:8], in_values=im[:], imm_value=-1e30), r=[im, m_], w=[w2_])
        k.op("dve", lambda e: e.max(out=m_[:, 8:16], in_=w2_[:]), r=[w2_], w=[m_])
        if SUB <= 4:
            return
        k.op("dve", lambda e: e.tensor_reduce(out=m_[:, 16:17], in_=m_[:, 8:16], axis=AX.X, op=ALU.min), r=[m_], w=[m_])
        k.op("dve", lambda e: e.tensor_scalar(out=mb[i][:], in0=im[:], scalar1=m_[:, 16:17], scalar2=NEG, op0=ALU.is_lt, op1=ALU.mult),
             r=[im, m_], w=[mb[i]])
    def C2_tile(qt):
        k.tag = "C2"
        i = qt % 2
        k.op("pe", lambda e: e.transpose(PT[0:64, 0:128], mb[i][:, :], idb[:, :]), r=[mb[i], idb], w=[PT])
        for h in range(4):
            k.op("act", lambda e, h=h: e.copy(out=mbT4[i][:, h * 128:(h + 1) * 128], in_=PT[0:64, 0:128]), r=[PT], w=[mbT4[i]])

    sstate = {}

    def S_tile(qt, part):
        k.tag = "S"
        i = qt % 2
        if part == 0:
            sstate[qt] = {"pend": [], "started": False, "c": 0}
        st_ = sstate[qt]
        c_end = min(max(2, (qt + 1) // 2), qt + 1) if part == 0 else qt + 1
        for c in range(st_["c"], c_end):
            extra = [(esel[:, c * 128:(c + 1) * 128], mbT4[i][:, :], [esel, mbT4[i]])]
            if c == qt:
                extra.append((idb[:, :], BD[:, 0, :], [idb, BD]))
            elif c == qt - 1:
                extra.append((idb[:, :], BD[:, 1, :], [idb, BD]))
            eb = score_tile((kin["ks"][:, c * 128:(c + 1) * 128], kin["ks"]), qt, extra)
            st_["pend"].append((OS, 129, eb, vs_ext[:, c, :], [vs_ext], not st_["started"]))
            st_["started"] = True
            if len(st_["pend"]) > DEPTH:
                pv(*st_["pend"].pop(0))
        st_["c"] = c_end
        if part == 1:
            while st_["pend"]:
                pv(*st_["pend"].pop(0))
            finish(OS, 129, 128, qt, 1, False, sm[i])
            del sstate[qt]

    def W_tile(qt):
        k.tag = "W"
        i = qt % 2
        pend = []
        started = False
        for c in range(max(0, qt - 4), qt + 1):
            jj = qt - c
            extra = []
            if jj == 0:
                extra.append((idb[:, :], BD[:, 0, :], [idb, BD]))
            elif jj == 1:
                extra.append((idb[:, :], BD[:, 1, :], [idb, BD]))
            elif jj == 4:
                extra.append((idb[:, :], bw4[:, :], [idb, bw4]))
            eb = score_tile((kin["kw"][:, c * 128:(c + 1) * 128], kin["kw"]), qt, extra)
            pend.append((OA, 193, eb, vw_ext[:, c, :], [vw_ext], len(pend) == 0 and not started))
            started = True
            if len(pend) > DEPTH:
                pv(*pend.pop(0))
        while pend:
            pv(*pend.pop(0))
        finish(OA, 193, 128, qt, 2, False, sm[i])

    def F_tile(qt):
        k.tag = "F"
        i = qt % 2
        bz, s_, y_, yt_ = bzb[i], sg[i], yb[i], ybT[i]
        k.dma("sp", bz[:], I["bz"][qt * 128:(qt + 1) * 128, :], w=[bz])
        k.op("act", lambda e: e.activation(out=s_[:], in_=bz[:], func=AF.Exp, scale=-1.0), r=[bz], w=[s_])
        k.op("pool", lambda e: e.tensor_scalar(out=s_[:], in0=s_[:], scalar1=1.0, scalar2=None, op0=ALU.add), r=[s_], w=[s_])
        k.op("dve", lambda e: e.reciprocal(out=s_[:], in_=s_[:]), r=[s_], w=[s_])
        k.op("pool", lambda e: e.tensor_tensor(out=s_[:], in0=s_[:], in1=bz[:], op=ALU.mult), r=[s_, bz], w=[s_])
        k.op("pool", lambda e: e.tensor_tensor(out=y_[:], in0=s_[:], in1=oacc[i][:], op=ALU.mult), r=[s_, oacc[i]], w=[y_])

    def F2_tile(qt):
        k.tag = "F2"
        i = qt % 2
        y_, yt_ = yb[i], ybT[i]
        for h in range(4):
            k.op("pe", lambda e, h=h: e.transpose(PT[:, 512 + h * 128:512 + (h + 1) * 128], y_[:, h * 128:(h + 1) * 128], idb[:, :]),
                 r=[y_, idb], w=[PT])
        k.op("act", lambda e: e.copy(out=yt_[:], in_=PT[:, 512:1024]), r=[PT], w=[yt_])
        k.dma("sp", yT_d[512:1024, qt * 128:(qt + 1) * 128].rearrange("(h d) q -> d h q", d=128),
              yt_[:].rearrange("d (h q) -> d h q", h=4), r=[yt_], is_output=True)

    nqt = NQT if stage >= 99 else int(stage - 3)
    for it_ in range(nqt + 2):
        if it_ < nqt:
            Cqk_tile(it_)
        if 0 <= it_ - 1 < nqt:
            W_tile(it_ - 1)
            S_tile(it_ - 1, 0)
        if it_ < nqt:
            C_tile(it_)
        if 0 <= it_ - 1 < nqt:
            S_tile(it_ - 1, 1)
        if 0 <= it_ - 2 < nqt:
            F2_tile(it_ - 2)
        if it_ < nqt:
            C2_tile(it_)
        if 0 <= it_ - 1 < nqt:
            F_tile(it_ - 1)


B_INPUTS = [("cin", [4, 512, 4096], BF16), ("qT", [4, 128, 4096], BF16), ("kcT", [128, 4096], BF16), ("vcT", [128, 4096], BF16),
            ("ksT", [128, 4096], BF16), ("kwT", [128, 4096], BF16), ("vs", [4096, 128], BF16), ("vw", [4096, 128], BF16),
            ("gates", [4096, 12], BF16), ("bz", [4096, 512], BF16), ("conv_w", [512, 3], F32), ("cmp_pos", [2, 32, 128], F32),
            ("cmp_w1", [2, 4096, 128], F32), ("cmp_w2", [2, 128, 128], F32), ("table", [32, 4], F32), ("table31", [32, 4], F32)]


def build_B(stage=99):
    nc = bass.Bass("TRN2", target_bir_lowering=False)
    I = {}
    for n, shp, dt in B_INPUTS:
        if SIMDBG and n in ("cmp_w1", "cmp_w2"):
            dt = BF16
        I[n] = nc.dram_tensor(n, shp, dt, kind="ExternalInput").ap()
    for n, shp in CONST_SHAPES.items():
        I[n] = nc.dram_tensor(n, shp, (BF16 if (SIMDBG and n in ("jflip", "onehot", "esel", "ovx", "bw4")) else F32), kind="ExternalInput").ap()
    yT = nc.dram_tensor("yT", [1024, 4096], BF16, kind="ExternalOutput").ap()
    dbg = nc.dram_tensor("dbg", [128, 4096], F32, kind="ExternalOutput").ap()
    k = KB(nc)
    emit_B(k, nc, I, yT, stage, dbg)
    k.finish()
    return nc


def emit_conv_tok(k, hfm_own, hfm_par, halo_p, conv_w, mh, yS, mid=None):
    TP = 512
    NT = 2048
    NB = 4
    cw = k.sbuf("cwt", [128, 8, 3], F32)
    k.dma("sp", cw[:], conv_w.rearrange("(c p) k -> p c k", p=128), w=[cw])
    mht = k.sbuf("mht", [128, 1], F32)
    k.dma("sp", mht[:], mh[:, :], w=[mht])
    cb_ = {n: [k.sbuf("ct_%s%d" % (n, i), [128, TP + 2], BF16) for i in range(NB)] for n in ("h", "b", "c", "z")}
    pbuf = [k.sbuf("ct_p%d" % i, [128, TP + 2], F32) for i in range(NB)]
    abuf = [k.sbuf("ct_a%d" % i, [128, TP], F32) for i in range(NB)]
    zbuf = [k.sbuf("ct_s%d" % i, [128, TP], F32) for i in range(NB)]
    ybuf = [k.sbuf("ct_y%d" % i, [128, TP], BF16) for i in range(NB)]
    it = 0
    order = [(c8, tp) for tp in (1, 2, 3, 0) for c8 in range(8)]
    for (c8, tp) in order:
        if it == 16 and mid is not None:
            mid()
        T = hfm_own if c8 < 4 else hfm_par
        cl = c8 % 4
        hc8 = (c8 + 4) % 8
        if True:
            i = it % NB
            it += 1
            t0 = tp * TP
            bh, bb, bc, bz = cb_["h"][i], cb_["b"][i], cb_["c"][i], cb_["z"][i]
            rj = [slice(j * 512 + cl * 128, j * 512 + (cl + 1) * 128) for j in range(4)]
            if tp == 0:
                k.dma("sp", bh[:, 0:2], halo_p[hc8 * 128:(hc8 + 1) * 128, 0:2], r=["halo_p"], w=[bh])
                k.dma("sp", bc[:, 0:2], halo_p[1024 + hc8 * 128:1024 + (hc8 + 1) * 128, 0:2], r=["halo_p"], w=[bc])
                k.dma("sp", bh[:, 2:], T[rj[0], 0:TP], w=[bh])
                k.dma("sp", bc[:, 2:], T[rj[2], 0:TP], w=[bc])
            else:
                k.dma("sp", bh[:, :], T[rj[0], t0 - 2:t0 + TP], w=[bh])
                k.dma("sp", bc[:, :], T[rj[2], t0 - 2:t0 + TP], w=[bc])
            k.dma("act", bb[:, 0:TP], T[rj[1], t0:t0 + TP], w=[bb])
            k.dma("act", bz[:, 0:TP], T[rj[3], t0:t0 + TP], w=[bz])
            p, a, sz, y = pbuf[i], abuf[i], zbuf[i], ybuf[i]
            k.op("dve", lambda e, p=p, bc=bc, bh=bh: e.tensor_tensor(out=p[:], in0=bc[:], in1=bh[:], op=ALU.mult), r=[bc, bh], w=[p])
            if tp == 0:
                k.op("dve", lambda e, p=p: e.tensor_scalar(out=p[:, 0:2], in0=p[:, 0:2], scalar1=mht[:, 0:1], scalar2=None, op0=ALU.mult),
                     r=[p, mht], w=[p])
            k.op("dve", lambda e, a=a, p=p, c8=c8: e.tensor_scalar(out=a[:], in0=p[:, 2:TP + 2], scalar1=cw[:, c8, 2:3], scalar2=None, op0=ALU.mult), r=[p, cw], w=[a])
            k.op("dve", lambda e, a=a, p=p, c8=c8: e.scalar_tensor_tensor(out=a[:], in0=p[:, 1:TP + 1], scalar=cw[:, c8, 1:2], in1=a[:], op0=ALU.mult, op1=ALU.add), r=[p, cw, a], w=[a])
            k.op("dve", lambda e, a=a, p=p, c8=c8: e.scalar_tensor_tensor(out=a[:], in0=p[:, 0:TP], scalar=cw[:, c8, 0:1], in1=a[:], op0=ALU.mult, op1=ALU.add), r=[p, cw, a], w=[a])
            k.op("act", lambda e, sz=sz, bz=bz: e.activation(out=sz[:], in_=bz[:, 0:TP], func=AF.Silu), r=[bz], w=[sz])
            k.op("pool", lambda e, a=a, bb=bb: e.tensor_tensor(out=a[:], in0=a[:], in1=bb[:, 0:TP], op=ALU.mult), r=[a, bb], w=[a])
            k.op("pool", lambda e, a=a, sz=sz, y=y: e.tensor_tensor(out=y[:], in0=a[:], in1=sz[:], op=ALU.mult), r=[a, sz], w=[y])
            k.dma("sp", yS[c8 * 128:(c8 + 1) * 128, t0:t0 + TP], y[:], r=[y], is_output=True)


from concourse.bass_utils import run_bass_kernel_spmd

I32 = mybir.dt.int32
PAIRS = [[0, 1], [2, 3], [4, 5], [6, 7]]


def build_fused(nlayers=4):
    nc = bass.Bass("TRN2", target_bir_lowering=False)

    def din(name, shape, dt=F32):
        return nc.dram_tensor(name, list(shape), dt, kind="ExternalInput").ap()

    def scr(name, shape, dt):
        return nc.dram_tensor(name, list(shape), dt).ap()

    x_in = din("x", [2048, 2048])
    sel = din("sel", [1, 8], I32)
    ident = din("ident", [128, 128])
    triu = din("triu", [128, 128])
    W = {}
    for i in range(2):
        W["ev_w_in%d" % i] = din("ev_w_in%d" % i, [2048, EVEN_IN])
        W["ev_w_out%d" % i] = din("ev_w_out%d" % i, [2048, 2048])
        W["conv_w%d" % i] = din("conv_w%d" % i, [1024, 3])
        W["cmp_pos%d" % i] = din("cmp_pos%d" % i, [2, 32, 128])
        W["cmp_w1%d" % i] = din("cmp_w1%d" % i, [2, 4096, 128])
        W["cmp_w2%d" % i] = din("cmp_w2%d" % i, [2, 128, 128])
        W["od_w_in%d" % i] = din("od_w_in%d" % i, [2048, 6144])
        W["od_w_out%d" % i] = din("od_w_out%d" % i, [2048, 2048])
        W["og%d" % i] = din("og%d" % i, [2048])
        W["ob%d" % i] = din("ob%d" % i, [2048])
        W["sgw_t%d" % i] = din("sgw_t%d" % i, [8, 128, 128])
        W["sgb%d" % i] = din("sgb%d" % i, [1024])
    for l in range(4):
        W["ln_g%d" % l] = din("ln_g%d" % l, [2048])
        W["ln_b%d" % l] = din("ln_b%d" % l, [2048])
    mh = din("mh", [128, 1])
    table = din("table", [32, 4])
    table31 = din("table31", [32, 4])
    CB = {n: din("c_" + n, shp) for n, shp in CONST_SHAPES.items() if n != "ident"}
    out = nc.dram_tensor("out", [2048, 2048], F32, kind="ExternalOutput").ap()

    hfm_own = scr("hfm_own", [GR, 2048], BF16)
    hfm_par = scr("hfm_par", [GR, 2048], BF16)
    htm_own = scr("htm_own", [2048, GC], BF16)
    htm_par = scr("htm_par", [2048, GC], BF16)
    hg_fm = scr("hg_fm", [2 * 1024, 2048], BF16)
    halo_loc = scr("halo_loc", [2048, 2], BF16)
    halo_g = scr("halo_g", [4096, 2], BF16)
    halo_p = scr("halo_p", [2048, 2], BF16)
    hg_tm = scr("hg_tm", [4096, GC], BF16)
    hB_fm = scr("hB_fm", [1024, 4096], BF16)
    hB_tm = scr("hB_tm", [4096, GC], BF16)
    yloc = scr("yloc", [1024, 4096], BF16)
    yg = scr("yg", [1024, 4096], BF16)
    yS = scr("yS", [2048, 2048], BF16)
    xa = scr("xa", [2048, 2048], F32)
    xb_ = scr("xb", [2048, 2048], F32)
    xc = scr("xc", [2048, 2048], F32)

    k = KB(nc)

    MULTS = ("own", "oth", "slot", 2048)

    def load_sel():
        if "gown" in k.vals:
            return
        selt = k.sbuf("selt", [1, 8], I32)
        k.dma("sp", selt[:], sel[0:1, 0:8], w=[selt])

        def ld(e):
            for j, m in enumerate(MULTS):
                reg = e.alloc_register("selreg%d_%s" % (k.phase_no, m))
                e.reg_load(reg, selt[0:1, j:j + 1])
                k.vals["g%s" % m] = e.snap(reg, min_val=0, max_val=(1 if m == "slot" else 2048))
            return e.nop()
        k.op("sp", ld, r=[selt], w=[])

    def G(m):
        return k.vals["g%s" % m]

    x_cur = x_in
    for layer in range(nlayers):
        i = layer // 2
        if layer % 2 == 0:
            emit_A(k, x_cur, W["ev_w_in%d" % i], ident, (hfm_own, hfm_par), (htm_own, htm_par), halo=halo_loc)
            k.end_phase()
            k.collective("AllGather", PAIRS, halo_loc[:, :], halo_g[:, :], w=["halo_g"], serialize=False)
            for c_ in range(2):
                k.collective("AllGather", PAIRS, hfm_par[2048 + c_ * 512:2048 + (c_ + 1) * 512, :], hg_fm[c_ * 1024:(c_ + 1) * 1024, :],
                             w=[("hg_fm", c_)], serialize=False)
            for c_ in range(2):
                k.collective("AllGather", PAIRS, htm_par[c_ * 1024:(c_ + 1) * 1024, :], hg_tm[c_ * 2048:(c_ + 1) * 2048, :],
                             w=[("hg_tm", c_)], serialize=False)
            load_sel()
            halo_g3 = halo_g.rearrange("(s r) t -> s r t", s=2)
            k.dma("sp", halo_p[:, :], (lambda: halo_g3[bass.ds(G("slot"), 1), :, :].rearrange("s r t -> (s r) t")),
                  r=["halo_g", ("hg_fm", 0), ("hg_fm", 1), ("hg_tm", 0), ("hg_tm", 1)], w=["halo_p"])
            hg_fm4 = hg_fm.rearrange("(c s r) t -> c s r t", c=2, s=2)
            hg_tm4 = hg_tm.rearrange("(c s r) n -> c s r n", c=2, s=2)
            k.dma("sp", (lambda: hB_fm[:, bass.ds(G("own"), 2048)]), hfm_own[2048:3072, :])
            k.dma("sp", (lambda: hB_tm[bass.ds(G("own"), 2048), :]), htm_own[:, :])

            def partner_relayout():
                allk = ["halo_g", ("hg_fm", 0), ("hg_fm", 1), ("hg_tm", 0), ("hg_tm", 1)]
                k.dma("sp", (lambda: hB_fm[:, bass.ds(G("oth"), 2048)].rearrange("(c r) t -> c r t", c=2)),
                      (lambda: hg_fm4[:, bass.ds(G("slot"), 1), :, :].rearrange("c s r t -> c (s r) t")), r=allk)
                k.dma("sp", (lambda: hB_tm[bass.ds(G("oth"), 2048), :].rearrange("(c r) n -> c r n", c=2)),
                      (lambda: hg_tm4[:, bass.ds(G("slot"), 1), :, :].rearrange("c s r n -> c (s r) n")), r=allk)
            emit_conv_tok(k, hfm_own, hfm_par, halo_p, W["conv_w%d" % i], mh, yS, mid=partner_relayout)
            k.end_phase()
            IB = dict(CB)
            IB["qT"] = hB_fm[0:512, :].rearrange("(h d) t -> h d t", h=4)
            IB["kcT"] = hB_fm[512:640, :]
            IB["vcT"] = hB_fm[640:768, :]
            IB["ksT"] = hB_fm[768:896, :]
            IB["kwT"] = hB_fm[896:1024, :]
            IB["vs"] = hB_tm[:, 0:128]
            IB["vw"] = hB_tm[:, 128:256]
            IB["gates"] = hB_tm[:, 256:268]
            IB["bz"] = hB_tm[:, 268:780]
            IB["ident"] = ident
            IB["table"] = table
            IB["table31"] = table31
            IB["cmp_pos"] = W["cmp_pos%d" % i]
            IB["cmp_w1"] = W["cmp_w1%d" % i]
            IB["cmp_w2"] = W["cmp_w2%d" % i]
            emit_B(k, nc, IB, yloc, lut_name="lutd%d" % i)
            k.end_phase()
            for c_ in range(2):
                k.collective("AllGather", PAIRS, yloc[512 + c_ * 256:512 + (c_ + 1) * 256, :], yg[c_ * 512:(c_ + 1) * 512, :], w=["yg"])
            load_sel()
            yg4 = yg.rearrange("(c s r) t -> c s r t", c=2, s=2)
            for s_ in range(2):
                k.dma("sp", yS[1024 + s_ * 512:1024 + (s_ + 1) * 512, :].rearrange("(c r) t -> c r t", c=2),
                      (lambda s_=s_: yg4[:, s_, :, bass.ds(G(2048), 2048)]), r=["yg"])
            k.end_phase()
            x_next = out if layer == nlayers - 1 else (xa if layer == 0 else xc)
            emit_C(k, yS, x_cur, W["ev_w_out%d" % i], W["ln_g%d" % layer], W["ln_b%d" % layer], x_next)
            if layer < nlayers - 1:
                k.end_phase()
            x_cur = x_next
        else:
            x_next = out if layer == nlayers - 1 else xb_
            emit_D(k, x_cur, W["od_w_in%d" % i], W["og%d" % i], W["ob%d" % i], W["sgw_t%d" % i], W["sgb%d" % i], triu,
                   W["od_w_out%d" % i], W["ln_g%d" % layer], W["ln_b%d" % layer], ident, x_next)
            if layer < nlayers - 1:
                k.end_phase()
            x_cur = x_next
    k.finish()
    return nc


_NC = {}
NLAYERS = 4


def kernel(x, rel_bias_table, ln_g, ln_b, ev_w_in, ev_conv_w, ev_cmp_pos, ev_cmp_w1, ev_cmp_w2, ev_w_out,
           od_w_in, od_ln_g, od_ln_b, od_sgu_w, od_sgu_b, od_w_out):
    f32 = np.float32
    A = lambda a: np.ascontiguousarray(np.asarray(a, dtype=f32))
    x = A(x).reshape(8, 2048, 2048)
    rel = A(rel_bias_table)
    consts = host_consts_B()
    common = {"ident": consts["ident"], "triu": np.triu(np.ones((128, 128), f32))}
    for n, v in consts.items():
        if n != "ident":
            common["c_" + n] = v
    gperm = np.arange(EVEN_IN)
    gidx = np.arange(24).reshape(3, 2, 4).transpose(1, 0, 2).reshape(-1)
    gperm[6656:6680] = 6656 + gidx
    operm = np.concatenate([np.arange(0, 512), np.arange(1024, 1536), np.arange(512, 1024), np.arange(1536, 2048)])
    convw = {}
    wout = {}
    win = {}
    swap = np.arange(EVEN_IN)
    def _sw(a0, b0, n):
        swap[a0:a0 + n] = np.arange(b0, b0 + n)
        swap[b0:b0 + n] = np.arange(a0, a0 + n)
    for j in range(4):
        _sw(j * 1024, j * 1024 + 512, 512)
    _sw(4096, 4608, 512)
    for base in (5120, 5376, 5632, 5888, 6144, 6400):
        _sw(base, base + 128, 128)
    _sw(6656, 6668, 12)
    _sw(6680, 7192, 512)
    for i in range(2):
        w_can = np.asarray(ev_w_in[i], dtype=f32)[:, gperm]
        win[i] = [A(w_can), A(w_can[:, swap])]
        wo_can = np.asarray(ev_w_out[i], dtype=f32)
        wout[i] = [A(wo_can[np.concatenate([np.arange(0, 512), np.arange(512, 1024), np.arange(1024, 2048)])]),
                   A(wo_can[np.concatenate([np.arange(512, 1024), np.arange(0, 512), np.arange(1024, 2048)])])]
        cw = np.asarray(ev_conv_w[i], dtype=f32)
        cwt = cw.T
        convw[i] = [A(cwt), A(np.concatenate([cwt[512:1024], cwt[0:512]], axis=0))]
        common["cmp_pos%d" % i] = A(ev_cmp_pos[i])
        common["cmp_w1%d" % i] = A(ev_cmp_w1[i])
        common["cmp_w2%d" % i] = A(ev_cmp_w2[i])
        common["od_w_in%d" % i] = A(od_w_in[i])
        common["od_w_out%d" % i] = A(od_w_out[i])
        common["og%d" % i] = A(od_ln_g[i])
        common["ob%d" % i] = A(od_ln_b[i])
        common["sgw_t%d" % i] = A(np.asarray(od_sgu_w[i], dtype=f32).transpose(0, 2, 1))
        common["sgb%d" % i] = A(np.asarray(od_sgu_b[i], dtype=f32).reshape(-1))
    for l in range(4):
        common["ln_g%d" % l] = A(ln_g[l])
        common["ln_b%d" % l] = A(ln_b[l])
    if "nc" not in _NC:
        _NC["nc"] = build_fused(NLAYERS)
    in_maps = []
    for c in range(8):
        d = dict(common)
        d["x"] = np.ascontiguousarray(x[c])
        g_ = c % 2
        d["sel"] = np.array([[g_ * 2048, (1 - g_) * 2048, 1 - g_, g_ * 2048, 0, 0, 0, 0]], dtype=np.int32)
        for i in range(2):
            d["conv_w%d" % i] = convw[i][g_]
            d["ev_w_out%d" % i] = wout[i][g_]
            d["ev_w_in%d" % i] = win[i][g_]
        d["mh"] = np.full((128, 1), float(g_), dtype=f32)
        d["table"] = A(rel[:, 4 * g_:4 * g_ + 4])
        d["table31"] = A(np.tile(rel[31:32, 4 * g_:4 * g_ + 4], (32, 1)))
        in_maps.append(d)
    res = run_bass_kernel_spmd(_NC["nc"], in_maps, core_ids=list(range(8)))
    return np.stack([np.asarray(res.results[c]["out"]) for c in range(8)]).reshape(4, 4096, 2048).astype(f32)
```

```python
import math
import numpy as np
from contextlib import ExitStack
import concourse.bass as bass
import concourse.mybir as mybir

F32 = mybir.dt.float32
BF16 = mybir.dt.bfloat16
I32 = mybir.dt.int32
U32 = mybir.dt.uint32
AF = mybir.ActivationFunctionType
ALU = mybir.AluOpType
AX = mybir.AxisListType

ENGS = ("pe", "dve", "act", "pool", "sp")
N_DMA_SEMS = 24


class Tile:
    def __init__(self, name, ap_handle, nsub=1):
        self.name = name
        self.t = ap_handle
        self.nsub = nsub

    def __getitem__(self, idx):
        return self.t[idx]

    def k(self, i):
        assert 0 <= i < self.nsub
        return (self.name, i)

    def all(self):
        return [(self.name, i) for i in range(self.nsub)]


def _keys(lst):
    out = []
    for x in lst:
        if isinstance(x, Tile):
            out.extend(x.all())
        elif isinstance(x, list):
            out.extend(_keys(x))
        else:
            out.append(x)
    return out


def _dma_dbg(e, out, in_, kw):
    o = out() if callable(out) else out
    i = in_() if callable(in_) else in_
    try:
        return e.dma_start(out=o, in_=i, **kw)
    except Exception:
        print("DMA FAILED out=", o, " in=", i)
        raise


class KB:
    def __init__(self, nc, same_engine_sync=True):
        self.nc = nc
        self.ses = same_engine_sync
        self.stack = ExitStack()
        self.semstack = ExitStack()
        self.csem = None
        self.base = {e: 0 for e in ENGS}
        self.cc_uses = 0
        self.cc_last = None
        self.vals = {}
        self.phase_no = 0
        self.tag = None
        self.ops = {e: [] for e in ENGS}
        self.res = {}
        self.seen = {e: {} for e in ENGS}
        self.dma_last = [None] * N_DMA_SEMS
        self.dma_uses = [0] * N_DMA_SEMS
        self.dma_rr = 0
        self.ntiles = 0
        self.out_tokens = []
        self.excl = set()

    def sbuf(self, name, shape, dtype, nsub=1):
        t = self.stack.enter_context(self.nc.sbuf_tensor("sb%d_" % self.phase_no + name, list(shape), dtype))
        return Tile(name, t, nsub)

    def psum(self, name, shape, dtype, nsub=1):
        t = self.stack.enter_context(self.nc.psum_tensor("ps%d_" % self.phase_no + name, list(shape), dtype))
        self.excl.add(name)
        return Tile(name, t, nsub)

    def _need(self, eng, tok, waits):
        if tok is None:
            return
        kind, src, idx = tok
        if kind == "c" and src == eng and not self.ses:
            return
        if kind == "c" and src == eng and eng == "pe":
            return
        key = (kind, src)
        if self.seen[eng].get(key, -1) >= idx:
            return
        self.seen[eng][key] = idx
        waits.append(tok)

    def _deps(self, eng, r, w, tok):
        waits = []
        for key in _keys(r):
            ent = self.res.setdefault(key, [None, {}])
            self._need(eng, ent[0], waits)
            if isinstance(key, tuple) and key[0] in self.excl:
                for t in ent[1].values():
                    if not (t[0] == "c" and t[1] == eng):
                        self._need(eng, t, waits)
        for key in _keys(w):
            ent = self.res.setdefault(key, [None, {}])
            self._need(eng, ent[0], waits)
            for t in ent[1].values():
                self._need(eng, t, waits)
        for key in _keys(r):
            ent = self.res[key]
            ent[1][(tok[0], tok[1])] = tok
        for key in _keys(w):
            ent = self.res[key]
            ent[0] = tok
            ent[1] = {}
        return waits

    def op(self, eng, fn, r=(), w=(), tag=None):
        idx = len(self.ops[eng])
        tok = ("c", eng, idx)
        waits = self._deps(eng, list(r), list(w), tok)
        self.ops[eng].append({"fn": fn, "waits": waits, "tok": tok, "dma": None, "tag": tag or self.tag})
        return tok

    def raw(self, eng, fn):
        self.ops[eng].append({"fn": fn, "waits": [], "tok": None, "dma": "noinc"})

    def dma(self, eng, out, in_, r=(), w=(), is_output=False, **kw):
        s = self.dma_rr
        self.dma_rr = (self.dma_rr + 1) % N_DMA_SEMS
        pre = []
        self._need(eng, self.dma_last[s], pre)
        self.dma_uses[s] += 1
        tok = ("d", s, self.dma_uses[s])
        self.dma_last[s] = tok
        waits = pre + self._deps(eng, list(r), list(w), tok)
        self.ops[eng].append(
            {"fn": (lambda e: _dma_dbg(e, out, in_, kw)), "waits": waits, "tok": tok, "dma": s}
        )
        if is_output:
            self.out_tokens.append(tok)
        return tok

    def collective(self, kind, groups, in_ap, out_ap, r=(), w=(), serialize=True):
        pre = []
        if serialize:
            self._need("pool", self.cc_last, pre)
        self.cc_uses += 1
        tok = ("x", 0, self.cc_uses)
        self.cc_last = tok
        waits = pre + self._deps("pool", list(r), list(w), tok)
        self.ops["pool"].append(
            {"fn": (lambda e: e.collective_compute(kind, ALU.bypass, replica_groups=groups, ins=[in_ap], outs=[out_ap])),
             "waits": waits, "tok": tok, "dma": "cc"})
        return tok

    def _alloc_sems(self):
        if self.csem is None:
            nc = self.nc
            self.csem = {e: self.semstack.enter_context(nc.semaphore("c_" + e)) for e in ENGS}
            self.dsem = [self.semstack.enter_context(nc.semaphore("d_%d" % i)) for i in range(N_DMA_SEMS)]
            self.ccsem = self.semstack.enter_context(nc.semaphore("ccs"))

    def end_phase(self):
        self.emit(final=False)
        self.stack = ExitStack()
        for e in ENGS:
            self.base[e] += self.n_incs[e]
        self.ops = {e: [] for e in ENGS}
        self.res = {}
        self.seen = {e: {} for e in ENGS}
        self.out_tokens = []
        self.excl = set()
        self.phase_no += 1

    def finish(self):
        self.emit(final=True)
        self.semstack.close()

    def emit(self, final=True):
        nc = self.nc
        self._alloc_sems()
        last = {}
        for e in ENGS:
            for o in reversed(self.ops[e]):
                if o["fn"] is not None and o["dma"] is None:
                    last[e] = o["tok"]
                    break
        for e in ENGS:
            fin = []
            for t in self.out_tokens:
                self._need(e, t, fin)
            for s in range(N_DMA_SEMS):
                self._need(e, self.dma_last[s], fin)
            self._need(e, self.cc_last, fin)
            for e2 in ENGS:
                if e2 != e and e2 in last:
                    self._need(e, last[e2], fin)
            self.ops[e].append({"fn": None, "waits": fin, "tok": None, "dma": None})

        waited = {e: set() for e in ENGS}
        for e in ENGS:
            for o in self.ops[e]:
                for (kind, src, idx) in o["waits"]:
                    if kind == "c":
                        waited[src].add(idx)
        rank = {}
        for e in ENGS:
            for i, idx in enumerate(sorted(waited[e])):
                rank[(e, idx)] = self.base[e] + i + 1
        self.n_incs = {e: len(waited[e]) for e in ENGS}
        csem, dsem, ccsem = self.csem, self.dsem, self.ccsem

        def run(ename, e):
            for o in self.ops[ename]:
                for (kind, src, idx) in o["waits"]:
                    if kind == "c":
                        wi = e.wait_ge(csem[src], rank[(src, idx)])
                    elif kind == "x":
                        wi = e.wait_ge(ccsem, idx)
                    else:
                        wi = e.wait_ge(dsem[src], 16 * idx)
                    if o.get("tag") and wi is not None:
                        try:
                            wi.annotate("wait[%s]<-%s" % (o["tag"], src if kind == "c" else kind))
                        except Exception:
                            pass
                if o["fn"] is None:
                    continue
                ins = o["fn"](e)
                if o.get("tag") and ins is not None:
                    try:
                        ins.annotate(o["tag"])
                    except Exception:
                        pass
                if o["dma"] == "cc":
                    ins.then_inc(ccsem, 1)
                elif o["dma"] == "noinc":
                    pass
                elif o["dma"] is not None:
                    ins.then_inc(dsem[o["dma"]], 16)
                else:
                    _, en, idx = o["tok"]
                    if (en, idx) in rank:
                        ins.then_inc(csem[en], 1)

        with nc.Block() as block:
            @block.tensor
            def _(e):
                run("pe", e)

            @block.vector
            def _(e):
                run("dve", e)

            @block.scalar
            def _(e):
                run("act", e)

            @block.gpsimd
            def _(e):
                run("pool", e)

            @block.sync
            def _(e):
                run("sp", e)
        self.stack.close()
        if final:
            pass


NTOK = 2048
D = 2048
EVEN_IN = 7704
SCALE = 128 ** -0.5

GR = 3072
FM_JOBS = []
for j in range(4):
    for g in range(2):
        FM_JOBS.append((j * 1024 + g * 512, 512, [g * GR + j * 512 + c * 128 for c in range(4)], 1.0))
for g in range(2):
    FM_JOBS.append((4096 + g * 512, 512, [g * GR + 2048 + c * 128 for c in range(4)], SCALE))
FM_JOBS.append((5120, 512, [2560, GR + 2560, 2688, GR + 2688], 1.0))
FM_JOBS.append((5632, 256, [2816, GR + 2816], 1.0))
FM_JOBS.append((6144, 256, [2944, GR + 2944], 1.0))
NFM = 6144
GC = 780
TM_JOBS = [(5888, 256, [(0, 128, 0), (128, 128, GC)]), (6400, 256, [(0, 128, 128), (128, 128, GC + 128)]),
           (6656, 24, [(0, 12, 256), (12, 12, GC + 256)]), (6680, 512, [(0, 512, 268)]), (7192, 512, [(0, 512, GC + 268)])]
NTM = 1560


def build_A():
    nc = bass.Bass("TRN2", target_bir_lowering=False)
    x = nc.dram_tensor("x", [NTOK, D], F32, kind="ExternalInput").ap()
    w = nc.dram_tensor("w", [D, EVEN_IN], F32, kind="ExternalInput").ap()
    ident = nc.dram_tensor("ident", [128, 128], F32, kind="ExternalInput").ap()
    hfm = nc.dram_tensor("hfm", [NFM, NTOK], BF16, kind="ExternalOutput").ap()
    htm = nc.dram_tensor("htm", [NTOK, NTM], BF16, kind="ExternalOutput").ap()
    k = KB(nc)
    emit_A(k, x, w, ident, hfm, htm)
    k.finish()
    return nc


def load_xT(k, x, idb, xT, ntt, pT, name="xb"):
    xb = [k.sbuf("%s%d" % (name, i), [128, D], BF16) for i in range(2)]
    n = 0
    for tt in range(ntt):
        b = xb[tt % 2]
        k.dma("pool", b[:], x[tt * 128:(tt + 1) * 128, :], w=[b])
        for g in range(4):
            p = pT[n % len(pT)]
            for j in range(4):
                kc = g * 4 + j
                k.op("pe", lambda e, p=p, b=b, kc=kc, j=j: e.transpose(
                    p[:, j * 128:(j + 1) * 128], b[:, kc * 128:(kc + 1) * 128], idb[:]),
                    r=[b, idb], w=[p])
            eng = "dve" if n % 2 == 0 else "act"
            src = p[:, 0:512].rearrange("p (a b) -> p a b", a=4)
            dst = xT[:, g * 4:(g + 1) * 4, tt * 128:(tt + 1) * 128]
            if eng == "dve":
                k.op("dve", lambda e, dst=dst, src=src: e.tensor_copy(out=dst, in_=src), r=[p], w=[xT.k(tt)])
            else:
                k.op("act", lambda e, dst=dst, src=src: e.copy(out=dst, in_=src), r=[p], w=[xT.k(tt)])
            n += 1


def load_ident(k, ident):
    idf = k.sbuf("idf", [128, 128], F32)
    idb = k.sbuf("idb", [128, 128], BF16)
    k.dma("sp", idf[:], ident[:, :], w=[idf])
    k.op("dve", lambda e: e.tensor_copy(out=idb[:], in_=idf[:]), r=[idf], w=[idb])
    return idf, idb


def emit_A(k, x, w, ident, hfm, htm, gather=None, halo=None):
    idf, idb = load_ident(k, ident)
    xT = k.sbuf("xT", [128, 16, NTOK], BF16, nsub=16)
    pT = [k.psum("pT%d" % i, [128, 1024], BF16) for i in range(2)]
    pm = [k.psum("pm%d" % i, [128, 512], F32) for i in range(4)]
    load_xT(k, x, idb, xT, 16, pT)
    wb = [k.sbuf("wb%d" % i, [128, 16, 512], BF16) for i in range(3)]
    stg = [k.sbuf("stg%d" % i, [128, NTOK], BF16) for i in range(3)]
    wv = w.rearrange("(c p) n -> p c n", p=128)
    nw = 0
    npm = 0
    nst = 0
    nev = 0
    jobs = [("tm",) + j for j in TM_JOBS] + [("fm",) + j for j in FM_JOBS]
    pending = []
    done_rows = {}

    def flush(jidx):
        while pending and pending[0][0] <= jidx:
            pending.pop(0)[1]()

    def cc_fm(c):
        hg_fm, hg_tm, pairs = gather
        k.collective("AllGather", pairs, hfm[c * 512:(c + 1) * 512, :], hg_fm[c * 1024:(c + 1) * 1024, :],
                     r=[("hfm", 4 * c + i) for i in range(4)], w=["hg_fm"])

    def cc_tm(c):
        hg_fm, hg_tm, pairs = gather
        k.collective("AllGather", pairs, htm[c * 512:(c + 1) * 512, :], hg_tm[c * 1024:(c + 1) * 1024, :],
                     r=[("htm", c)], w=["hg_tm"])

    for jidx, job in enumerate(jobs):
        kind = job[0]
        col0, ncols = job[1], job[2]
        wt = wb[nw % 3]
        nw += 1
        k.dma("pool", wt[:, :, 0:ncols], wv[:, :, col0:col0 + ncols], w=[wt])
        flush(jidx)
        if kind == "fm":
            rows0, scale = job[3], job[4]
            for ch in range(ncols // 128):
                st = stg[nst % 3]
                nst += 1
                for tg in range(4):
                    p = pm[npm % 4]
                    npm += 1
                    for kc in range(16):
                        k.op("pe", lambda e, p=p, wt=wt, kc=kc, ch=ch, tg=tg: e.matmul(
                            p[:, :], lhsT=wt[:, kc, ch * 128:(ch + 1) * 128],
                            rhs=xT[:, kc, tg * 512:(tg + 1) * 512], start=(kc == 0), stop=(kc == 15)),
                            r=[wt] + [xT.k(t) for t in range(tg * 4, tg * 4 + 4)], w=[p])
                    dst = st[:, tg * 512:(tg + 1) * 512]
                    if nev % 2 == 0:
                        k.op("act", lambda e, dst=dst, p=p, scale=scale: e.activation(
                            out=dst, in_=p[:, :], func=AF.Copy, scale=float(scale)), r=[p], w=[st])
                    else:
                        k.op("dve", lambda e, dst=dst, p=p, scale=scale: e.tensor_scalar(
                            out=dst, in0=p[:, :], scalar1=float(scale), scalar2=None, op0=ALU.mult), r=[p], w=[st])
                    nev += 1
                r0 = rows0[ch]
                if isinstance(hfm, tuple):
                    dst_t = hfm[r0 // GR]
                    rr = r0 % GR
                    k.dma("sp", dst_t[rr:rr + 128, :], st[:, :], r=[st], w=[("hfm", r0 // 128)], is_output=True)
                else:
                    k.dma("sp", hfm[r0:r0 + 128, :], st[:, :], r=[st], w=[("hfm", r0 // 128)], is_output=True)
                if halo is not None and col0 < 4096 and (col0 // 1024) in (0, 2):
                    c8 = (r0 // GR) * 4 + ch
                    hr = (0 if col0 // 1024 == 0 else 1024) + c8 * 128
                    k.dma("sp", halo[hr:hr + 128, 0:2], st[:, 2046:2048], r=[st], is_output=True)
                c512 = r0 // 512
                done_rows[c512] = done_rows.get(c512, 0) + 1
                if gather is not None and done_rows[c512] == 4:
                    pending.append((jidx + 2, (lambda c=c512: cc_fm(c))))
        else:
            pieces = job[3]
            for tg in range(4):
                st = stg[nst % 3]
                nst += 1
                for t4 in range(4):
                    tt = tg * 4 + t4
                    p = pm[npm % 4]
                    npm += 1
                    for kc in range(16):
                        k.op("pe", lambda e, p=p, wt=wt, kc=kc, tt=tt, ncols=ncols: e.matmul(
                            p[:, 0:ncols], lhsT=xT[:, kc, tt * 128:(tt + 1) * 128],
                            rhs=wt[:, kc, 0:ncols], start=(kc == 0), stop=(kc == 15)),
                            r=[wt, xT.k(tt)], w=[p])
                    dst = st[:, t4 * 512:t4 * 512 + ncols]
                    if nev % 2 == 0:
                        k.op("act", lambda e, dst=dst, p=p, ncols=ncols: e.copy(out=dst, in_=p[:, 0:ncols]), r=[p], w=[st])
                    else:
                        k.op("dve", lambda e, dst=dst, p=p, ncols=ncols: e.tensor_copy(out=dst, in_=p[:, 0:ncols]), r=[p], w=[st])
                    nev += 1
                for (so, n_, ocol) in pieces:
                    src = st[:, :].rearrange("p (a b) -> p a b", a=4)[:, :, so:so + n_]
                    if isinstance(htm, tuple):
                        dst = htm[ocol // GC][tg * 512:(tg + 1) * 512, ocol % GC:ocol % GC + n_].rearrange("(a p) n -> p a n", p=128)
                    else:
                        dst = htm[tg * 512:(tg + 1) * 512, ocol:ocol + n_].rearrange("(a p) n -> p a n", p=128)
                    k.dma("sp", dst, src, r=[st, ("htm", tg)], w=[("htm", tg)], is_output=True)
            if gather is not None and jidx == len(TM_JOBS) - 1:
                for c in range(4):
                    pending.append((jidx + 2, (lambda c=c: cc_tm(c))))
    flush(10 ** 9)


D = 2048
ALPHA = 8 ** 0.25
LN_EPS = 1e-5


def ln_rows(k, z, outt, gB, bB, width, tagres, scr, aff_eng="dve"):
    st, mv = scr["st"], scr["mv"]
    nch = width // 512
    for c in range(nch):
        k.op("dve", lambda e, c=c: e.bn_stats(out=st[:, c, :], in_=z[:, c * 512:(c + 1) * 512]), r=[tagres], w=[st])
    k.op("dve", lambda e: e.bn_aggr(out=mv[:, 0:2], in_=st[:, 0:nch, :]), r=[st], w=[mv])
    k.op("act", lambda e: e.activation(out=mv[:, 2:3], in_=mv[:, 1:2], func=AF.Sqrt, bias=LN_EPS), r=[mv], w=[mv])
    k.op("dve", lambda e: e.reciprocal(out=mv[:, 3:4], in_=mv[:, 2:3]), r=[mv], w=[mv])
    if outt is not None and not isinstance(outt, Tile) and not (outt is z):
        oap, ores = outt
        k.op("dve", lambda e: e.tensor_scalar(out=oap, in0=z[:, 0:width], scalar1=mv[:, 0:1], scalar2=mv[:, 3:4],
                                              op0=ALU.subtract, op1=ALU.mult), r=[mv, tagres], w=[ores])
        return
    k.op("dve", lambda e: e.tensor_scalar(out=z[:, 0:width], in0=z[:, 0:width], scalar1=mv[:, 0:1], scalar2=mv[:, 3:4],
                                          op0=ALU.subtract, op1=ALU.mult), r=[mv, tagres], w=[tagres])
    if gB is not None:
        k.op(aff_eng, lambda e: e.tensor_tensor(out=z[:, 0:width], in0=z[:, 0:width], in1=gB[:, 0:width], op=ALU.mult),
             r=[tagres, gB], w=[tagres])


def emit_C(k, yT_d, x, w_out, g, b, xout, ntok=2048):
    ntt = ntok // 128
    wo = k.sbuf("wo", [128, 16, D], BF16, nsub=4)
    wv = w_out.rearrange("(c p) n -> p c n", p=128)
    for cb in range(4):
        k.dma("pool", wo[:, :, cb * 512:(cb + 1) * 512], wv[:, :, cb * 512:(cb + 1) * 512], w=[wo.k(cb)])
    yT = k.sbuf("yT", [128, 16, ntok], BF16, nsub=16)
    for kc in range(16):
        k.dma("sp", yT[:, kc, :], yT_d[kc * 128:(kc + 1) * 128, :], w=[yT.k(kc)])
    gB = k.sbuf("gB", [128, D], F32)
    bB = k.sbuf("bB", [128, D], F32)
    k.dma("sp", gB[:], g.partition_broadcast(128), w=[gB])
    k.dma("sp", bB[:], b.partition_broadcast(128), w=[bB])
    pm = [k.psum("pmc%d" % i, [128, 512], F32) for i in range(4)]
    xs = [k.sbuf("xsc%d" % i, [128, D], F32) for i in range(2)]
    zs = [k.sbuf("zsc%d" % i, [128, D], F32) for i in range(2)]
    scr = [{"st": k.sbuf("stc%d" % i, [128, 4, 6], F32), "mv": k.sbuf("mvc%d" % i, [128, 4], F32)} for i in range(2)]
    npm = 0
    for tt in range(ntt):
        xt = xs[tt % 2]
        z = zs[tt % 2]
        k.dma("sp", xt[:], x[tt * 128:(tt + 1) * 128, :], w=[xt])
        for cb in range(4):
            p = pm[npm % 4]
            npm += 1
            for kc in range(16):
                k.op("pe", lambda e, p=p, kc=kc, tt=tt, cb=cb: e.matmul(
                    p[:, :], lhsT=yT[:, kc, tt * 128:(tt + 1) * 128], rhs=wo[:, kc, cb * 512:(cb + 1) * 512],
                    start=(kc == 0), stop=(kc == 15)), r=[yT.k(kc), wo.k(cb)], w=[p])
            k.op("dve", lambda e, p=p, z=z, xt=xt, cb=cb: e.scalar_tensor_tensor(
                out=z[:, cb * 512:(cb + 1) * 512], in0=xt[:, cb * 512:(cb + 1) * 512], scalar=float(ALPHA),
                in1=p[:, :], op0=ALU.mult, op1=ALU.add), r=[p, xt], w=[z])
        ln_rows(k, z, z, gB, bB, D, z, scr[tt % 2])
        k.op("dve", lambda e, z=z: e.tensor_tensor(out=z[:, :], in0=z[:, :], in1=bB[:, :], op=ALU.add), r=[z, bB], w=[z])
        k.dma("sp", xout[tt * 128:(tt + 1) * 128, :], z[:, :], r=[z], is_output=True)


def build_C():
    nc = bass.Bass("TRN2", target_bir_lowering=False)
    yT = nc.dram_tensor("yT", [D, 2048], BF16, kind="ExternalInput").ap()
    x = nc.dram_tensor("x", [2048, D], F32, kind="ExternalInput").ap()
    w = nc.dram_tensor("w", [D, D], F32, kind="ExternalInput").ap()
    g = nc.dram_tensor("g", [D], F32, kind="ExternalInput").ap()
    b = nc.dram_tensor("b", [D], F32, kind="ExternalInput").ap()
    xout = nc.dram_tensor("xout", [2048, D], F32, kind="ExternalOutput").ap()
    k = KB(nc)
    emit_C(k, yT, x, w, g, b, xout)
    k.finish()
    return nc


def emit_D(k, x, w_in, og, ob, sgw_t, sgb, triu, w_out, g, b, ident, xout, ntok=2048):
    G = 512
    ngrp = ntok // G
    idf, idb = load_ident(k, ident)
    pT = [k.psum("pT%d" % i, [128, 1024], BF16) for i in range(2)]
    pm = [k.psum("pm%d" % i, [128, 512], F32) for i in range(6)]
    gB = k.sbuf("gB", [128, D], F32)
    bB = k.sbuf("bB", [128, D], F32)
    k.dma("sp", gB[:], g.partition_broadcast(128), w=[gB])
    k.dma("sp", bB[:], b.partition_broadcast(128), w=[bB])
    ogc = k.sbuf("ogc", [128, 16], F32)
    obc = k.sbuf("obc", [128, 16], F32)
    k.dma("sp", ogc[:], og.rearrange("(f p) -> p f", p=128), w=[ogc], allow_slow_non_contiguous=True)
    k.dma("sp", obc[:], ob.rearrange("(f p) -> p f", p=128), w=[obc], allow_slow_non_contiguous=True)
    wcf = k.sbuf("wcf", [128, 8, 128], F32)
    tri = k.sbuf("tri", [128, 128], F32)
    wct = k.sbuf("wct", [128, 8, 128], BF16)
    ones = k.sbuf("ones", [128, 128], BF16)
    k.op("dve", lambda e: e.memset(ones[:], 1.0), w=[ones])
    k.dma("sp", wcf[:], sgw_t.rearrange("g s t -> s g t"), w=[wcf])
    k.dma("sp", tri[:], triu[:, :], w=[tri])
    for gq in range(8):
        k.op("dve", lambda e, gq=gq: e.tensor_tensor(out=wct[:, gq, :], in0=wcf[:, gq, :], in1=tri[:, :], op=ALU.mult),
             r=[wcf, tri], w=[wct])
    sb1 = k.sbuf("sb1", [128, 8, 128], F32)
    k.dma("sp", sb1[:].rearrange("p g t -> p (g t)"), sgb.partition_broadcast(128), w=[sb1])
    bias2 = k.sbuf("bias2", [128, 16, 128], F32)
    for half in range(2):
        p = pm[half]
        k.op("pe", lambda e, p=p, half=half: e.matmul(
            p[:, :], lhsT=ones[:, :], rhs=wct[:, half * 4:(half + 1) * 4, :].rearrange("p a b -> p (a b)"),
            start=True, stop=True), r=[ones, wct], w=[p])
        for f in range(half * 8, half * 8 + 8):
            gq = f // 2
            gl = gq - half * 4
            k.op("dve", lambda e, p=p, f=f, gq=gq, gl=gl: e.scalar_tensor_tensor(
                out=bias2[:, f, :], in0=p[:, gl * 128:(gl + 1) * 128], scalar=obc[:, f:f + 1], in1=sb1[:, gq, :],
                op0=ALU.mult, op1=ALU.add), r=[p, obc, sb1], w=[bias2])

    wb = [k.sbuf("wb%d" % i, [128, 16, 512], BF16) for i in range(3)]
    xT = k.sbuf("xT", [128, 16, G], BF16, nsub=4)
    zs = k.sbuf("zso", [128, 4, D], F32, nsub=4)
    vg = k.sbuf("vgo", [128, 4, D], BF16, nsub=4)
    vn = k.sbuf("vn", [128, 4, D], BF16, nsub=4)
    yT = k.sbuf("yTo", [128, 16, G], BF16, nsub=16)
    xs = [k.sbuf("xso%d" % i, [128, D], F32) for i in range(1)]
    gu = [k.sbuf("gu%d" % i, [128, 512], F32) for i in range(2)]
    sz = [k.sbuf("sz%d" % i, [128, 512], F32) for i in range(2)]
    mm = [k.sbuf("mm%d" % i, [128, 512], F32) for i in range(1)]
    scr = [{"st": k.sbuf("sto%d" % i, [128, 4, 6], F32), "mv": k.sbuf("mvo%d" % i, [128, 4], F32)} for i in range(2)]
    wv = w_in.rearrange("(c p) n -> p c n", p=128)
    wov = w_out.rearrange("(c p) n -> p c n", p=128)
    cnt = {"w": 0, "p": 0, "e": 0, "s": 0, "x": 0}

    def nextw():
        t = wb[cnt["w"] % 3]
        cnt["w"] += 1
        return t

    def nextp():
        t = pm[cnt["p"] % 6]
        cnt["p"] += 1
        return t

    deferred = []
    for grp in range(ngrp):
        t0 = grp * G
        if grp == 0:
            k.tag = "xT"
            _load_xT_grp(k, x, t0, idb, xT, pT)
        k.tag = "p1"
        for cb in range(4):
            wt = nextw()
            c0 = 2048 + cb * 512
            k.dma("pool", wt[:], wv[:, :, c0:c0 + 512], w=[wt])
            for t4 in range(4):
                p = nextp()
                for kc in range(16):
                    k.op("pe", lambda e, p=p, wt=wt, kc=kc, t4=t4: e.matmul(
                        p[:, :], lhsT=xT[:, kc, t4 * 128:(t4 + 1) * 128], rhs=wt[:, kc, :],
                        start=(kc == 0), stop=(kc == 15)), r=[wt, xT.k(t4)], w=[p])
                k.op("act", lambda e, p=p, t4=t4, cb=cb: e.activation(
                    out=vg[:, t4, cb * 512:(cb + 1) * 512], in_=p[:, :], func=AF.Gelu_apprx_tanh), r=[p], w=[vg.k(t4)])
        k.tag = "ln1"
        for t4 in range(4):
            s = scr[cnt["s"] % 2]
            cnt["s"] += 1
            ln_rows(k, vg[:, t4, :], (vn[:, t4, :], vn.k(t4)), None, None, D, vg.k(t4), s)
        k.tag = "p2"
        for cb8 in range(8):
            if cb8 % 2 == 1 and deferred:
                deferred.pop(0)()
                k.tag = "p2"
            wt = nextw()
            k.dma("pool", wt[:, :, 0:256], wv[:, :, cb8 * 256:(cb8 + 1) * 256], w=[wt])
            k.dma("pool", wt[:, :, 256:512], wv[:, :, 4096 + cb8 * 256:4096 + (cb8 + 1) * 256], w=[wt])
            for fc in range(2):
                f = cb8 * 2 + fc
                gq = f // 2
                pu = nextp()
                pz = nextp()
                px = nextp()
                def mm_u():
                    for kc in range(16):
                        k.op("pe", lambda e, p=pu, wt=wt, kc=kc, fc=fc: e.matmul(
                            p[:, :], lhsT=wt[:, kc, fc * 128:(fc + 1) * 128], rhs=xT[:, kc, :],
                            start=(kc == 0), stop=(kc == 15)), r=[wt, xT], w=[pu])

                def mm_z():
                    for kc in range(16):
                        k.op("pe", lambda e, p=pz, wt=wt, kc=kc, fc=fc: e.matmul(
                            p[:, :], lhsT=wt[:, kc, 256 + fc * 128:256 + (fc + 1) * 128], rhs=xT[:, kc, :],
                            start=(kc == 0), stop=(kc == 15)), r=[wt, xT], w=[pz])
                if f % 2 == 0:
                    mm_u()
                    mm_z()
                else:
                    mm_z()
                    mm_u()
                for t4 in range(4):
                    k.op("pe", lambda e, p=px, t4=t4, f=f, gq=gq: e.matmul(
                        p[:, t4 * 128:(t4 + 1) * 128], lhsT=vn[:, t4, f * 128:(f + 1) * 128], rhs=wct[:, gq, :],
                        start=True, stop=True), r=[vn.k(t4), wct], w=[px])
                i = cnt["e"] % 2
                cnt["e"] += 1
                g_, s_, m_ = gu[i], sz[i], mm[0]
                def act_u():
                    k.op("act", lambda e, g_=g_, pu=pu: e.activation(out=g_[:], in_=pu[:, :], func=AF.Gelu_apprx_tanh), r=[pu], w=[g_])

                def act_z():
                    k.op("act", lambda e, s_=s_, pz=pz: e.activation(out=s_[:], in_=pz[:, :], func=AF.Silu), r=[pz], w=[s_])
                if f % 2 == 0:
                    act_u()
                    act_z()
                else:
                    act_z()
                    act_u()
                for t4 in range(4):
                    k.op("dve", lambda e, m_=m_, px=px, f=f, t4=t4: e.scalar_tensor_tensor(
                        out=m_[:, t4 * 128:(t4 + 1) * 128], in0=px[:, t4 * 128:(t4 + 1) * 128], scalar=ogc[:, f:f + 1],
                        in1=bias2[:, f, :], op0=ALU.mult, op1=ALU.add), r=[px, ogc, bias2], w=[m_])
                k.op("dve", lambda e, m_=m_, g_=g_: e.tensor_tensor(out=m_[:], in0=m_[:], in1=g_[:], op=ALU.mult),
                     r=[m_, g_], w=[m_])
                k.op("dve", lambda e, m_=m_, s_=s_, f=f: e.tensor_tensor(out=yT[:, f, :], in0=m_[:], in1=s_[:], op=ALU.mult),
                     r=[m_, s_], w=[yT.k(f)])
        if grp + 1 < ngrp:
            k.tag = "xT"
            _load_xT_grp(k, x, t0 + G, idb, xT, pT)
        k.tag = "p3"
        for cb in range(4):
            wt = nextw()
            k.dma("pool", wt[:], wov[:, :, cb * 512:(cb + 1) * 512], w=[wt])
            for t4 in range(4):
                p = nextp()
                for kc in range(16):
                    k.op("pe", lambda e, p=p, wt=wt, kc=kc, t4=t4: e.matmul(
                        p[:, :], lhsT=yT[:, kc, t4 * 128:(t4 + 1) * 128], rhs=wt[:, kc, :],
                        start=(kc == 0), stop=(kc == 15)), r=[wt, yT.k(kc)], w=[p])
                k.op("act", lambda e, p=p, t4=t4, cb=cb: e.copy(out=zs[:, t4, cb * 512:(cb + 1) * 512], in_=p[:, :]),
                     r=[p], w=[zs.k(t4)])
        def ln3(t4, t0=t0):
            k.tag = "ln3"
            if True:
                xt = xs[0]
                cnt["x"] += 1
                k.dma("sp", xt[:], x[t0 + t4 * 128:t0 + (t4 + 1) * 128, :], w=[xt])
                k.op("dve", lambda e, t4=t4, xt=xt: e.scalar_tensor_tensor(
                    out=zs[:, t4, :], in0=xt[:, :], scalar=float(ALPHA), in1=zs[:, t4, :], op0=ALU.mult, op1=ALU.add),
                    r=[xt, zs.k(t4)], w=[zs.k(t4)])
                s = scr[cnt["s"] % 2]
                cnt["s"] += 1
                ln_rows(k, zs[:, t4, :], None, gB, bB, D, zs.k(t4), s)
                k.op("dve", lambda e, t4=t4: e.tensor_tensor(out=zs[:, t4, :], in0=zs[:, t4, :], in1=bB[:, :], op=ALU.add),
                     r=[zs.k(t4), bB], w=[zs.k(t4)])
                k.dma("sp", xout[t0 + t4 * 128:t0 + (t4 + 1) * 128, :], zs[:, t4, :], r=[zs.k(t4)], is_output=True)
        for t4_ in range(4):
            deferred.append(lambda t4_=t4_, ln3=ln3: ln3(t4_))
    while deferred:
        deferred.pop(0)()


_xb_tiles = {}


def _load_xT_grp(k, x, t0, idb, xT, pT):
    if (id(k), k.phase_no) not in _xb_tiles:
        _xb_tiles[(id(k), k.phase_no)] = [k.sbuf("xbo%d" % i, [128, D], BF16) for i in range(2)]
    xb = _xb_tiles[(id(k), k.phase_no)]
    n = 0
    for t4 in range(4):
        b = xb[t4 % 2]
        k.dma("pool", b[:], x[t0 + t4 * 128:t0 + (t4 + 1) * 128, :], w=[b])
        for g in range(4):
            p = pT[n % len(pT)]
            for j in range(4):
                kc = g * 4 + j
                k.op("pe", lambda e, p=p, b=b, kc=kc, j=j: e.transpose(
                    p[:, j * 128:(j + 1) * 128], b[:, kc * 128:(kc + 1) * 128], idb[:]), r=[b, idb], w=[p])
            src = p[:, 0:512].rearrange("p (a b) -> p a b", a=4)
            dst = xT[:, g * 4:(g + 1) * 4, t4 * 128:(t4 + 1) * 128]
            if n % 2 == 0:
                k.op("dve", lambda e, dst=dst, src=src: e.tensor_copy(out=dst, in_=src), r=[p], w=[xT.k(t4)])
            else:
                k.op("act", lambda e, dst=dst, src=src: e.copy(out=dst, in_=src), r=[p], w=[xT.k(t4)])
            n += 1


def build_D(ntok=2048):
    nc = bass.Bass("TRN2", target_bir_lowering=False)
    x = nc.dram_tensor("x", [ntok, D], F32, kind="ExternalInput").ap()
    w_in = nc.dram_tensor("w_in", [D, 3 * D], F32, kind="ExternalInput").ap()
    og = nc.dram_tensor("og", [D], F32, kind="ExternalInput").ap()
    ob = nc.dram_tensor("ob", [D], F32, kind="ExternalInput").ap()
    sgw_t = nc.dram_tensor("sgw_t", [8, 128, 128], F32, kind="ExternalInput").ap()
    sgb = nc.dram_tensor("sgb", [1024], F32, kind="ExternalInput").ap()
    triu = nc.dram_tensor("triu", [128, 128], F32, kind="ExternalInput").ap()
    w_out = nc.dram_tensor("w_out", [D, D], F32, kind="ExternalInput").ap()
    g = nc.dram_tensor("g", [D], F32, kind="ExternalInput").ap()
    b = nc.dram_tensor("b", [D], F32, kind="ExternalInput").ap()
    ident = nc.dram_tensor("ident", [128, 128], F32, kind="ExternalInput").ap()
    xout = nc.dram_tensor("xout", [ntok, D], F32, kind="ExternalOutput").ap()
    k = KB(nc)
    emit_D(k, x, w_in, og, ob, sgw_t, sgb, triu, w_out, g, b, ident, xout, ntok)
    k.finish()
    return nc

import math

S_LEN = 4096
NQT = 32
NEG = -30000.0
DMIN = -2063
NL = 4608
SIMDBG = False
SUB = 99
DEPTH = 2


def t5_bucket_np(d):
    n = np.maximum(d, 0)
    large = 16 + (np.log(np.maximum(n, 1).astype(np.float32) / 16) / math.log(128 / 16) * 16).astype(np.int32)
    large = np.minimum(large, 31)
    return np.where(n < 16, n, large)


def host_consts_B():
    c = {}
    c["ident"] = np.eye(128, dtype=np.float32)
    c["jflip"] = np.ascontiguousarray(np.eye(128, dtype=np.float32)[::-1])
    d = DMIN + np.arange(NL)
    oh = np.zeros((33, NL), np.float32)
    bk = t5_bucket_np(d)
    oh[bk[d >= 0], np.nonzero(d >= 0)[0]] = 1.0
    oh[32, d < 0] = 1.0
    c["onehot"] = oh
    es = np.zeros((64, 32, 128), np.float32)
    for cc in range(32):
        es[2 * cc, cc, 0:64] = 1.0
        es[2 * cc + 1, cc, 64:128] = 1.0
    c["esel"] = es.reshape(64, 4096)
    n = np.arange(256)[:, None]
    j = np.arange(64)[None, :]
    ov = ((16 * n < 64 * j + 64) & (16 * n + 32 > 64 * j)).astype(np.float32)
    ovx = np.concatenate([ov, np.ones((256, 1), np.float32)], axis=1)
    ovx[255, :] = 0.0
    c["ovx"] = ovx
    keep = np.zeros((32, 128, 64), np.float32)
    fix = np.zeros((32, 128, 64), np.float32)
    for qt in range(32):
        t = qt * 128 + np.arange(128)
        cur = (t // 64)[:, None]
        blk = np.arange(64)[None, :]
        forced = (blk == 0) | (blk == cur) | (blk == cur - 1)
        fut = blk > cur
        keep[qt] = (~forced & ~fut).astype(np.float32)
        f = np.zeros((128, 64), np.float32)
        f = np.where(blk == cur - 1, 1e9, f)
        f = np.where(blk == cur, 2e9, f)
        f = np.where(blk == 0, 3e9, f)
        f = np.where(fut, -1.0 - 0.001 * blk, f)
        fix[qt] = f
    c["keep"] = keep
    c["fix"] = fix
    qi = np.arange(128)[None, :]
    ki = np.arange(128)[:, None]
    bw4 = np.where(qi < ki, 0.0, NEG).astype(np.float32)
    c["bw4"] = np.tile(bw4, (1, 4))
    return c


CONST_SHAPES = {"ident": [128, 128], "jflip": [128, 128], "onehot": [33, NL], "esel": [64, 4096], "ovx": [256, 65],
                "keep": [32, 128, 64], "fix": [32, 128, 64], "bw4": [128, 512]}


def emit_B(k, nc, I, yT_d, stage=99, dbg=None, lut_name="lutd"):
    idf, idb = load_ident(k, I["ident"])
    S = [k.psum("S%d" % i, [128, 512], F32) for i in range(2)]
    OA = [k.psum("OA%d" % i, [128, 512], F32) for i in range(2)]
    OS = [k.psum("OS%d" % i, [128, 512], F32) for i in range(2)]
    PT = k.psum("PT", [128, 1024], BF16)
    PX = k.psum("PX", [128, 512], F32)

    cin = I.get("cin")
    cw = k.sbuf("cw", [128, 4, 3], F32)
    if cin is not None:
        k.dma("sp", cw[:], I["conv_w"].rearrange("(c p) k -> p c k", p=128), w=[cw])
    TP = 512
    cb_ = {n: [k.sbuf("cv_%s%d" % (n, i), [128, TP + 2], BF16) for i in range(2)] for n in ("h", "b", "c", "z")}
    pbuf = [k.sbuf("cv_p%d" % i, [128, TP + 2], F32) for i in range(2)]
    abuf = [k.sbuf("cv_a%d" % i, [128, TP], F32) for i in range(2)]
    zbuf = [k.sbuf("cv_s%d" % i, [128, TP], F32) for i in range(2)]
    ybuf = [k.sbuf("cv_y%d" % i, [128, TP], BF16) for i in range(2)]
    cin = I.get("cin")
    conv_jobs = []
    it = 0
    for cc in range(4):
        for tp in range(S_LEN // TP):
            conv_jobs.append((cc, tp, it % 2))
            it += 1
    ebuf = [k.sbuf("cv_e%d" % i, [128, TP], F32) for i in range(2)]

    def conv_piece(cc, tp, i):
        k.tag = "conv"
        t0 = tp * TP
        bh, bb, bc, bz = cb_["h"][i], cb_["b"][i], cb_["c"][i], cb_["z"][i]
        rows = slice(cc * 128, (cc + 1) * 128)
        if tp == 0:
            k.op("pool", lambda e, bh=bh: e.memset(bh[:, 0:2], 0.0), w=[bh])
            k.op("pool", lambda e, bc=bc: e.memset(bc[:, 0:2], 0.0), w=[bc])
            k.dma("sp", bh[:, 2:], cin[0, rows, 0:TP], w=[bh])
            k.dma("sp", bc[:, 2:], cin[2, rows, 0:TP], w=[bc])
        else:
            k.dma("sp", bh[:, :], cin[0, rows, t0 - 2:t0 + TP], w=[bh])
            k.dma("sp", bc[:, :], cin[2, rows, t0 - 2:t0 + TP], w=[bc])
        k.dma("sp", bb[:, 0:TP], cin[1, rows, t0:t0 + TP], w=[bb])
        k.dma("sp", bz[:, 0:TP], cin[3, rows, t0:t0 + TP], w=[bz])
        p, a, sz, y, ex = pbuf[i], abuf[i], zbuf[i], ybuf[i], ebuf[i]
        k.op("dve", lambda e: e.tensor_tensor(out=p[:], in0=bc[:], in1=bh[:], op=ALU.mult), r=[bc, bh], w=[p])
        k.op("dve", lambda e: e.tensor_scalar(out=a[:], in0=p[:, 2:TP + 2], scalar1=cw[:, cc, 2:3], scalar2=None, op0=ALU.mult), r=[p, cw], w=[a])
        k.op("dve", lambda e: e.scalar_tensor_tensor(out=a[:], in0=p[:, 1:TP + 1], scalar=cw[:, cc, 1:2], in1=a[:], op0=ALU.mult, op1=ALU.add), r=[p, cw, a], w=[a])
        k.op("dve", lambda e: e.scalar_tensor_tensor(out=a[:], in0=p[:, 0:TP], scalar=cw[:, cc, 0:1], in1=a[:], op0=ALU.mult, op1=ALU.add), r=[p, cw, a], w=[a])
        k.op("act", lambda e: e.activation(out=sz[:], in_=bz[:, 0:TP], func=AF.Silu), r=[bz], w=[sz])
        k.op("pool", lambda e: e.tensor_tensor(out=a[:], in0=a[:], in1=bb[:, 0:TP], op=ALU.mult), r=[a, bb], w=[a])
        k.op("pool", lambda e: e.tensor_tensor(out=y[:], in0=a[:], in1=sz[:], op=ALU.mult), r=[a, sz], w=[y])
        k.dma("sp", yT_d[rows, t0:t0 + TP], y[:], r=[y], is_output=True)

    while conv_jobs and cin is not None:
        conv_piece(*conv_jobs.pop(0))
    if stage <= 1:
        return
    qT4 = k.sbuf("qT4", [128, NQT, 4, 128], BF16)
    for h in range(4):
        k.dma("sp", qT4[:, :, h, :], I["qT"][h].rearrange("d (t q) -> d t q", q=128), w=[qT4])
    kin = {}
    for n in ("kc", "vc", "ks", "kw"):
        kin[n] = k.sbuf("in_" + n, [128, S_LEN], BF16)
        k.dma("sp", kin[n][:], I[n + "T"][:, :], w=[kin[n]])
    vs_ext = k.sbuf("vs_ext", [128, 32, 129], BF16)
    vw_ext = k.sbuf("vw_ext", [128, 32, 129], BF16)
    for t, n in ((vs_ext, "vs"), (vw_ext, "vw")):
        k.dma("sp", t[:, :, 0:128], I[n].rearrange("(c p) d -> p c d", p=128), w=[t])
        k.op("pool", lambda e, t=t: e.memset(t[:, :, 128:129], 1.0), w=[t])
    esel = k.sbuf("esel", [64, 4096], BF16)
    k.dma(("sp" if SIMDBG else "pool"), esel[:], I["esel"][:, :], w=[esel])
    jfl = k.sbuf("jfl", [128, 128], BF16)
    k.dma(("sp" if SIMDBG else "pool"), jfl[:], I["jflip"][:, :], w=[jfl])
    bw4 = k.sbuf("bw4", [128, 512], BF16)
    k.dma(("sp" if SIMDBG else "pool"), bw4[:], I["bw4"][:, :], w=[bw4])
    graw = k.sbuf("graw", [128, NQT, 12], BF16)
    k.dma("sp", graw[:], I["gates"].rearrange("(t p) c -> p t c", p=128), w=[graw])
    gs = k.sbuf("gs", [128, NQT, 12], F32)
    k.op("act", lambda e: e.activation(out=gs[:], in_=graw[:], func=AF.Exp, scale=-1.0), r=[graw], w=[gs])
    k.op("dve", lambda e: e.tensor_scalar(out=gs[:], in0=gs[:], scalar1=1.0, scalar2=None, op0=ALU.add), r=[gs], w=[gs])
    k.op("dve", lambda e: e.reciprocal(out=gs[:], in_=gs[:]), r=[gs], w=[gs])

    if stage <= 1.2:
        return
    tab = k.sbuf("tab", [33, 4], F32)
    t31 = k.sbuf("t31", [32, 4], F32)
    tabb = k.sbuf("tabb", [33, 4], BF16)
    k.dma("sp", tab[0:32, :], I["table"][:, :], w=[tab])
    k.dma("sp", t31[:], I["table31"][:, :], w=[t31])
    k.op("dve", lambda e: e.memset(tabb[:], NEG), w=[tabb])
    k.op("dve", lambda e: e.tensor_tensor(out=tabb[0:32, :], in0=tab[0:32, :], in1=t31[:], op=ALU.subtract), r=[tab, t31, tabb], w=[tabb])
    oh = k.sbuf("oh", [33, NL], BF16)
    k.dma(("sp" if SIMDBG else "pool"), oh[:], I["onehot"][:, :], w=[oh])
    lutd = nc.dram_tensor(lut_name, [4, NL], F32).ap()
    lst = [k.sbuf("lst%d" % i, [4, 512], F32) for i in range(2)]
    for i in range(NL // 512):
        k.op("pe", lambda e, i=i: e.matmul(PX[0:4, :], lhsT=tabb[:, :], rhs=oh[:, i * 512:(i + 1) * 512], start=True, stop=True),
             r=[tabb, oh], w=[PX])
        st_ = lst[i % 2]
        k.op("dve", lambda e, st_=st_: e.tensor_copy(out=st_[:, :], in_=PX[0:4, :]), r=[PX], w=[st_])
        k.dma("sp", lutd[:, i * 512:(i + 1) * 512], st_[:, :], r=[st_], w=["lutd"])
    if stage <= 1.5:
        return
    BC = k.sbuf("BC", [128, 17, 512], BF16)
    BD = k.sbuf("BD", [128, 2, 512], BF16)
    hkf = [k.sbuf("hkf%d" % i, [128, 4, 128], F32) for i in range(2)]
    hk = [k.sbuf("hk%d" % i, [128, 4, 128], BF16) for i in range(2)]
    specs = []
    for dl in range(17):
        specs.append((BC[:, dl, :], 128 * dl - 31 - 2032, 16))
    specs.append((BD[:, 0, :], 0 - 127, 1))
    specs.append((BD[:, 1, :], 128 - 127, 1))
    for i, (dst, c0, pstr) in enumerate(specs):
        hh = hk[i % 2]
        src = bass.AP(lutd.tensor, c0 - DMIN, [[pstr, 128], [NL, 4], [1, 128]])
        hf = hkf[i % 2]
        k.dma("sp", hf[:], src, r=["lutd"], w=[hf])
        k.op("pool", lambda e, hh=hh, hf=hf: e.tensor_copy(out=hh[:], in_=hf[:]), r=[hf], w=[hh])
        k.op("pe", lambda e, hh=hh: e.matmul(PX[:, :], lhsT=jfl[:, :], rhs=hh[:].rearrange("p a b -> p (a b)"), start=True, stop=True),
             r=[jfl, hh], w=[PX])
        dres = BC if i < 17 else BD
        k.op("dve", lambda e, dst=dst: e.tensor_copy(out=dst, in_=PX[:, :]), r=[PX], w=[dres])

    if stage <= 2:
        dt = k.sbuf("dbgt", [128, 2048], F32)
        k.op("dve", lambda e: e.tensor_copy(out=dt[:, 0:512], in_=BD[:, 0, :]), r=[BD], w=[dt])
        k.op("dve", lambda e: e.tensor_copy(out=dt[:, 512:1024], in_=BD[:, 1, :]), r=[BD], w=[dt])
        k.op("dve", lambda e: e.tensor_copy(out=dt[:, 1024:1536], in_=BC[:, 0, :]), r=[BC], w=[dt])
        k.op("dve", lambda e: e.tensor_copy(out=dt[:, 1536:2048], in_=BC[:, 16, :]), r=[BC], w=[dt])
        k.dma("sp", dbg[:, 0:2048], dt[:], r=[dt], is_output=True)
        return
    vc_ext = k.sbuf("vc_ext", [128, 2, 193], BF16)
    k.dma(("sp" if SIMDBG else "pool"), vc_ext[:, :, 128:193], I["ovx"].rearrange("(c p) j -> p c j", p=128), w=[vc_ext])
    kcT = k.sbuf("kcT", [128, 256], BF16)
    w1 = k.sbuf("w1", [128, 32, 128], BF16)
    w2 = k.sbuf("w2", [128, 128], BF16)
    posf = k.sbuf("posf", [128, 32], F32)
    posb = k.sbuf("posb", [128, 32], BF16)
    pb = k.sbuf("pb", [128, 1], F32)
    hidT = k.sbuf("hidT", [128, 256], BF16)
    for kv, src in ((0, kin["kc"]), (1, kin["vc"])):
        k.dma(("sp" if SIMDBG else "pool"), w1[:], I["cmp_w1"][kv].rearrange("(l d) h -> d l h", d=128), w=[w1])
        k.dma(("sp" if SIMDBG else "pool"), w2[:], I["cmp_w2"][kv], w=[w2])
        k.dma("sp", posf[:], I["cmp_pos"][kv].rearrange("l d -> d l"), w=[posf], allow_slow_non_contiguous=True)
        k.op("dve", lambda e: e.tensor_copy(out=posb[:], in_=posf[:]), r=[posf], w=[posb])
        for l in range(32):
            k.op("pe", lambda e, l=l: e.matmul(PX[:, 0:1], lhsT=w1[:, l, :], rhs=posb[:, l:l + 1], start=(l == 0), stop=(l == 31)),
                 r=[w1, posb], w=[PX])
        k.op("dve", lambda e: e.tensor_copy(out=pb[:], in_=PX[:, 0:1]), r=[PX], w=[pb])
        for l in range(32):
            k.op("pe", lambda e, l=l, src=src: e.matmul(S[0][:, 0:255], lhsT=w1[:, l, :], rhs=src[:, l:l + 16 * 254 + 1:16],
                                                        start=(l == 0), stop=(l == 31)), r=[w1, src], w=[S[0]])
        k.op("dve", lambda e: e.memset(hidT[:], 0.0), w=[hidT])
        k.op("act", lambda e: e.activation(out=hidT[:, 0:255], in_=S[0][:, 0:255], func=AF.Silu, bias=pb[:, 0:1]), r=[S[0], pb], w=[hidT])
        if kv == 0:
            k.op("pe", lambda e: e.matmul(S[1][:, 0:256], lhsT=w2[:, :], rhs=hidT[:, :], start=True, stop=True), r=[w2, hidT], w=[S[1]])
            k.op("dve", lambda e: e.tensor_copy(out=kcT[:], in_=S[1][:, 0:256]), r=[S[1]], w=[kcT])
        else:
            for c in range(2):
                k.op("pe", lambda e, c=c: e.matmul(S[1][:, c * 128:(c + 1) * 128], lhsT=hidT[:, c * 128:(c + 1) * 128], rhs=w2[:, :],
                                                   start=True, stop=True), r=[w2, hidT], w=[S[1]])
            k.op("dve", lambda e: e.tensor_copy(out=vc_ext[:, :, 0:128], in_=S[1][:, 0:256].rearrange("p (c d) -> p c d", c=2)),
                 r=[S[1]], w=[vc_ext])

    if stage <= 3:
        dt = k.sbuf("dbgt", [128, 2048], F32)
        k.op("dve", lambda e: e.tensor_copy(out=dt[:, 0:256], in_=kcT[:, :]), r=[kcT], w=[dt])
        k.op("dve", lambda e: e.tensor_copy(out=dt[:, 256:256 + 386], in_=vc_ext[:, :, :].rearrange("p a b -> p (a b)")), r=[vc_ext], w=[dt])
        k.dma("sp", dbg[:, 0:1024], dt[:, 0:1024], r=[dt], is_output=True)
        return
    Eb = [k.sbuf("Eb%d" % i, [128, 512], BF16) for i in range(6)]
    oacc = [k.sbuf("oacc%d" % i, [128, 512], F32) for i in range(2)]
    mbT4 = [k.sbuf("mbT%d" % i, [64, 512], BF16) for i in range(2)]
    keepb = [k.sbuf("keep%d" % i, [128, 64], F32) for i in range(2)]
    fixb = [k.sbuf("fix%d" % i, [128, 64], F32) for i in range(2)]
    sm = [k.sbuf("sm%d" % i, [128, 64], F32) for i in range(2)]
    imp = [k.sbuf("imp%d" % i, [128, 64], F32) for i in range(2)]
    wk2 = [k.sbuf("wk2%d" % i, [128, 64], F32) for i in range(2)]
    mx = [k.sbuf("mx%d" % i, [128, 24], F32) for i in range(2)]
    mb = [k.sbuf("mb%d" % i, [128, 64], BF16) for i in range(2)]
    bzb = [k.sbuf("bzb%d" % i, [128, 512], BF16) for i in range(2)]
    sg = [k.sbuf("sg%d" % i, [128, 512], F32) for i in range(2)]
    yb = [k.sbuf("yb%d" % i, [128, 512], BF16) for i in range(2)]
    ybT = [k.sbuf("ybT%d" % i, [128, 512], BF16) for i in range(2)]
    cnt = {"S": 0, "E": 0}

    S3 = [S[0], S[1], PX]

    def nextS():
        t = S3[cnt["S"] % 3]
        cnt["S"] += 1
        return t

    def nextE():
        t = Eb[cnt["E"] % 6]
        cnt["E"] += 1
        return t

    EbC = [k.sbuf("EbC%d" % i, [128, 512], BF16) for i in range(4)]

    def score_tile(lhsT_k, qt, extra, eb=None):
        s = nextS()
        n = len(extra)
        k.op("pe", lambda e: e.matmul(s[:, :], lhsT=lhsT_k[0], rhs=qT4[:, qt, :, :].rearrange("p h q -> p (h q)"),
                                      start=True, stop=(n == 0)), r=[lhsT_k[1], qT4], w=[s], tag=(k.tag or "") + ".qk")
        for i, (l, r_, rd) in enumerate(extra):
            k.op("pe", lambda e, l=l, r_=r_, i=i: e.matmul(s[:, :], lhsT=l, rhs=r_, start=False, stop=(i == n - 1)), r=rd, w=[s], tag=(k.tag or "") + ".x%d" % i)
        if eb is None:
            eb = nextE()
        k.op("act", lambda e: e.activation(out=eb[:], in_=s[:, :], func=AF.Exp), r=[s], w=[eb])
        return eb

    def pv(acc, width, eb, rhs_ap, rd, first=False):
        for h in range(4):
            a = acc[h // 2]
            o0 = (h % 2) * width
            st = bool(first and h % 2 == 0)
            k.op("pe", lambda e, a=a, o0=o0, h=h, st=st: e.matmul(a[:, o0:o0 + rhs_ap.shape[-1]], lhsT=eb[:, h * 128:(h + 1) * 128], rhs=rhs_ap,
                                                                   start=st, stop=True, skip_group_check=True), r=[eb] + rd, w=[a], tag=(k.tag or "") + ".pv%d" % h)

    def finish(acc, width, dcol, qt, branch, first, smt):
        for i in range(2):
            a = acc[i]
            dv = a[:, 0:2 * width].rearrange("p (h c) -> p h c", h=2)[:, :, dcol:dcol + 1]
            k.op("dve", lambda e, dv=dv, i=i: e.tensor_scalar(out=smt[:, 2 * i:2 * i + 2].rearrange("p (h c) -> p h c", c=1), in0=dv,
                                                              scalar1=1e-30, scalar2=None, op0=ALU.max), r=[a], w=[smt])
        k.op("dve", lambda e: e.reciprocal(out=smt[:, 4:8], in_=smt[:, 0:4]), r=[smt], w=[smt])
        k.op("dve", lambda e: e.tensor_tensor(out=smt[:, 8:12], in0=smt[:, 4:8], in1=gs[:, qt, branch * 4:branch * 4 + 4], op=ALU.mult),
             r=[smt, gs], w=[smt])
        oa = oacc[qt % 2]
        for h in range(4):
            a = acc[h // 2]
            o0 = (h % 2) * width
            if first:
                k.op("act", lambda e, a=a, o0=o0, h=h: e.activation(out=oa[:, h * 128:(h + 1) * 128], in_=a[:, o0:o0 + 128], func=AF.Copy,
                                                                     scale=smt[:, 8 + h:9 + h]), r=[a, smt], w=[oa])
            else:
                k.op("dve", lambda e, a=a, o0=o0, h=h: e.scalar_tensor_tensor(out=oa[:, h * 128:(h + 1) * 128], in0=a[:, o0:o0 + 128],
                                                                             scalar=smt[:, 8 + h:9 + h], in1=oa[:, h * 128:(h + 1) * 128],
                                                                             op0=ALU.mult, op1=ALU.add), r=[a, smt, oa], w=[oa])

    cstate = {}

    def Cqk_tile(qt):
        k.tag = "C"
        i = qt % 2
        k.dma("sp", keepb[i][:], I["keep"][qt], w=[keepb[i]])
        k.dma("sp", fixb[i][:], I["fix"][qt], w=[fixb[i]])
        lst = []
        for c in range(1 if qt < 16 else 2):
            dl = qt - 16 * c
            extra = []
            if dl <= 16:
                extra.append((idb[:, :], BC[:, dl, :], [idb, BC]))
            eb = score_tile((kcT[:, c * 128:(c + 1) * 128], kcT), qt, extra, eb=EbC[(qt % 2) * 2 + c])
            lst.append((OA, 193, eb, vc_ext[:, c, :], [vc_ext], c == 0))
        cstate[qt] = lst

    def C_tile(qt):
        k.tag = "C"
        i = qt % 2
        for args in cstate.pop(qt):
            pv(*args)
        if SUB <= 1:
            return
        smt = sm[i]
        finish(OA, 193, 192, qt, 0, True, smt)
        if SUB <= 2:
            return
        im = imp[i]
        for h in range(4):
            if SUB < 2.5 and h >= round((SUB - 2) * 10):
                return
            a = OA[h // 2]
            o0 = (h % 2) * 193 + 128
            if h == 0:
                k.op("dve", lambda e, a=a, o0=o0: e.tensor_scalar(out=im[:], in0=a[:, o0:o0 + 64], scalar1=smt[:, 4:5], scalar2=None, op0=ALU.mult),
                     r=[a, smt], w=[im])
            else:
                k.op("dve", lambda e, a=a, o0=o0, h=h: e.scalar_tensor_tensor(out=im[:], in0=a[:, o0:o0 + 64], scalar=smt[:, 4 + h:5 + h], in1=im[:],
                                                                             op0=ALU.mult, op1=ALU.add), r=[a, smt, im], w=[im])
        if SUB <= 2.5:
            return
        k.op("dve", lambda e: e.tensor_tensor(out=im[:], in0=im[:], in1=keepb[i][:], op=ALU.mult), r=[im, keepb[i]], w=[im])
        k.op("dve", lambda e: e.tensor_tensor(out=im[:], in0=im[:], in1=fixb[i][:], op=ALU.add), r=[im, fixb[i]], w=[im])
        if SUB <= 3:
            return
        m_, w2_ = mx[i], wk2[i]
        k.op("dve", lambda e: e.max(out=m_[:, 0:8], in_=im[:]), r=[im], w=[m_])
        k.op("dve", lambda e: e.match_replace(out=w2_[:], in_to_replace=m_[:, 0:8], in_values=im[:], imm_value=-1e30), r=[im, m_], w=[w2_])
        k.op("dve", lambda e: e.max(out=m_[:, 8:16], in_=w2_[:]), r=[w2_], w=[m_])
        if SUB <= 4:
            return
        k.op("dve", lambda e: e.tensor_reduce(out=m_[:, 16:17], in_=m_[:, 8:16], axis=AX.X, op=ALU.min), r=[m_], w=[m_])
        k.op("dve", lambda e: e.tensor_scalar(out=mb[i][:], in0=im[:], scalar1=m_[:, 16:17], scalar2=NEG, op0=ALU.is_lt, op1=ALU.mult),
             r=[im, m_], w=[mb[i]])
    def C2_tile(qt):
        k.tag = "C2"
        i = qt % 2
        k.op("pe", lambda e: e.transpose(PT[0:64, 0:128], mb[i][:, :], idb[:, :]), r=[mb[i], idb], w=[PT])
        for h in range(4):
            k.op("act", lambda e, h=h: e.copy(out=mbT4[i][:, h * 128:(h + 1) * 128], in_=PT[0:64, 0:128]), r=[PT], w=[mbT4[i]])

    sstate = {}

    def S_tile(qt, part):
        k.tag = "S"
        i = qt % 2
        if part == 0:
            sstate[qt] = {"pend": [], "started": False, "c": 0}
        st_ = sstate[qt]
        c_end = min(max(2, (qt + 1) // 2), qt + 1) if part == 0 else qt + 1
        for c in range(st_["c"], c_end):
            extra = [(esel[:, c * 128:(c + 1) * 128], mbT4[i][:, :], [esel, mbT4[i]])]
            if c == qt:
                extra.append((idb[:, :], BD[:, 0, :], [idb, BD]))
            elif c == qt - 1:
                extra.append((idb[:, :], BD[:, 1, :], [idb, BD]))
            eb = score_tile((kin["ks"][:, c * 128:(c + 1) * 128], kin["ks"]), qt, extra)
            st_["pend"].append((OS, 129, eb, vs_ext[:, c, :], [vs_ext], not st_["started"]))
            st_["started"] = True
            if len(st_["pend"]) > DEPTH:
                pv(*st_["pend"].pop(0))
        st_["c"] = c_end
        if part == 1:
            while st_["pend"]:
                pv(*st_["pend"].pop(0))
            finish(OS, 129, 128, qt, 1, False, sm[i])
            del sstate[qt]

    def W_tile(qt):
        k.tag = "W"
        i = qt % 2
        pend = []
        started = False
        for c in range(max(0, qt - 4), qt + 1):
            jj = qt - c
            extra = []
            if jj == 0:
                extra.append((idb[:, :], BD[:, 0, :], [idb, BD]))
            elif jj == 1:
                extra.append((idb[:, :], BD[:, 1, :], [idb, BD]))
            elif jj == 4:
                extra.append((idb[:, :], bw4[:, :], [idb, bw4]))
            eb = score_tile((kin["kw"][:, c * 128:(c + 1) * 128], kin["kw"]), qt, extra)
            pend.append((OA, 193, eb, vw_ext[:, c, :], [vw_ext], len(pend) == 0 and not started))
            started = True
            if len(pend) > DEPTH:
                pv(*pend.pop(0))
        while pend:
            pv(*pend.pop(0))
        finish(OA, 193, 128, qt, 2, False, sm[i])

    def F_tile(qt):
        k.tag = "F"
        i = qt % 2
        bz, s_, y_, yt_ = bzb[i], sg[i], yb[i], ybT[i]
        k.dma("sp", bz[:], I["bz"][qt * 128:(qt + 1) * 128, :], w=[bz])
        k.op("act", lambda e: e.activation(out=s_[:], in_=bz[:], func=AF.Exp, scale=-1.0), r=[bz], w=[s_])
        k.op("pool", lambda e: e.tensor_scalar(out=s_[:], in0=s_[:], scalar1=1.0, scalar2=None, op0=ALU.add), r=[s_], w=[s_])
        k.op("dve", lambda e: e.reciprocal(out=s_[:], in_=s_[:]), r=[s_], w=[s_])
        k.op("pool", lambda e: e.tensor_tensor(out=s_[:], in0=s_[:], in1=bz[:], op=ALU.mult), r=[s_, bz], w=[s_])
        k.op("pool", lambda e: e.tensor_tensor(out=y_[:], in0=s_[:], in1=oacc[i][:], op=ALU.mult), r=[s_, oacc[i]], w=[y_])

    def F2_tile(qt):
        k.tag = "F2"
        i = qt % 2
        y_, yt_ = yb[i], ybT[i]
        for h in range(4):
            k.op("pe", lambda e, h=h: e.transpose(PT[:, 512 + h * 128:512 + (h + 1) * 128], y_[:, h * 128:(h + 1) * 128], idb[:, :]),
                 r=[y_, idb], w=[PT])
        k.op("act", lambda e: e.copy(out=yt_[:], in_=PT[:, 512:1024]), r=[PT], w=[yt_])
        k.dma("sp", yT_d[512:1024, qt * 128:(qt + 1) * 128].rearrange("(h d) q -> d h q", d=128),
              yt_[:].rearrange("d (h q) -> d h q", h=4), r=[yt_], is_output=True)

    nqt = NQT if stage >= 99 else int(stage - 3)
    for it_ in range(nqt + 2):
        if it_ < nqt:
            Cqk_tile(it_)
        if 0 <= it_ - 1 < nqt:
            W_tile(it_ - 1)
            S_tile(it_ - 1, 0)
        if it_ < nqt:
            C_tile(it_)
        if 0 <= it_ - 1 < nqt:
            S_tile(it_ - 1, 1)
        if 0 <= it_ - 2 < nqt:
            F2_tile(it_ - 2)
        if it_ < nqt:
            C2_tile(it_)
        if 0 <= it_ - 1 < nqt:
            F_tile(it_ - 1)


B_INPUTS = [("cin", [4, 512, 4096], BF16), ("qT", [4, 128, 4096], BF16), ("kcT", [128, 4096], BF16), ("vcT", [128, 4096], BF16),
            ("ksT", [128, 4096], BF16), ("kwT", [128, 4096], BF16), ("vs", [4096, 128], BF16), ("vw", [4096, 128], BF16),
            ("gates", [4096, 12], BF16), ("bz", [4096, 512], BF16), ("conv_w", [512, 3], F32), ("cmp_pos", [2, 32, 128], F32),
            ("cmp_w1", [2, 4096, 128], F32), ("cmp_w2", [2, 128, 128], F32), ("table", [32, 4], F32), ("table31", [32, 4], F32)]


def build_B(stage=99):
    nc = bass.Bass("TRN2", target_bir_lowering=False)
    I = {}
    for n, shp, dt in B_INPUTS:
        if SIMDBG and n in ("cmp_w1", "cmp_w2"):
            dt = BF16
        I[n] = nc.dram_tensor(n, shp, dt, kind="ExternalInput").ap()
    for n, shp in CONST_SHAPES.items():
        I[n] = nc.dram_tensor(n, shp, (BF16 if (SIMDBG and n in ("jflip", "onehot", "esel", "ovx", "bw4")) else F32), kind="ExternalInput").ap()
    yT = nc.dram_tensor("yT", [1024, 4096], BF16, kind="ExternalOutput").ap()
    dbg = nc.dram_tensor("dbg", [128, 4096], F32, kind="ExternalOutput").ap()
    k = KB(nc)
    emit_B(k, nc, I, yT, stage, dbg)
    k.finish()
    return nc


def emit_conv_tok(k, hfm_own, hfm_par, halo_p, conv_w, mh, yS, mid=None):
    TP = 1024
    NT = 2048
    NB = 4
    cw = k.sbuf("cwt", [128, 8, 3], F32)
    k.dma("sp", cw[:], conv_w.rearrange("(c p) k -> p c k", p=128), w=[cw])
    mht = k.sbuf("mht", [128, 1], F32)
    k.dma("sp", mht[:], mh[:, :], w=[mht])
    cb_ = {n: [k.sbuf("ct_%s%d" % (n, i), [128, TP + 2], BF16) for i in range(NB)] for n in ("h", "b", "c", "z")}
    pbuf = [k.sbuf("ct_p%d" % i, [128, TP + 2], F32) for i in range(NB)]
    abuf = [k.sbuf("ct_a%d" % i, [128, TP], F32) for i in range(NB)]
    zbuf = [k.sbuf("ct_s%d" % i, [128, TP], F32) for i in range(NB)]
    ybuf = [k.sbuf("ct_y%d" % i, [128, TP], BF16) for i in range(NB)]
    it = 0
    order = [(c8, tp) for tp in (1, 0) for c8 in range(8)]
    for (c8, tp) in order:
        if it == 8 and mid is not None:
            mid()
        T = hfm_own if c8 < 4 else hfm_par
        cl = c8 % 4
        hc8 = (c8 + 4) % 8
        if True:
            i = it % NB
            it += 1
            t0 = tp * TP
            bh, bb, bc, bz = cb_["h"][i], cb_["b"][i], cb_["c"][i], cb_["z"][i]
            rj = [slice(j * 512 + cl * 128, j * 512 + (cl + 1) * 128) for j in range(4)]
            if tp == 0:
                k.dma("sp", bh[:, 0:2], halo_p[hc8 * 128:(hc8 + 1) * 128, 0:2], r=["halo_p"], w=[bh])
                k.dma("sp", bc[:, 0:2], halo_p[1024 + hc8 * 128:1024 + (hc8 + 1) * 128, 0:2], r=["halo_p"], w=[bc])
                k.dma("sp", bh[:, 2:], T[rj[0], 0:TP], w=[bh])
                k.dma("sp", bc[:, 2:], T[rj[2], 0:TP], w=[bc])
            else:
                k.dma("sp", bh[:, :], T[rj[0], t0 - 2:t0 + TP], w=[bh])
                k.dma("sp", bc[:, :], T[rj[2], t0 - 2:t0 + TP], w=[bc])
            k.dma("act", bb[:, 0:TP], T[rj[1], t0:t0 + TP], w=[bb])
            k.dma("act", bz[:, 0:TP], T[rj[3], t0:t0 + TP], w=[bz])
            p, a, sz, y = pbuf[i], abuf[i], zbuf[i], ybuf[i]
            k.op("dve", lambda e, p=p, bc=bc, bh=bh: e.tensor_tensor(out=p[:], in0=bc[:], in1=bh[:], op=ALU.mult), r=[bc, bh], w=[p])
            if tp == 0:
                k.op("dve", lambda e, p=p: e.tensor_scalar(out=p[:, 0:2], in0=p[:, 0:2], scalar1=mht[:, 0:1], scalar2=None, op0=ALU.mult),
                     r=[p, mht], w=[p])
            k.op("dve", lambda e, a=a, p=p, c8=c8: e.tensor_scalar(out=a[:], in0=p[:, 2:TP + 2], scalar1=cw[:, c8, 2:3], scalar2=None, op0=ALU.mult), r=[p, cw], w=[a])
            k.op("dve", lambda e, a=a, p=p, c8=c8: e.scalar_tensor_tensor(out=a[:], in0=p[:, 1:TP + 1], scalar=cw[:, c8, 1:2], in1=a[:], op0=ALU.mult, op1=ALU.add), r=[p, cw, a], w=[a])
            k.op("dve", lambda e, a=a, p=p, c8=c8: e.scalar_tensor_tensor(out=a[:], in0=p[:, 0:TP], scalar=cw[:, c8, 0:1], in1=a[:], op0=ALU.mult, op1=ALU.add), r=[p, cw, a], w=[a])
            k.op("act", lambda e, sz=sz, bz=bz: e.activation(out=sz[:], in_=bz[:, 0:TP], func=AF.Silu), r=[bz], w=[sz])
            k.op("pool", lambda e, a=a, bb=bb: e.tensor_tensor(out=a[:], in0=a[:], in1=bb[:, 0:TP], op=ALU.mult), r=[a, bb], w=[a])
            k.op("pool", lambda e, a=a, sz=sz, y=y: e.tensor_tensor(out=y[:], in0=a[:], in1=sz[:], op=ALU.mult), r=[a, sz], w=[y])
            k.dma("sp", yS[c8 * 128:(c8 + 1) * 128, t0:t0 + TP], y[:], r=[y], is_output=True)


from concourse.bass_utils import run_bass_kernel_spmd

I32 = mybir.dt.int32
PAIRS = [[0, 1], [2, 3], [4, 5], [6, 7]]


def build_fused(nlayers=4):
    nc = bass.Bass("TRN2", target_bir_lowering=False)

    def din(name, shape, dt=F32):
        return nc.dram_tensor(name, list(shape), dt, kind="ExternalInput").ap()

    def scr(name, shape, dt):
        return nc.dram_tensor(name, list(shape), dt).ap()

    x_in = din("x", [2048, 2048])
    sel = din("sel", [1, 8], I32)
    ident = din("ident", [128, 128])
    triu = din("triu", [128, 128])
    W = {}
    for i in range(2):
        W["ev_w_in%d" % i] = din("ev_w_in%d" % i, [2048, EVEN_IN])
        W["ev_w_out%d" % i] = din("ev_w_out%d" % i, [2048, 2048])
        W["conv_w%d" % i] = din("conv_w%d" % i, [1024, 3])
        W["cmp_pos%d" % i] = din("cmp_pos%d" % i, [2, 32, 128])
        W["cmp_w1%d" % i] = din("cmp_w1%d" % i, [2, 4096, 128])
        W["cmp_w2%d" % i] = din("cmp_w2%d" % i, [2, 128, 128])
        W["od_w_in%d" % i] = din("od_w_in%d" % i, [2048, 6144])
        W["od_w_out%d" % i] = din("od_w_out%d" % i, [2048, 2048])
        W["og%d" % i] = din("og%d" % i, [2048])
        W["ob%d" % i] = din("ob%d" % i, [2048])
        W["sgw_t%d" % i] = din("sgw_t%d" % i, [8, 128, 128])
        W["sgb%d" % i] = din("sgb%d" % i, [1024])
    for l in range(4):
        W["ln_g%d" % l] = din("ln_g%d" % l, [2048])
        W["ln_b%d" % l] = din("ln_b%d" % l, [2048])
    mh = din("mh", [128, 1])
    table = din("table", [32, 4])
    table31 = din("table31", [32, 4])
    CB = {n: din("c_" + n, shp) for n, shp in CONST_SHAPES.items() if n != "ident"}
    out = nc.dram_tensor("out", [2048, 2048], F32, kind="ExternalOutput").ap()

    hfm_own = scr("hfm_own", [GR, 2048], BF16)
    hfm_par = scr("hfm_par", [GR, 2048], BF16)
    htm_own = scr("htm_own", [2048, GC], BF16)
    htm_par = scr("htm_par", [2048, GC], BF16)
    hg_fm = scr("hg_fm", [2 * 1024, 2048], BF16)
    halo_loc = scr("halo_loc", [2048, 2], BF16)
    halo_g = scr("halo_g", [4096, 2], BF16)
    halo_p = scr("halo_p", [2048, 2], BF16)
    hg_tm = scr("hg_tm", [4096, GC], BF16)
    hB_fm = scr("hB_fm", [1024, 4096], BF16)
    hB_tm = scr("hB_tm", [4096, GC], BF16)
    yloc = scr("yloc", [1024, 4096], BF16)
    yg = scr("yg", [1024, 4096], BF16)
    yS = scr("yS", [2048, 2048], BF16)
    xa = scr("xa", [2048, 2048], F32)
    xb_ = scr("xb", [2048, 2048], F32)
    xc = scr("xc", [2048, 2048], F32)

    k = KB(nc)

    MULTS = ("own", "oth", "slot", 2048)

    def load_sel():
        if "gown" in k.vals:
            return
        selt = k.sbuf("selt", [1, 8], I32)
        k.dma("sp", selt[:], sel[0:1, 0:8], w=[selt])

        def ld(e):
            for j, m in enumerate(MULTS):
                reg = e.alloc_register("selreg%d_%s" % (k.phase_no, m))
                e.reg_load(reg, selt[0:1, j:j + 1])
                k.vals["g%s" % m] = e.snap(reg, min_val=0, max_val=(1 if m == "slot" else 2048))
            return e.nop()
        k.op("sp", ld, r=[selt], w=[])

    def G(m):
        return k.vals["g%s" % m]

    x_cur = x_in
    for layer in range(nlayers):
        i = layer // 2
        if layer % 2 == 0:
            emit_A(k, x_cur, W["ev_w_in%d" % i], ident, (hfm_own, hfm_par), (htm_own, htm_par), halo=halo_loc)
            k.end_phase()
            k.collective("AllGather", PAIRS, halo_loc[:, :], halo_g[:, :], w=["halo_g"], serialize=False)
            for c_ in range(2):
                k.collective("AllGather", PAIRS, hfm_par[2048 + c_ * 512:2048 + (c_ + 1) * 512, :], hg_fm[c_ * 1024:(c_ + 1) * 1024, :],
                             w=[("hg_fm", c_)], serialize=False)
            for c_ in range(2):
                k.collective("AllGather", PAIRS, htm_par[c_ * 1024:(c_ + 1) * 1024, :], hg_tm[c_ * 2048:(c_ + 1) * 2048, :],
                             w=[("hg_tm", c_)], serialize=False)
            load_sel()
            halo_g3 = halo_g.rearrange("(s r) t -> s r t", s=2)
            k.dma("sp", halo_p[:, :], (lambda: halo_g3[bass.ds(G("slot"), 1), :, :].rearrange("s r t -> (s r) t")),
                  r=["halo_g", ("hg_fm", 0), ("hg_fm", 1), ("hg_tm", 0), ("hg_tm", 1)], w=["halo_p"])
            hg_fm4 = hg_fm.rearrange("(c s r) t -> c s r t", c=2, s=2)
            hg_tm4 = hg_tm.rearrange("(c s r) n -> c s r n", c=2, s=2)
            k.dma("sp", (lambda: hB_fm[:, bass.ds(G("own"), 2048)]), hfm_own[2048:3072, :])
            k.dma("sp", (lambda: hB_tm[bass.ds(G("own"), 2048), :]), htm_own[:, :])

            def partner_relayout():
                allk = ["halo_g", ("hg_fm", 0), ("hg_fm", 1), ("hg_tm", 0), ("hg_tm", 1)]
                k.dma("sp", (lambda: hB_fm[:, bass.ds(G("oth"), 2048)].rearrange("(c r) t -> c r t", c=2)),
                      (lambda: hg_fm4[:, bass.ds(G("slot"), 1), :, :].rearrange("c s r t -> c (s r) t")), r=allk)
                k.dma("sp", (lambda: hB_tm[bass.ds(G("oth"), 2048), :].rearrange("(c r) n -> c r n", c=2)),
                      (lambda: hg_tm4[:, bass.ds(G("slot"), 1), :, :].rearrange("c s r n -> c (s r) n")), r=allk)
            emit_conv_tok(k, hfm_own, hfm_par, halo_p, W["conv_w%d" % i], mh, yS, mid=partner_relayout)
            k.end_phase()
            IB = dict(CB)
            IB["qT"] = hB_fm[0:512, :].rearrange("(h d) t -> h d t", h=4)
            IB["kcT"] = hB_fm[512:640, :]
            IB["vcT"] = hB_fm[640:768, :]
            IB["ksT"] = hB_fm[768:896, :]
            IB["kwT"] = hB_fm[896:1024, :]
            IB["vs"] = hB_tm[:, 0:128]
            IB["vw"] = hB_tm[:, 128:256]
            IB["gates"] = hB_tm[:, 256:268]
            IB["bz"] = hB_tm[:, 268:780]
            IB["ident"] = ident
            IB["table"] = table
            IB["table31"] = table31
            IB["cmp_pos"] = W["cmp_pos%d" % i]
            IB["cmp_w1"] = W["cmp_w1%d" % i]
            IB["cmp_w2"] = W["cmp_w2%d" % i]
            emit_B(k, nc, IB, yloc, lut_name="lutd%d" % i)
            k.end_phase()
            for c_ in range(2):
                k.collective("AllGather", PAIRS, yloc[512 + c_ * 256:512 + (c_ + 1) * 256, :], yg[c_ * 512:(c_ + 1) * 512, :], w=["yg"])
            load_sel()
            yg4 = yg.rearrange("(c s r) t -> c s r t", c=2, s=2)
            for s_ in range(2):
                k.dma("sp", yS[1024 + s_ * 512:1024 + (s_ + 1) * 512, :].rearrange("(c r) t -> c r t", c=2),
                      (lambda s_=s_: yg4[:, s_, :, bass.ds(G(2048), 2048)]), r=["yg"])
            k.end_phase()
            x_next = out if layer == nlayers - 1 else (xa if layer == 0 else xc)
            emit_C(k, yS, x_cur, W["ev_w_out%d" % i], W["ln_g%d" % layer], W["ln_b%d" % layer], x_next)
            if layer < nlayers - 1:
                k.end_phase()
            x_cur = x_next
        else:
            x_next = out if layer == nlayers - 1 else xb_
            emit_D(k, x_cur, W["od_w_in%d" % i], W["og%d" % i], W["ob%d" % i], W["sgw_t%d" % i], W["sgb%d" % i], triu,
                   W["od_w_out%d" % i], W["ln_g%d" % layer], W["ln_b%d" % layer], ident, x_next)
            if layer < nlayers - 1:
                k.end_phase()
            x_cur = x_next
    k.finish()
    return nc


_NC = {}
NLAYERS = 4


def kernel(x, rel_bias_table, ln_g, ln_b, ev_w_in, ev_conv_w, ev_cmp_pos, ev_cmp_w1, ev_cmp_w2, ev_w_out,
           od_w_in, od_ln_g, od_ln_b, od_sgu_w, od_sgu_b, od_w_out):
    f32 = np.float32
    A = lambda a: np.ascontiguousarray(np.asarray(a, dtype=f32))
    x = A(x).reshape(8, 2048, 2048)
    rel = A(rel_bias_table)
    consts = host_consts_B()
    common = {"ident": consts["ident"], "triu": np.triu(np.ones((128, 128), f32))}
    for n, v in consts.items():
        if n != "ident":
            common["c_" + n] = v
    gperm = np.arange(EVEN_IN)
    gidx = np.arange(24).reshape(3, 2, 4).transpose(1, 0, 2).reshape(-1)
    gperm[6656:6680] = 6656 + gidx
    operm = np.concatenate([np.arange(0, 512), np.arange(1024, 1536), np.arange(512, 1024), np.arange(1536, 2048)])
    convw = {}
    wout = {}
    win = {}
    swap = np.arange(EVEN_IN)
    def _sw(a0, b0, n):
        swap[a0:a0 + n] = np.arange(b0, b0 + n)
        swap[b0:b0 + n] = np.arange(a0, a0 + n)
    for j in range(4):
        _sw(j * 1024, j * 1024 + 512, 512)
    _sw(4096, 4608, 512)
    for base in (5120, 5376, 5632, 5888, 6144, 6400):
        _sw(base, base + 128, 128)
    _sw(6656, 6668, 12)
    _sw(6680, 7192, 512)
    for i in range(2):
        w_can = np.asarray(ev_w_in[i], dtype=f32)[:, gperm]
        win[i] = [A(w_can), A(w_can[:, swap])]
        wo_can = np.asarray(ev_w_out[i], dtype=f32)
        wout[i] = [A(wo_can[np.concatenate([np.arange(0, 512), np.arange(512, 1024), np.arange(1024, 2048)])]),
                   A(wo_can[np.concatenate([np.arange(512, 1024), np.arange(0, 512), np.arange(1024, 2048)])])]
        cw = np.asarray(ev_conv_w[i], dtype=f32)
        cwt = cw.T
        convw[i] = [A(cwt), A(np.concatenate([cwt[512:1024], cwt[0:512]], axis=0))]
        common["cmp_pos%d" % i] = A(ev_cmp_pos[i])
        common["cmp_w1%d" % i] = A(ev_cmp_w1[i])
        common["cmp_w2%d" % i] = A(ev_cmp_w2[i])
        common["od_w_in%d" % i] = A(od_w_in[i])
        common["od_w_out%d" % i] = A(od_w_out[i])
        common["og%d" % i] = A(od_ln_g[i])
        common["ob%d" % i] = A(od_ln_b[i])
        common["sgw_t%d" % i] = A(np.asarray(od_sgu_w[i], dtype=f32).transpose(0, 2, 1))
        common["sgb%d" % i] = A(np.asarray(od_sgu_b[i], dtype=f32).reshape(-1))
    for l in range(4):
        common["ln_g%d" % l] = A(ln_g[l])
        common["ln_b%d" % l] = A(ln_b[l])
    if "nc" not in _NC:
        _NC["nc"] = build_fused(NLAYERS)
    in_maps = []
    for c in range(8):
        d = dict(common)
        d["x"] = np.ascontiguousarray(x[c])
        g_ = c % 2
        d["sel"] = np.array([[g_ * 2048, (1 - g_) * 2048, 1 - g_, g_ * 2048, 0, 0, 0, 0]], dtype=np.int32)
        for i in range(2):
            d["conv_w%d" % i] = convw[i][g_]
            d["ev_w_out%d" % i] = wout[i][g_]
            d["ev_w_in%d" % i] = win[i][g_]
        d["mh"] = np.full((128, 1), float(g_), dtype=f32)
        d["table"] = A(rel[:, 4 * g_:4 * g_ + 4])
        d["table31"] = A(np.tile(rel[31:32, 4 * g_:4 * g_ + 4], (32, 1)))
        in_maps.append(d)
    res = run_bass_kernel_spmd(_NC["nc"], in_maps, core_ids=list(range(8)))
    return np.stack([np.asarray(res.results[c]["out"]) for c in range(8)]).reshape(4, 4096, 2048).astype(f32)
```

```python
import math
import numpy as np
from contextlib import ExitStack
import concourse.bass as bass
import concourse.mybir as mybir

F32 = mybir.dt.float32
BF16 = mybir.dt.bfloat16
I32 = mybir.dt.int32
U32 = mybir.dt.uint32
AF = mybir.ActivationFunctionType
ALU = mybir.AluOpType
AX = mybir.AxisListType

ENGS = ("pe", "dve", "act", "pool", "sp")
N_DMA_SEMS = 24


class Tile:
    def __init__(self, name, ap_handle, nsub=1):
        self.name = name
        self.t = ap_handle
        self.nsub = nsub

    def __getitem__(self, idx):
        return self.t[idx]

    def k(self, i):
        assert 0 <= i < self.nsub
        return (self.name, i)

    def all(self):
        return [(self.name, i) for i in range(self.nsub)]


def _keys(lst):
    out = []
    for x in lst:
        if isinstance(x, Tile):
            out.extend(x.all())
        elif isinstance(x, list):
            out.extend(_keys(x))
        else:
            out.append(x)
    return out


def _dma_dbg(e, out, in_, kw):
    o = out() if callable(out) else out
    i = in_() if callable(in_) else in_
    try:
        return e.dma_start(out=o, in_=i, **kw)
    except Exception:
        print("DMA FAILED out=", o, " in=", i)
        raise


class KB:
    def __init__(self, nc, same_engine_sync=True):
        self.nc = nc
        self.ses = same_engine_sync
        self.stack = ExitStack()
        self.semstack = ExitStack()
        self.csem = None
        self.base = {e: 0 for e in ENGS}
        self.cc_uses = 0
        self.cc_last = None
        self.vals = {}
        self.phase_no = 0
        self.tag = None
        self.ops = {e: [] for e in ENGS}
        self.res = {}
        self.seen = {e: {} for e in ENGS}
        self.dma_last = [None] * N_DMA_SEMS
        self.dma_uses = [0] * N_DMA_SEMS
        self.dma_rr = 0
        self.ntiles = 0
        self.out_tokens = []
        self.excl = set()

    def sbuf(self, name, shape, dtype, nsub=1):
        t = self.stack.enter_context(self.nc.sbuf_tensor("sb%d_" % self.phase_no + name, list(shape), dtype))
        return Tile(name, t, nsub)

    def psum(self, name, shape, dtype, nsub=1):
        t = self.stack.enter_context(self.nc.psum_tensor("ps%d_" % self.phase_no + name, list(shape), dtype))
        self.excl.add(name)
        return Tile(name, t, nsub)

    def _need(self, eng, tok, waits):
        if tok is None:
            return
        kind, src, idx = tok
        if kind == "c" and src == eng and not self.ses:
            return
        if kind == "c" and src == eng and eng == "pe":
            return
        key = (kind, src)
        if self.seen[eng].get(key, -1) >= idx:
            return
        self.seen[eng][key] = idx
        waits.append(tok)

    def _deps(self, eng, r, w, tok):
        waits = []
        for key in _keys(r):
            ent = self.res.setdefault(key, [None, {}])
            self._need(eng, ent[0], waits)
            if isinstance(key, tuple) and key[0] in self.excl:
                for t in ent[1].values():
                    if not (t[0] == "c" and t[1] == eng):
                        self._need(eng, t, waits)
        for key in _keys(w):
            ent = self.res.setdefault(key, [None, {}])
            self._need(eng, ent[0], waits)
            for t in ent[1].values():
                self._need(eng, t, waits)
        for key in _keys(r):
            ent = self.res[key]
            ent[1][(tok[0], tok[1])] = tok
        for key in _keys(w):
            ent = self.res[key]
            ent[0] = tok
            ent[1] = {}
        return waits

    def op(self, eng, fn, r=(), w=(), tag=None):
        idx = len(self.ops[eng])
        tok = ("c", eng, idx)
        waits = self._deps(eng, list(r), list(w), tok)
        self.ops[eng].append({"fn": fn, "waits": waits, "tok": tok, "dma": None, "tag": tag or self.tag})
        return tok

    def raw(self, eng, fn):
        self.ops[eng].append({"fn": fn, "waits": [], "tok": None, "dma": "noinc"})

    def dma(self, eng, out, in_, r=(), w=(), is_output=False, **kw):
        s = self.dma_rr
        self.dma_rr = (self.dma_rr + 1) % N_DMA_SEMS
        pre = []
        self._need(eng, self.dma_last[s], pre)
        self.dma_uses[s] += 1
        tok = ("d", s, self.dma_uses[s])
        self.dma_last[s] = tok
        waits = pre + self._deps(eng, list(r), list(w), tok)
        self.ops[eng].append(
            {"fn": (lambda e: _dma_dbg(e, out, in_, kw)), "waits": waits, "tok": tok, "dma": s}
        )
        if is_output:
            self.out_tokens.append(tok)
        return tok

    def collective(self, kind, groups, in_ap, out_ap, r=(), w=(), serialize=True):
        pre = []
        if serialize:
            self._need("pool", self.cc_last, pre)
        self.cc_uses += 1
        tok = ("x", 0, self.cc_uses)
        self.cc_last = tok
        waits = pre + self._deps("pool", list(r), list(w), tok)
        self.ops["pool"].append(
            {"fn": (lambda e: e.collective_compute(kind, ALU.bypass, replica_groups=groups, ins=[in_ap], outs=[out_ap])),
             "waits": waits, "tok": tok, "dma": "cc"})
        return tok

    def _alloc_sems(self):
        if self.csem is None:
            nc = self.nc
            self.csem = {e: self.semstack.enter_context(nc.semaphore("c_" + e)) for e in ENGS}
            self.dsem = [self.semstack.enter_context(nc.semaphore("d_%d" % i)) for i in range(N_DMA_SEMS)]
            self.ccsem = self.semstack.enter_context(nc.semaphore("ccs"))

    def end_phase(self):
        self.emit(final=False)
        self.stack = ExitStack()
        for e in ENGS:
            self.base[e] += self.n_incs[e]
        self.ops = {e: [] for e in ENGS}
        self.res = {}
        self.seen = {e: {} for e in ENGS}
        self.out_tokens = []
        self.excl = set()
        self.phase_no += 1

    def finish(self):
        self.emit(final=True)
        self.semstack.close()

    def emit(self, final=True):
        nc = self.nc
        self._alloc_sems()
        last = {}
        for e in ENGS:
            for o in reversed(self.ops[e]):
                if o["fn"] is not None and o["dma"] is None:
                    last[e] = o["tok"]
                    break
        for e in ENGS:
            fin = []
            for t in self.out_tokens:
                self._need(e, t, fin)
            for s in range(N_DMA_SEMS):
                self._need(e, self.dma_last[s], fin)
            self._need(e, self.cc_last, fin)
            for e2 in ENGS:
                if e2 != e and e2 in last:
                    self._need(e, last[e2], fin)
            self.ops[e].append({"fn": None, "waits": fin, "tok": None, "dma": None})

        waited = {e: set() for e in ENGS}
        for e in ENGS:
            for o in self.ops[e]:
                for (kind, src, idx) in o["waits"]:
                    if kind == "c":
                        waited[src].add(idx)
        rank = {}
        for e in ENGS:
            for i, idx in enumerate(sorted(waited[e])):
                rank[(e, idx)] = self.base[e] + i + 1
        self.n_incs = {e: len(waited[e]) for e in ENGS}
        csem, dsem, ccsem = self.csem, self.dsem, self.ccsem

        def run(ename, e):
            for o in self.ops[ename]:
                for (kind, src, idx) in o["waits"]:
                    if kind == "c":
                        wi = e.wait_ge(csem[src], rank[(src, idx)])
                    elif kind == "x":
                        wi = e.wait_ge(ccsem, idx)
                    else:
                        wi = e.wait_ge(dsem[src], 16 * idx)
                    if o.get("tag") and wi is not None:
                        try:
                            wi.annotate("wait[%s]<-%s" % (o["tag"], src if kind == "c" else kind))
                        except Exception:
                            pass
                if o["fn"] is None:
                    continue
                ins = o["fn"](e)
                if o.get("tag") and ins is not None:
                    try:
                        ins.annotate(o["tag"])
                    except Exception:
                        pass
                if o["dma"] == "cc":
                    ins.then_inc(ccsem, 1)
                elif o["dma"] == "noinc":
                    pass
                elif o["dma"] is not None:
                    ins.then_inc(dsem[o["dma"]], 16)
                else:
                    _, en, idx = o["tok"]
                    if (en, idx) in rank:
                        ins.then_inc(csem[en], 1)

        with nc.Block() as block:
            @block.tensor
            def _(e):
                run("pe", e)

            @block.vector
            def _(e):
                run("dve", e)

            @block.scalar
            def _(e):
                run("act", e)

            @block.gpsimd
            def _(e):
                run("pool", e)

            @block.sync
            def _(e):
                run("sp", e)
        self.stack.close()
        if final:
            pass


NTOK = 2048
D = 2048
EVEN_IN = 7704
SCALE = 128 ** -0.5

GR = 3072
FM_JOBS = []
for j in range(4):
    for g in range(2):
        FM_JOBS.append((j * 1024 + g * 512, 512, [g * GR + j * 512 + c * 128 for c in range(4)], 1.0))
for g in range(2):
    FM_JOBS.append((4096 + g * 512, 512, [g * GR + 2048 + c * 128 for c in range(4)], SCALE))
FM_JOBS.append((5120, 512, [2560, GR + 2560, 2688, GR + 2688], 1.0))
FM_JOBS.append((5632, 256, [2816, GR + 2816], 1.0))
FM_JOBS.append((6144, 256, [2944, GR + 2944], 1.0))
NFM = 6144
GC = 780
TM_JOBS = [(5888, 256, [(0, 128, 0), (128, 128, GC)]), (6400, 256, [(0, 128, 128), (128, 128, GC + 128)]),
           (6656, 24, [(0, 12, 256), (12, 12, GC + 256)]), (6680, 512, [(0, 512, 268)]), (7192, 512, [(0, 512, GC + 268)])]
NTM = 1560


def build_A():
    nc = bass.Bass("TRN2", target_bir_lowering=False)
    x = nc.dram_tensor("x", [NTOK, D], F32, kind="ExternalInput").ap()
    w = nc.dram_tensor("w", [D, EVEN_IN], F32, kind="ExternalInput").ap()
    ident = nc.dram_tensor("ident", [128, 128], F32, kind="ExternalInput").ap()
    hfm = nc.dram_tensor("hfm", [NFM, NTOK], BF16, kind="ExternalOutput").ap()
    htm = nc.dram_tensor("htm", [NTOK, NTM], BF16, kind="ExternalOutput").ap()
    k = KB(nc)
    emit_A(k, x, w, ident, hfm, htm)
    k.finish()
    return nc


def load_xT(k, x, idb, xT, ntt, pT, name="xb"):
    xb = [k.sbuf("%s%d" % (name, i), [128, D], BF16) for i in range(2)]
    n = 0
    for tt in range(ntt):
        b = xb[tt % 2]
        k.dma("pool", b[:], x[tt * 128:(tt + 1) * 128, :], w=[b])
        for g in range(4):
            p = pT[n % len(pT)]
            for j in range(4):
                kc = g * 4 + j
                k.op("pe", lambda e, p=p, b=b, kc=kc, j=j: e.transpose(
                    p[:, j * 128:(j + 1) * 128], b[:, kc * 128:(kc + 1) * 128], idb[:]),
                    r=[b, idb], w=[p])
            eng = "dve" if n % 2 == 0 else "act"
            src = p[:, 0:512].rearrange("p (a b) -> p a b", a=4)
            dst = xT[:, g * 4:(g + 1) * 4, tt * 128:(tt + 1) * 128]
            if eng == "dve":
                k.op("dve", lambda e, dst=dst, src=src: e.tensor_copy(out=dst, in_=src), r=[p], w=[xT.k(tt)])
            else:
                k.op("act", lambda e, dst=dst, src=src: e.copy(out=dst, in_=src), r=[p], w=[xT.k(tt)])
            n += 1


def load_ident(k, ident):
    idf = k.sbuf("idf", [128, 128], F32)
    idb = k.sbuf("idb", [128, 128], BF16)
    k.dma("sp", idf[:], ident[:, :], w=[idf])
    k.op("dve", lambda e: e.tensor_copy(out=idb[:], in_=idf[:]), r=[idf], w=[idb])
    return idf, idb


def emit_A(k, x, w, ident, hfm, htm, gather=None, halo=None):
    idf, idb = load_ident(k, ident)
    xT = k.sbuf("xT", [128, 16, NTOK], BF16, nsub=16)
    pT = [k.psum("pT%d" % i, [128, 1024], BF16) for i in range(2)]
    pm = [k.psum("pm%d" % i, [128, 512], F32) for i in range(6)]
    load_xT(k, x, idb, xT, 16, pT)
    wb = [k.sbuf("wb%d" % i, [128, 16, 512], BF16) for i in range(3)]
    stg = [k.sbuf("stg%d" % i, [128, NTOK], BF16) for i in range(3)]
    wv = w.rearrange("(c p) n -> p c n", p=128)
    nw = 0
    npm = 0
    nst = 0
    nev = 0
    jobs = [("tm",) + j for j in TM_JOBS] + [("fm",) + j for j in FM_JOBS]
    pending = []
    done_rows = {}

    def flush(jidx):
        while pending and pending[0][0] <= jidx:
            pending.pop(0)[1]()

    def cc_fm(c):
        hg_fm, hg_tm, pairs = gather
        k.collective("AllGather", pairs, hfm[c * 512:(c + 1) * 512, :], hg_fm[c * 1024:(c + 1) * 1024, :],
                     r=[("hfm", 4 * c + i) for i in range(4)], w=["hg_fm"])

    def cc_tm(c):
        hg_fm, hg_tm, pairs = gather
        k.collective("AllGather", pairs, htm[c * 512:(c + 1) * 512, :], hg_tm[c * 1024:(c + 1) * 1024, :],
                     r=[("htm", c)], w=["hg_tm"])

    for jidx, job in enumerate(jobs):
        kind = job[0]
        col0, ncols = job[1], job[2]
        wt = wb[nw % 3]
        nw += 1
        k.dma("pool", wt[:, :, 0:ncols], wv[:, :, col0:col0 + ncols], w=[wt])
        flush(jidx)
        if kind == "fm":
            rows0, scale = job[3], job[4]
            for ch in range(ncols // 128):
                st = stg[nst % 3]
                nst += 1
                for tg in range(4):
                    p = pm[npm % 6]
                    npm += 1
                    for kc in range(16):
                        k.op("pe", lambda e, p=p, wt=wt, kc=kc, ch=ch, tg=tg: e.matmul(
                            p[:, :], lhsT=wt[:, kc, ch * 128:(ch + 1) * 128],
                            rhs=xT[:, kc, tg * 512:(tg + 1) * 512], start=(kc == 0), stop=(kc == 15)),
                            r=[wt] + [xT.k(t) for t in range(tg * 4, tg * 4 + 4)], w=[p])
                    dst = st[:, tg * 512:(tg + 1) * 512]
                    if nev % 2 == 0:
                        k.op("act", lambda e, dst=dst, p=p, scale=scale: e.activation(
                            out=dst, in_=p[:, :], func=AF.Copy, scale=float(scale)), r=[p], w=[st])
                    else:
                        k.op("dve", lambda e, dst=dst, p=p, scale=scale: e.tensor_scalar(
                            out=dst, in0=p[:, :], scalar1=float(scale), scalar2=None, op0=ALU.mult), r=[p], w=[st])
                    nev += 1
                r0 = rows0[ch]
                if isinstance(hfm, tuple):
                    dst_t = hfm[r0 // GR]
                    rr = r0 % GR
                    k.dma("sp", dst_t[rr:rr + 128, :], st[:, :], r=[st], w=[("hfm", r0 // 128)], is_output=True)
                else:
                    k.dma("sp", hfm[r0:r0 + 128, :], st[:, :], r=[st], w=[("hfm", r0 // 128)], is_output=True)
                if halo is not None and col0 < 4096 and (col0 // 1024) in (0, 2):
                    c8 = (r0 // GR) * 4 + ch
                    hr = (0 if col0 // 1024 == 0 else 1024) + c8 * 128
                    k.dma("sp", halo[hr:hr + 128, 0:2], st[:, 2046:2048], r=[st], is_output=True)
                c512 = r0 // 512
                done_rows[c512] = done_rows.get(c512, 0) + 1
                if gather is not None and done_rows[c512] == 4:
                    pending.append((jidx + 2, (lambda c=c512: cc_fm(c))))
        else:
            pieces = job[3]
            for tg in range(4):
                st = stg[nst % 3]
                nst += 1
                for t4 in range(4):
                    tt = tg * 4 + t4
                    p = pm[npm % 6]
                    npm += 1
                    for kc in range(16):
                        k.op("pe", lambda e, p=p, wt=wt, kc=kc, tt=tt, ncols=ncols: e.matmul(
                            p[:, 0:ncols], lhsT=xT[:, kc, tt * 128:(tt + 1) * 128],
                            rhs=wt[:, kc, 0:ncols], start=(kc == 0), stop=(kc == 15)),
                            r=[wt, xT.k(tt)], w=[p])
                    dst = st[:, t4 * 512:t4 * 512 + ncols]
                    if nev % 2 == 0:
                        k.op("act", lambda e, dst=dst, p=p, ncols=ncols: e.copy(out=dst, in_=p[:, 0:ncols]), r=[p], w=[st])
                    else:
                        k.op("dve", lambda e, dst=dst, p=p, ncols=ncols: e.tensor_copy(out=dst, in_=p[:, 0:ncols]), r=[p], w=[st])
                    nev += 1
                for (so, n_, ocol) in pieces:
                    src = st[:, :].rearrange("p (a b) -> p a b", a=4)[:, :, so:so + n_]
                    if isinstance(htm, tuple):
                        dst = htm[ocol // GC][tg * 512:(tg + 1) * 512, ocol % GC:ocol % GC + n_].rearrange("(a p) n -> p a n", p=128)
                    else:
                        dst = htm[tg * 512:(tg + 1) * 512, ocol:ocol + n_].rearrange("(a p) n -> p a n", p=128)
                    k.dma("sp", dst, src, r=[st, ("htm", tg)], w=[("htm", tg)], is_output=True)
            if gather is not None and jidx == len(TM_JOBS) - 1:
                for c in range(4):
                    pending.append((jidx + 2, (lambda c=c: cc_tm(c))))
    flush(10 ** 9)


D = 2048
ALPHA = 8 ** 0.25
LN_EPS = 1e-5


def ln_rows(k, z, outt, gB, bB, width, tagres, scr, aff_eng="dve"):
    st, mv = scr["st"], scr["mv"]
    nch = width // 512
    for c in range(nch):
        k.op("dve", lambda e, c=c: e.bn_stats(out=st[:, c, :], in_=z[:, c * 512:(c + 1) * 512]), r=[tagres], w=[st])
    k.op("dve", lambda e: e.bn_aggr(out=mv[:, 0:2], in_=st[:, 0:nch, :]), r=[st], w=[mv])
    k.op("act", lambda e: e.activation(out=mv[:, 2:3], in_=mv[:, 1:2], func=AF.Sqrt, bias=LN_EPS), r=[mv], w=[mv])
    k.op("dve", lambda e: e.reciprocal(out=mv[:, 3:4], in_=mv[:, 2:3]), r=[mv], w=[mv])
    if outt is not None and not isinstance(outt, Tile) and not (outt is z):
        oap, ores = outt
        k.op("dve", lambda e: e.tensor_scalar(out=oap, in0=z[:, 0:width], scalar1=mv[:, 0:1], scalar2=mv[:, 3:4],
                                              op0=ALU.subtract, op1=ALU.mult), r=[mv, tagres], w=[ores])
        return
    k.op("dve", lambda e: e.tensor_scalar(out=z[:, 0:width], in0=z[:, 0:width], scalar1=mv[:, 0:1], scalar2=mv[:, 3:4],
                                          op0=ALU.subtract, op1=ALU.mult), r=[mv, tagres], w=[tagres])
    if gB is not None:
        k.op(aff_eng, lambda e: e.tensor_tensor(out=z[:, 0:width], in0=z[:, 0:width], in1=gB[:, 0:width], op=ALU.mult),
             r=[tagres, gB], w=[tagres])


def emit_C(k, yT_d, x, w_out, g, b, xout, ntok=2048):
    ntt = ntok // 128
    wo = k.sbuf("wo", [128, 16, D], BF16, nsub=4)
    wv = w_out.rearrange("(c p) n -> p c n", p=128)
    for cb in range(4):
        k.dma("pool", wo[:, :, cb * 512:(cb + 1) * 512], wv[:, :, cb * 512:(cb + 1) * 512], w=[wo.k(cb)])
    yT = k.sbuf("yT", [128, 16, ntok], BF16, nsub=16)
    for kc in range(16):
        k.dma("sp", yT[:, kc, :], yT_d[kc * 128:(kc + 1) * 128, :], w=[yT.k(kc)])
    gB = k.sbuf("gB", [128, D], F32)
    bB = k.sbuf("bB", [128, D], F32)
    k.dma("sp", gB[:], g.partition_broadcast(128), w=[gB])
    k.dma("sp", bB[:], b.partition_broadcast(128), w=[bB])
    pm = [k.psum("pmc%d" % i, [128, 512], F32) for i in range(8)]
    xs = [k.sbuf("xsc%d" % i, [128, D], F32) for i in range(2)]
    zs = [k.sbuf("zsc%d" % i, [128, D], F32) for i in range(2)]
    scr = [{"st": k.sbuf("stc%d" % i, [128, 4, 6], F32), "mv": k.sbuf("mvc%d" % i, [128, 4], F32)} for i in range(2)]
    npm = 0
    for tt in range(ntt):
        xt = xs[tt % 2]
        z = zs[tt % 2]
        k.dma("sp", xt[:], x[tt * 128:(tt + 1) * 128, :], w=[xt])
        for cb in range(4):
            p = pm[npm % 8]
            npm += 1
            for kc in range(16):
                k.op("pe", lambda e, p=p, kc=kc, tt=tt, cb=cb: e.matmul(
                    p[:, :], lhsT=yT[:, kc, tt * 128:(tt + 1) * 128], rhs=wo[:, kc, cb * 512:(cb + 1) * 512],
                    start=(kc == 0), stop=(kc == 15)), r=[yT.k(kc), wo.k(cb)], w=[p])
            k.op("dve", lambda e, p=p, z=z, xt=xt, cb=cb: e.scalar_tensor_tensor(
                out=z[:, cb * 512:(cb + 1) * 512], in0=xt[:, cb * 512:(cb + 1) * 512], scalar=float(ALPHA),
                in1=p[:, :], op0=ALU.mult, op1=ALU.add), r=[p, xt], w=[z])
        ln_rows(k, z, z, gB, bB, D, z, scr[tt % 2])
        k.op("dve", lambda e, z=z: e.tensor_tensor(out=z[:, :], in0=z[:, :], in1=bB[:, :], op=ALU.add), r=[z, bB], w=[z])
        k.dma("sp", xout[tt * 128:(tt + 1) * 128, :], z[:, :], r=[z], is_output=True)


def build_C():
    nc = bass.Bass("TRN2", target_bir_lowering=False)
    yT = nc.dram_tensor("yT", [D, 2048], BF16, kind="ExternalInput").ap()
    x = nc.dram_tensor("x", [2048, D], F32, kind="ExternalInput").ap()
    w = nc.dram_tensor("w", [D, D], F32, kind="ExternalInput").ap()
    g = nc.dram_tensor("g", [D], F32, kind="ExternalInput").ap()
    b = nc.dram_tensor("b", [D], F32, kind="ExternalInput").ap()
    xout = nc.dram_tensor("xout", [2048, D], F32, kind="ExternalOutput").ap()
    k = KB(nc)
    emit_C(k, yT, x, w, g, b, xout)
    k.finish()
    return nc


def emit_D(k, x, w_in, og, ob, sgw_t, sgb, triu, w_out, g, b, ident, xout, ntok=2048):
    G = 512
    ngrp = ntok // G
    idf, idb = load_ident(k, ident)
    pT = [k.psum("pT%d" % i, [128, 1024], BF16) for i in range(2)]
    pm = [k.psum("pm%d" % i, [128, 512], F32) for i in range(6)]
    gB = k.sbuf("gB", [128, D], F32)
    bB = k.sbuf("bB", [128, D], F32)
    k.dma("sp", gB[:], g.partition_broadcast(128), w=[gB])
    k.dma("sp", bB[:], b.partition_broadcast(128), w=[bB])
    ogc = k.sbuf("ogc", [128, 16], F32)
    obc = k.sbuf("obc", [128, 16], F32)
    k.dma("sp", ogc[:], og.rearrange("(f p) -> p f", p=128), w=[ogc], allow_slow_non_contiguous=True)
    k.dma("sp", obc[:], ob.rearrange("(f p) -> p f", p=128), w=[obc], allow_slow_non_contiguous=True)
    wcf = k.sbuf("wcf", [128, 8, 128], F32)
    tri = k.sbuf("tri", [128, 128], F32)
    wct = k.sbuf("wct", [128, 8, 128], BF16)
    ones = k.sbuf("ones", [128, 128], BF16)
    k.op("dve", lambda e: e.memset(ones[:], 1.0), w=[ones])
    k.dma("sp", wcf[:], sgw_t.rearrange("g s t -> s g t"), w=[wcf])
    k.dma("sp", tri[:], triu[:, :], w=[tri])
    for gq in range(8):
        k.op("dve", lambda e, gq=gq: e.tensor_tensor(out=wct[:, gq, :], in0=wcf[:, gq, :], in1=tri[:, :], op=ALU.mult),
             r=[wcf, tri], w=[wct])
    sb1 = k.sbuf("sb1", [128, 8, 128], F32)
    k.dma("sp", sb1[:].rearrange("p g t -> p (g t)"), sgb.partition_broadcast(128), w=[sb1])
    bias2 = k.sbuf("bias2", [128, 16, 128], F32)
    for half in range(2):
        p = pm[half]
        k.op("pe", lambda e, p=p, half=half: e.matmul(
            p[:, :], lhsT=ones[:, :], rhs=wct[:, half * 4:(half + 1) * 4, :].rearrange("p a b -> p (a b)"),
            start=True, stop=True), r=[ones, wct], w=[p])
        for f in range(half * 8, half * 8 + 8):
            gq = f // 2
            gl = gq - half * 4
            k.op("dve", lambda e, p=p, f=f, gq=gq, gl=gl: e.scalar_tensor_tensor(
                out=bias2[:, f, :], in0=p[:, gl * 128:(gl + 1) * 128], scalar=obc[:, f:f + 1], in1=sb1[:, gq, :],
                op0=ALU.mult, op1=ALU.add), r=[p, obc, sb1], w=[bias2])

    wb = [k.sbuf("wb%d" % i, [128, 16, 512], BF16) for i in range(3)]
    xT = k.sbuf("xT", [128, 16, G], BF16, nsub=4)
    zs = k.sbuf("zso", [128, 4, D], F32, nsub=4)
    vg = k.sbuf("vgo", [128, 4, D], BF16, nsub=4)
    vn = k.sbuf("vn", [128, 4, D], BF16, nsub=4)
    yT = k.sbuf("yTo", [128, 16, G], BF16, nsub=16)
    xs = [k.sbuf("xso%d" % i, [128, D], F32) for i in range(1)]
    gu = [k.sbuf("gu%d" % i, [128, 512], F32) for i in range(2)]
    sz = [k.sbuf("sz%d" % i, [128, 512], F32) for i in range(2)]
    mm = [k.sbuf("mm%d" % i, [128, 512], F32) for i in range(1)]
    scr = [{"st": k.sbuf("sto%d" % i, [128, 4, 6], F32), "mv": k.sbuf("mvo%d" % i, [128, 4], F32)} for i in range(2)]
    wv = w_in.rearrange("(c p) n -> p c n", p=128)
    wov = w_out.rearrange("(c p) n -> p c n", p=128)
    cnt = {"w": 0, "p": 0, "e": 0, "s": 0, "x": 0}

    def nextw():
        t = wb[cnt["w"] % 3]
        cnt["w"] += 1
        return t

    def nextp():
        t = pm[cnt["p"] % 6]
        cnt["p"] += 1
        return t

    deferred = []
    for grp in range(ngrp):
        t0 = grp * G
        if grp == 0:
            k.tag = "xT"
            _load_xT_grp(k, x, t0, idb, xT, pT)
        k.tag = "p1"
        for cb in range(4):
            wt = nextw()
            c0 = 2048 + cb * 512
            k.dma("pool", wt[:], wv[:, :, c0:c0 + 512], w=[wt])
            for t4 in range(4):
                p = nextp()
                for kc in range(16):
                    k.op("pe", lambda e, p=p, wt=wt, kc=kc, t4=t4: e.matmul(
                        p[:, :], lhsT=xT[:, kc, t4 * 128:(t4 + 1) * 128], rhs=wt[:, kc, :],
                        start=(kc == 0), stop=(kc == 15)), r=[wt, xT.k(t4)], w=[p])
                k.op("act", lambda e, p=p, t4=t4, cb=cb: e.activation(
                    out=vg[:, t4, cb * 512:(cb + 1) * 512], in_=p[:, :], func=AF.Gelu_apprx_tanh), r=[p], w=[vg.k(t4)])
        k.tag = "ln1"
        for t4 in range(4):
            s = scr[cnt["s"] % 2]
            cnt["s"] += 1
            ln_rows(k, vg[:, t4, :], (vn[:, t4, :], vn.k(t4)), None, None, D, vg.k(t4), s)
        k.tag = "p2"
        for cb8 in range(8):
            if cb8 % 2 == 1 and deferred:
                deferred.pop(0)()
                k.tag = "p2"
            wt = nextw()
            k.dma("pool", wt[:, :, 0:256], wv[:, :, cb8 * 256:(cb8 + 1) * 256], w=[wt])
            k.dma("pool", wt[:, :, 256:512], wv[:, :, 4096 + cb8 * 256:4096 + (cb8 + 1) * 256], w=[wt])
            for fc in range(2):
                f = cb8 * 2 + fc
                gq = f // 2
                pu = nextp()
                pz = nextp()
                px = nextp()
                def mm_u():
                    for kc in range(16):
                        k.op("pe", lambda e, p=pu, wt=wt, kc=kc, fc=fc: e.matmul(
                            p[:, :], lhsT=wt[:, kc, fc * 128:(fc + 1) * 128], rhs=xT[:, kc, :],
                            start=(kc == 0), stop=(kc == 15)), r=[wt, xT], w=[pu])

                def mm_z():
                    for kc in range(16):
                        k.op("pe", lambda e, p=pz, wt=wt, kc=kc, fc=fc: e.matmul(
                            p[:, :], lhsT=wt[:, kc, 256 + fc * 128:256 + (fc + 1) * 128], rhs=xT[:, kc, :],
                            start=(kc == 0), stop=(kc == 15)), r=[wt, xT], w=[pz])
                if f % 2 == 0:
                    mm_u()
                    mm_z()
                else:
                    mm_z()
                    mm_u()
                for t4 in range(4):
                    k.op("pe", lambda e, p=px, t4=t4, f=f, gq=gq: e.matmul(
                        p[:, t4 * 128:(t4 + 1) * 128], lhsT=vn[:, t4, f * 128:(f + 1) * 128], rhs=wct[:, gq, :],
                        start=True, stop=True), r=[vn.k(t4), wct], w=[px])
                i = cnt["e"] % 2
                cnt["e"] += 1
                g_, s_, m_ = gu[i], sz[i], mm[0]
                def act_u():
                    k.op("act", lambda e, g_=g_, pu=pu: e.activation(out=g_[:], in_=pu[:, :], func=AF.Gelu_apprx_tanh), r=[pu], w=[g_])

                def act_z():
                    k.op("act", lambda e, s_=s_, pz=pz: e.activation(out=s_[:], in_=pz[:, :], func=AF.Silu), r=[pz], w=[s_])
                if f % 2 == 0:
                    act_u()
                    act_z()
                else:
                    act_z()
                    act_u()
                for t4 in range(4):
                    k.op("dve", lambda e, m_=m_, px=px, f=f, t4=t4: e.scalar_tensor_tensor(
                        out=m_[:, t4 * 128:(t4 + 1) * 128], in0=px[:, t4 * 128:(t4 + 1) * 128], scalar=ogc[:, f:f + 1],
                        in1=bias2[:, f, :], op0=ALU.mult, op1=ALU.add), r=[px, ogc, bias2], w=[m_])
                k.op("dve", lambda e, m_=m_, g_=g_: e.tensor_tensor(out=m_[:], in0=m_[:], in1=g_[:], op=ALU.mult),
                     r=[m_, g_], w=[m_])
                k.op("dve", lambda e, m_=m_, s_=s_, f=f: e.tensor_tensor(out=yT[:, f, :], in0=m_[:], in1=s_[:], op=ALU.mult),
                     r=[m_, s_], w=[yT.k(f)])
        if grp + 1 < ngrp:
            k.tag = "xT"
            _load_xT_grp(k, x, t0 + G, idb, xT, pT)
        k.tag = "p3"
        for cb in range(4):
            wt = nextw()
            k.dma("pool", wt[:], wov[:, :, cb * 512:(cb + 1) * 512], w=[wt])
            for t4 in range(4):
                p = nextp()
                for kc in range(16):
                    k.op("pe", lambda e, p=p, wt=wt, kc=kc, t4=t4: e.matmul(
                        p[:, :], lhsT=yT[:, kc, t4 * 128:(t4 + 1) * 128], rhs=wt[:, kc, :],
                        start=(kc == 0), stop=(kc == 15)), r=[wt, yT.k(kc)], w=[p])
                k.op("act", lambda e, p=p, t4=t4, cb=cb: e.copy(out=zs[:, t4, cb * 512:(cb + 1) * 512], in_=p[:, :]),
                     r=[p], w=[zs.k(t4)])
        def ln3(t4, t0=t0):
            k.tag = "ln3"
            if True:
                xt = xs[0]
                cnt["x"] += 1
                k.dma("sp", xt[:], x[t0 + t4 * 128:t0 + (t4 + 1) * 128, :], w=[xt])
                k.op("dve", lambda e, t4=t4, xt=xt: e.scalar_tensor_tensor(
                    out=zs[:, t4, :], in0=xt[:, :], scalar=float(ALPHA), in1=zs[:, t4, :], op0=ALU.mult, op1=ALU.add),
                    r=[xt, zs.k(t4)], w=[zs.k(t4)])
                s = scr[cnt["s"] % 2]
                cnt["s"] += 1
                ln_rows(k, zs[:, t4, :], None, gB, bB, D, zs.k(t4), s)
                k.op("dve", lambda e, t4=t4: e.tensor_tensor(out=zs[:, t4, :], in0=zs[:, t4, :], in1=bB[:, :], op=ALU.add),
                     r=[zs.k(t4), bB], w=[zs.k(t4)])
                k.dma("sp", xout[t0 + t4 * 128:t0 + (t4 + 1) * 128, :], zs[:, t4, :], r=[zs.k(t4)], is_output=True)
        for t4_ in range(4):
            deferred.append(lambda t4_=t4_, ln3=ln3: ln3(t4_))
    while deferred:
        deferred.pop(0)()


_xb_tiles = {}


def _load_xT_grp(k, x, t0, idb, xT, pT):
    if (id(k), k.phase_no) not in _xb_tiles:
        _xb_tiles[(id(k), k.phase_no)] = [k.sbuf("xbo%d" % i, [128, D], BF16) for i in range(2)]
    xb = _xb_tiles[(id(k), k.phase_no)]
    n = 0
    for t4 in range(4):
        b = xb[t4 % 2]
        k.dma("pool", b[:], x[t0 + t4 * 128:t0 + (t4 + 1) * 128, :], w=[b])
        for g in range(4):
            p = pT[n % len(pT)]
            for j in range(4):
                kc = g * 4 + j
                k.op("pe", lambda e, p=p, b=b, kc=kc, j=j: e.transpose(
                    p[:, j * 128:(j + 1) * 128], b[:, kc * 128:(kc + 1) * 128], idb[:]), r=[b, idb], w=[p])
            src = p[:, 0:512].rearrange("p (a b) -> p a b", a=4)
            dst = xT[:, g * 4:(g + 1) * 4, t4 * 128:(t4 + 1) * 128]
            if n % 2 == 0:
                k.op("dve", lambda e, dst=dst, src=src: e.tensor_copy(out=dst, in_=src), r=[p], w=[xT.k(t4)])
            else:
                k.op("act", lambda e, dst=dst, src=src: e.copy(out=dst, in_=src), r=[p], w=[xT.k(t4)])
            n += 1


def build_D(ntok=2048):
    nc = bass.Bass("TRN2", target_bir_lowering=False)
    x = nc.dram_tensor("x", [ntok, D], F32, kind="ExternalInput").ap()
    w_in = nc.dram_tensor("w_in", [D, 3 * D], F32, kind="ExternalInput").ap()
    og = nc.dram_tensor("og", [D], F32, kind="ExternalInput").ap()
    ob = nc.dram_tensor("ob", [D], F32, kind="ExternalInput").ap()
    sgw_t = nc.dram_tensor("sgw_t", [8, 128, 128], F32, kind="ExternalInput").ap()
    sgb = nc.dram_tensor("sgb", [1024], F32, kind="ExternalInput").ap()
    triu = nc.dram_tensor("triu", [128, 128], F32, kind="ExternalInput").ap()
    w_out = nc.dram_tensor("w_out", [D, D], F32, kind="ExternalInput").ap()
    g = nc.dram_tensor("g", [D], F32, kind="ExternalInput").ap()
    b = nc.dram_tensor("b", [D], F32, kind="ExternalInput").ap()
    ident = nc.dram_tensor("ident", [128, 128], F32, kind="ExternalInput").ap()
    xout = nc.dram_tensor("xout", [ntok, D], F32, kind="ExternalOutput").ap()
    k = KB(nc)
    emit_D(k, x, w_in, og, ob, sgw_t, sgb, triu, w_out, g, b, ident, xout, ntok)
    k.finish()
    return nc

import math

S_LEN = 4096
NQT = 32
NEG = -30000.0
DMIN = -2063
NL = 4608
SIMDBG = False
SUB = 99
DEPTH = 2


def t5_bucket_np(d):
    n = np.maximum(d, 0)
    large = 16 + (np.log(np.maximum(n, 1).astype(np.float32) / 16) / math.log(128 / 16) * 16).astype(np.int32)
    large = np.minimum(large, 31)
    return np.where(n < 16, n, large)


def host_consts_B():
    c = {}
    c["ident"] = np.eye(128, dtype=np.float32)
    c["jflip"] = np.ascontiguousarray(np.eye(128, dtype=np.float32)[::-1])
    d = DMIN + np.arange(NL)
    oh = np.zeros((33, NL), np.float32)
    bk = t5_bucket_np(d)
    oh[bk[d >= 0], np.nonzero(d >= 0)[0]] = 1.0
    oh[32, d < 0] = 1.0
    c["onehot"] = oh
    es = np.zeros((64, 32, 128), np.float32)
    for cc in range(32):
        es[2 * cc, cc, 0:64] = 1.0
        es[2 * cc + 1, cc, 64:128] = 1.0
    c["esel"] = es.reshape(64, 4096)
    n = np.arange(256)[:, None]
    j = np.arange(64)[None, :]
    ov = ((16 * n < 64 * j + 64) & (16 * n + 32 > 64 * j)).astype(np.float32)
    ovx = np.concatenate([ov, np.ones((256, 1), np.float32)], axis=1)
    ovx[255, :] = 0.0
    c["ovx"] = ovx
    keep = np.zeros((32, 128, 64), np.float32)
    fix = np.zeros((32, 128, 64), np.float32)
    for qt in range(32):
        t = qt * 128 + np.arange(128)
        cur = (t // 64)[:, None]
        blk = np.arange(64)[None, :]
        forced = (blk == 0) | (blk == cur) | (blk == cur - 1)
        fut = blk > cur
        keep[qt] = (~forced & ~fut).astype(np.float32)
        f = np.zeros((128, 64), np.float32)
        f = np.where(blk == cur - 1, 1e9, f)
        f = np.where(blk == cur, 2e9, f)
        f = np.where(blk == 0, 3e9, f)
        f = np.where(fut, -1.0 - 0.001 * blk, f)
        fix[qt] = f
    c["keep"] = keep
    c["fix"] = fix
    qi = np.arange(128)[None, :]
    ki = np.arange(128)[:, None]
    bw4 = np.where(qi < ki, 0.0, NEG).astype(np.float32)
    c["bw4"] = np.tile(bw4, (1, 4))
    return c


CONST_SHAPES = {"ident": [128, 128], "jflip": [128, 128], "onehot": [33, NL], "esel": [64, 4096], "ovx": [256, 65],
                "keep": [32, 128, 64], "fix": [32, 128, 64], "bw4": [128, 512]}


def emit_B(k, nc, I, yT_d, stage=99, dbg=None, lut_name="lutd"):
    idf, idb = load_ident(k, I["ident"])
    S = [k.psum("S%d" % i, [128, 512], F32) for i in range(2)]
    OA = [k.psum("OA%d" % i, [128, 512], F32) for i in range(2)]
    OS = [k.psum("OS%d" % i, [128, 512], F32) for i in range(2)]
    PT = k.psum("PT", [128, 1024], BF16)
    PX = k.psum("PX", [128, 512], F32)

    cin = I.get("cin")
    cw = k.sbuf("cw", [128, 4, 3], F32)
    if cin is not None:
        k.dma("sp", cw[:], I["conv_w"].rearrange("(c p) k -> p c k", p=128), w=[cw])
    TP = 512
    cb_ = {n: [k.sbuf("cv_%s%d" % (n, i), [128, TP + 2], BF16) for i in range(2)] for n in ("h", "b", "c", "z")}
    pbuf = [k.sbuf("cv_p%d" % i, [128, TP + 2], F32) for i in range(2)]
    abuf = [k.sbuf("cv_a%d" % i, [128, TP], F32) for i in range(2)]
    zbuf = [k.sbuf("cv_s%d" % i, [128, TP], F32) for i in range(2)]
    ybuf = [k.sbuf("cv_y%d" % i, [128, TP], BF16) for i in range(2)]
    cin = I.get("cin")
    conv_jobs = []
    it = 0
    for cc in range(4):
        for tp in range(S_LEN // TP):
            conv_jobs.append((cc, tp, it % 2))
            it += 1
    ebuf = [k.sbuf("cv_e%d" % i, [128, TP], F32) for i in range(2)]

    def conv_piece(cc, tp, i):
        k.tag = "conv"
        t0 = tp * TP
        bh, bb, bc, bz = cb_["h"][i], cb_["b"][i], cb_["c"][i], cb_["z"][i]
        rows = slice(cc * 128, (cc + 1) * 128)
        if tp == 0:
            k.op("pool", lambda e, bh=bh: e.memset(bh[:, 0:2], 0.0), w=[bh])
            k.op("pool", lambda e, bc=bc: e.memset(bc[:, 0:2], 0.0), w=[bc])
            k.dma("sp", bh[:, 2:], cin[0, rows, 0:TP], w=[bh])
            k.dma("sp", bc[:, 2:], cin[2, rows, 0:TP], w=[bc])
        else:
            k.dma("sp", bh[:, :], cin[0, rows, t0 - 2:t0 + TP], w=[bh])
            k.dma("sp", bc[:, :], cin[2, rows, t0 - 2:t0 + TP], w=[bc])
        k.dma("sp", bb[:, 0:TP], cin[1, rows, t0:t0 + TP], w=[bb])
        k.dma("sp", bz[:, 0:TP], cin[3, rows, t0:t0 + TP], w=[bz])
        p, a, sz, y, ex = pbuf[i], abuf[i], zbuf[i], ybuf[i], ebuf[i]
        k.op("dve", lambda e: e.tensor_tensor(out=p[:], in0=bc[:], in1=bh[:], op=ALU.mult), r=[bc, bh], w=[p])
        k.op("dve", lambda e: e.tensor_scalar(out=a[:], in0=p[:, 2:TP + 2], scalar1=cw[:, cc, 2:3], scalar2=None, op0=ALU.mult), r=[p, cw], w=[a])
        k.op("dve", lambda e: e.scalar_tensor_tensor(out=a[:], in0=p[:, 1:TP + 1], scalar=cw[:, cc, 1:2], in1=a[:], op0=ALU.mult, op1=ALU.add), r=[p, cw, a], w=[a])
        k.op("dve", lambda e: e.scalar_tensor_tensor(out=a[:], in0=p[:, 0:TP], scalar=cw[:, cc, 0:1], in1=a[:], op0=ALU.mult, op1=ALU.add), r=[p, cw, a], w=[a])
        k.op("act", lambda e: e.activation(out=sz[:], in_=bz[:, 0:TP], func=AF.Silu), r=[bz], w=[sz])
        k.op("pool", lambda e: e.tensor_tensor(out=a[:], in0=a[:], in1=bb[:, 0:TP], op=ALU.mult), r=[a, bb], w=[a])
        k.op("pool", lambda e: e.tensor_tensor(out=y[:], in0=a[:], in1=sz[:], op=ALU.mult), r=[a, sz], w=[y])
        k.dma("sp", yT_d[rows, t0:t0 + TP], y[:], r=[y], is_output=True)

    while conv_jobs and cin is not None:
        conv_piece(*conv_jobs.pop(0))
    if stage <= 1:
        return
    qT4 = k.sbuf("qT4", [128, NQT, 4, 128], BF16)
    for h in range(4):
        k.dma("sp", qT4[:, :, h, :], I["qT"][h].rearrange("d (t q) -> d t q", q=128), w=[qT4])
    kin = {}
    for n in ("kc", "vc", "ks", "kw"):
        kin[n] = k.sbuf("in_" + n, [128, S_LEN], BF16)
        k.dma("sp", kin[n][:], I[n + "T"][:, :], w=[kin[n]])
    vs_ext = k.sbuf("vs_ext", [128, 32, 129], BF16)
    vw_ext = k.sbuf("vw_ext", [128, 32, 129], BF16)
    for t, n in ((vs_ext, "vs"), (vw_ext, "vw")):
        k.dma("sp", t[:, :, 0:128], I[n].rearrange("(c p) d -> p c d", p=128), w=[t])
        k.op("pool", lambda e, t=t: e.memset(t[:, :, 128:129], 1.0), w=[t])
    esel = k.sbuf("esel", [64, 4096], BF16)
    k.dma(("sp" if SIMDBG else "pool"), esel[:], I["esel"][:, :], w=[esel])
    jfl = k.sbuf("jfl", [128, 128], BF16)
    k.dma(("sp" if SIMDBG else "pool"), jfl[:], I["jflip"][:, :], w=[jfl])
    bw4 = k.sbuf("bw4", [128, 512], BF16)
    k.dma(("sp" if SIMDBG else "pool"), bw4[:], I["bw4"][:, :], w=[bw4])
    graw = k.sbuf("graw", [128, NQT, 12], BF16)
    k.dma("sp", graw[:], I["gates"].rearrange("(t p) c -> p t c", p=128), w=[graw])
    gs = k.sbuf("gs", [128, NQT, 12], F32)
    k.op("act", lambda e: e.activation(out=gs[:], in_=graw[:], func=AF.Exp, scale=-1.0), r=[graw], w=[gs])
    k.op("dve", lambda e: e.tensor_scalar(out=gs[:], in0=gs[:], scalar1=1.0, scalar2=None, op0=ALU.add), r=[gs], w=[gs])
    k.op("dve", lambda e: e.reciprocal(out=gs[:], in_=gs[:]), r=[gs], w=[gs])

    if stage <= 1.2:
        return
    tab = k.sbuf("tab", [33, 4], F32)
    t31 = k.sbuf("t31", [32, 4], F32)
    tabb = k.sbuf("tabb", [33, 4], BF16)
    k.dma("sp", tab[0:32, :], I["table"][:, :], w=[tab])
    k.dma("sp", t31[:], I["table31"][:, :], w=[t31])
    k.op("dve", lambda e: e.memset(tabb[:], NEG), w=[tabb])
    k.op("dve", lambda e: e.tensor_tensor(out=tabb[0:32, :], in0=tab[0:32, :], in1=t31[:], op=ALU.subtract), r=[tab, t31, tabb], w=[tabb])
    oh = k.sbuf("oh", [33, NL], BF16)
    k.dma(("sp" if SIMDBG else "pool"), oh[:], I["onehot"][:, :], w=[oh])
    lutd = nc.dram_tensor(lut_name, [4, NL], F32).ap()
    lst = [k.sbuf("lst%d" % i, [4, 512], F32) for i in range(2)]
    for i in range(NL // 512):
        k.op("pe", lambda e, i=i: e.matmul(PX[0:4, :], lhsT=tabb[:, :], rhs=oh[:, i * 512:(i + 1) * 512], start=True, stop=True),
             r=[tabb, oh], w=[PX])
        st_ = lst[i % 2]
        k.op("dve", lambda e, st_=st_: e.tensor_copy(out=st_[:, :], in_=PX[0:4, :]), r=[PX], w=[st_])
        k.dma("sp", lutd[:, i * 512:(i + 1) * 512], st_[:, :], r=[st_], w=["lutd"])
    if stage <= 1.5:
        return
    BC = k.sbuf("BC", [128, 17, 512], BF16)
    BD = k.sbuf("BD", [128, 2, 512], BF16)
    hkf = [k.sbuf("hkf%d" % i, [128, 4, 128], F32) for i in range(2)]
    hk = [k.sbuf("hk%d" % i, [128, 4, 128], BF16) for i in range(2)]
    specs = []
    for dl in range(17):
        specs.append((BC[:, dl, :], 128 * dl - 31 - 2032, 16))
    specs.append((BD[:, 0, :], 0 - 127, 1))
    specs.append((BD[:, 1, :], 128 - 127, 1))
    for i, (dst, c0, pstr) in enumerate(specs):
        hh = hk[i % 2]
        src = bass.AP(lutd.tensor, c0 - DMIN, [[pstr, 128], [NL, 4], [1, 128]])
        hf = hkf[i % 2]
        k.dma("sp", hf[:], src, r=["lutd"], w=[hf])
        k.op("pool", lambda e, hh=hh, hf=hf: e.tensor_copy(out=hh[:], in_=hf[:]), r=[hf], w=[hh])
        k.op("pe", lambda e, hh=hh: e.matmul(PX[:, :], lhsT=jfl[:, :], rhs=hh[:].rearrange("p a b -> p (a b)"), start=True, stop=True),
             r=[jfl, hh], w=[PX])
        dres = BC if i < 17 else BD
        k.op("dve", lambda e, dst=dst: e.tensor_copy(out=dst, in_=PX[:, :]), r=[PX], w=[dres])

    if stage <= 2:
        dt = k.sbuf("dbgt", [128, 2048], F32)
        k.op("dve", lambda e: e.tensor_copy(out=dt[:, 0:512], in_=BD[:, 0, :]), r=[BD], w=[dt])
        k.op("dve", lambda e: e.tensor_copy(out=dt[:, 512:1024], in_=BD[:, 1, :]), r=[BD], w=[dt])
        k.op("dve", lambda e: e.tensor_copy(out=dt[:, 1024:1536], in_=BC[:, 0, :]), r=[BC], w=[dt])
        k.op("dve", lambda e: e.tensor_copy(out=dt[:, 1536:2048], in_=BC[:, 16, :]), r=[BC], w=[dt])
        k.dma("sp", dbg[:, 0:2048], dt[:], r=[dt], is_output=True)
        return
    vc_ext = k.sbuf("vc_ext", [128, 2, 193], BF16)
    k.dma(("sp" if SIMDBG else "pool"), vc_ext[:, :, 128:193], I["ovx"].rearrange("(c p) j -> p c j", p=128), w=[vc_ext])
    kcT = k.sbuf("kcT", [128, 256], BF16)
    w1 = k.sbuf("w1", [128, 32, 128], BF16)
    w2 = k.sbuf("w2", [128, 128], BF16)
    posf = k.sbuf("posf", [128, 32], F32)
    posb = k.sbuf("posb", [128, 32], BF16)
    pb = k.sbuf("pb", [128, 1], F32)
    hidT = k.sbuf("hidT", [128, 256], BF16)
    for kv, src in ((0, kin["kc"]), (1, kin["vc"])):
        k.dma(("sp" if SIMDBG else "pool"), w1[:], I["cmp_w1"][kv].rearrange("(l d) h -> d l h", d=128), w=[w1])
        k.dma(("sp" if SIMDBG else "pool"), w2[:], I["cmp_w2"][kv], w=[w2])
        k.dma("sp", posf[:], I["cmp_pos"][kv].rearrange("l d -> d l"), w=[posf], allow_slow_non_contiguous=True)
        k.op("dve", lambda e: e.tensor_copy(out=posb[:], in_=posf[:]), r=[posf], w=[posb])
        for l in range(32):
            k.op("pe", lambda e, l=l: e.matmul(PX[:, 0:1], lhsT=w1[:, l, :], rhs=posb[:, l:l + 1], start=(l == 0), stop=(l == 31)),
                 r=[w1, posb], w=[PX])
        k.op("dve", lambda e: e.tensor_copy(out=pb[:], in_=PX[:, 0:1]), r=[PX], w=[pb])
        for l in range(32):
            k.op("pe", lambda e, l=l, src=src: e.matmul(S[0][:, 0:255], lhsT=w1[:, l, :], rhs=src[:, l:l + 16 * 254 + 1:16],
                                                        start=(l == 0), stop=(l == 31)), r=[w1, src], w=[S[0]])
        k.op("dve", lambda e: e.memset(hidT[:], 0.0), w=[hidT])
        k.op("act", lambda e: e.activation(out=hidT[:, 0:255], in_=S[0][:, 0:255], func=AF.Silu, bias=pb[:, 0:1]), r=[S[0], pb], w=[hidT])
        if kv == 0:
            k.op("pe", lambda e: e.matmul(S[1][:, 0:256], lhsT=w2[:, :], rhs=hidT[:, :], start=True, stop=True), r=[w2, hidT], w=[S[1]])
            k.op("dve", lambda e: e.tensor_copy(out=kcT[:], in_=S[1][:, 0:256]), r=[S[1]], w=[kcT])
        else:
            for c in range(2):
                k.op("pe", lambda e, c=c: e.matmul(S[1][:, c * 128:(c + 1) * 128], lhsT=hidT[:, c * 128:(c + 1) * 128], rhs=w2[:, :],
                                                   start=True, stop=True), r=[w2, hidT], w=[S[1]])
            k.op("dve", lambda e: e.tensor_copy(out=vc_ext[:, :, 0:128], in_=S[1][:, 0:256].rearrange("p (c d) -> p c d", c=2)),
                 r=[S[1]], w=[vc_ext])

    if stage <= 3:
        dt = k.sbuf("dbgt", [128, 2048], F32)
        k.op("dve", lambda e: e.tensor_copy(out=dt[:, 0:256], in_=kcT[:, :]), r=[kcT], w=[dt])
        k.op("dve", lambda e: e.tensor_copy(out=dt[:, 256:256 + 386], in_=vc_ext[:, :, :].rearrange("p a b -> p (a b)")), r=[vc_ext], w=[dt])
        k.dma("sp", dbg[:, 0:1024], dt[:, 0:1024], r=[dt], is_output=True)
        return
    Eb = [k.sbuf("Eb%d" % i, [128, 512], BF16) for i in range(6)]
    oacc = [k.sbuf("oacc%d" % i, [128, 512], F32) for i in range(2)]
    mbT4 = [k.sbuf("mbT%d" % i, [64, 512], BF16) for i in range(2)]
    keepb = [k.sbuf("keep%d" % i, [128, 64], F32) for i in range(2)]
    fixb = [k.sbuf("fix%d" % i, [128, 64], F32) for i in range(2)]
    sm = [k.sbuf("sm%d" % i, [128, 64], F32) for i in range(2)]
    imp = [k.sbuf("imp%d" % i, [128, 64], F32) for i in range(2)]
    wk2 = [k.sbuf("wk2%d" % i, [128, 64], F32) for i in range(2)]
    mx = [k.sbuf("mx%d" % i, [128, 24], F32) for i in range(2)]
    mb = [k.sbuf("mb%d" % i, [128, 64], BF16) for i in range(2)]
    bzb = [k.sbuf("bzb%d" % i, [128, 512], BF16) for i in range(2)]
    sg = [k.sbuf("sg%d" % i, [128, 512], F32) for i in range(2)]
    yb = [k.sbuf("yb%d" % i, [128, 512], BF16) for i in range(2)]
    ybT = [k.sbuf("ybT%d" % i, [128, 512], BF16) for i in range(2)]
    cnt = {"S": 0, "E": 0}

    S3 = [S[0], S[1], PX]

    def nextS():
        t = S3[cnt["S"] % 3]
        cnt["S"] += 1
        return t

    def nextE():
        t = Eb[cnt["E"] % 6]
        cnt["E"] += 1
        return t

    EbC = [k.sbuf("EbC%d" % i, [128, 512], BF16) for i in range(4)]

    def score_tile(lhsT_k, qt, extra, eb=None):
        s = nextS()
        n = len(extra)
        k.op("pe", lambda e: e.matmul(s[:, :], lhsT=lhsT_k[0], rhs=qT4[:, qt, :, :].rearrange("p h q -> p (h q)"),
                                      start=True, stop=(n == 0)), r=[lhsT_k[1], qT4], w=[s], tag=(k.tag or "") + ".qk")
        for i, (l, r_, rd) in enumerate(extra):
            k.op("pe", lambda e, l=l, r_=r_, i=i: e.matmul(s[:, :], lhsT=l, rhs=r_, start=False, stop=(i == n - 1)), r=rd, w=[s], tag=(k.tag or "") + ".x%d" % i)
        if eb is None:
            eb = nextE()
        k.op("act", lambda e: e.activation(out=eb[:], in_=s[:, :], func=AF.Exp), r=[s], w=[eb])
        return eb

    def pv(acc, width, eb, rhs_ap, rd, first=False):
        for h in range(4):
            a = acc[h // 2]
            o0 = (h % 2) * width
            st = bool(first and h % 2 == 0)
            k.op("pe", lambda e, a=a, o0=o0, h=h, st=st: e.matmul(a[:, o0:o0 + rhs_ap.shape[-1]], lhsT=eb[:, h * 128:(h + 1) * 128], rhs=rhs_ap,
                                                                   start=st, stop=True, skip_group_check=True), r=[eb] + rd, w=[a], tag=(k.tag or "") + ".pv%d" % h)

    def finish(acc, width, dcol, qt, branch, first, smt):
        for i in range(2):
            a = acc[i]
            dv = a[:, 0:2 * width].rearrange("p (h c) -> p h c", h=2)[:, :, dcol:dcol + 1]
            k.op("dve", lambda e, dv=dv, i=i: e.tensor_scalar(out=smt[:, 2 * i:2 * i + 2].rearrange("p (h c) -> p h c", c=1), in0=dv,
                                                              scalar1=1e-30, scalar2=None, op0=ALU.max), r=[a], w=[smt])
        k.op("dve", lambda e: e.reciprocal(out=smt[:, 4:8], in_=smt[:, 0:4]), r=[smt], w=[smt])
        k.op("dve", lambda e: e.tensor_tensor(out=smt[:, 8:12], in0=smt[:, 4:8], in1=gs[:, qt, branch * 4:branch * 4 + 4], op=ALU.mult),
             r=[smt, gs], w=[smt])
        oa = oacc[qt % 2]
        for h in range(4):
            a = acc[h // 2]
            o0 = (h % 2) * width
            if first:
                k.op("act", lambda e, a=a, o0=o0, h=h: e.activation(out=oa[:, h * 128:(h + 1) * 128], in_=a[:, o0:o0 + 128], func=AF.Copy,
                                                                     scale=smt[:, 8 + h:9 + h]), r=[a, smt], w=[oa])
            else:
                k.op("dve", lambda e, a=a, o0=o0, h=h: e.scalar_tensor_tensor(out=oa[:, h * 128:(h + 1) * 128], in0=a[:, o0:o0 + 128],
                                                                             scalar=smt[:, 8 + h:9 + h], in1=oa[:, h * 128:(h + 1) * 128],
                                                                             op0=ALU.mult, op1=ALU.add), r=[a, smt, oa], w=[oa])

    cstate = {}

    def Cqk_tile(qt):
        k.tag = "C"
        i = qt % 2
        k.dma("sp", keepb[i][:], I["keep"][qt], w=[keepb[i]])
        k.dma("sp", fixb[i][:], I["fix"][qt], w=[fixb[i]])
        lst = []
        for c in range(1 if qt < 16 else 2):
            dl = qt - 16 * c
            extra = []
            if dl <= 16:
                extra.append((idb[:, :], BC[:, dl, :], [idb, BC]))
            eb = score_tile((kcT[:, c * 128:(c + 1) * 128], kcT), qt, extra, eb=EbC[(qt % 2) * 2 + c])
            lst.append((OA, 193, eb, vc_ext[:, c, :], [vc_ext], c == 0))
        cstate[qt] = lst

    def C_tile(qt):
        k.tag = "C"
        i = qt % 2
        for args in cstate.pop(qt):
            pv(*args)
        if SUB <= 1:
            return
        smt = sm[i]
        finish(OA, 193, 192, qt, 0, True, smt)
        if SUB <= 2:
            return
        im = imp[i]
        for h in range(4):
            if SUB < 2.5 and h >= round((SUB - 2) * 10):
                return
            a = OA[h // 2]
            o0 = (h % 2) * 193 + 128
            if h == 0:
                k.op("dve", lambda e, a=a, o0=o0: e.tensor_scalar(out=im[:], in0=a[:, o0:o0 + 64], scalar1=smt[:, 4:5], scalar2=None, op0=ALU.mult),
                     r=[a, smt], w=[im])
            else:
                k.op("dve", lambda e, a=a, o0=o0, h=h: e.scalar_tensor_tensor(out=im[:], in0=a[:, o0:o0 + 64], scalar=smt[:, 4 + h:5 + h], in1=im[:],
                                                                             op0=ALU.mult, op1=ALU.add), r=[a, smt, im], w=[im])
        if SUB <= 2.5:
            return
        k.op("dve", lambda e: e.tensor_tensor(out=im[:], in0=im[:], in1=keepb[i][:], op=ALU.mult), r=[im, keepb[i]], w=[im])
        k.op("dve", lambda e: e.tensor_tensor(out=im[:], in0=im[:], in1=fixb[i][:], op=ALU.add), r=[im, fixb[i]], w=[im])
        if SUB <= 3:
            return
        m_, w2_ = mx[i], wk2[i]
        k.op("dve", lambda e: e.max(out=m_[:, 0:8], in_=im[:]), r=[im], w=[m_])
        k.op("dve", lambda e: e.match_replace(out=w2_[:], in_to_replace=m_[:, 0:8], in_values=im[:], imm_value=-1e30), r=[im, m_], w=[w2_])
        k.op("dve", lambda e: e.max(out=m_[:, 8:16], in_=w2_[:]), r=[w2_], w=[m_])
        if SUB <= 4:
            return
        k.op("dve", lambda e: e.tensor_reduce(out=m_[:, 16:17], in_=m_[:, 8:16], axis=AX.X, op=ALU.min), r=[m_], w=[m_])
        k.op("dve", lambda e: e.tensor_scalar(out=mb[i][:], in0=im[:], scalar1=m_[:, 16:17], scalar2=NEG, op0=ALU.is_lt, op1=ALU.mult),
             r=[im, m_], w=[mb[i]])
    def C2_tile(qt):
        k.tag = "C2"
        i = qt % 2
        k.op("pe", lambda e: e.transpose(PT[0:64, 0:128], mb[i][:, :], idb[:, :]), r=[mb[i], idb], w=[PT])
        for h in range(4):
            k.op("act", lambda e, h=h: e.copy(out=mbT4[i][:, h * 128:(h + 1) * 128], in_=PT[0:64, 0:128]), r=[PT], w=[mbT4[i]])

    sstate = {}

    def S_tile(qt, part):
        k.tag = "S"
        i = qt % 2
        if part == 0:
            sstate[qt] = {"pend": [], "started": False, "c": 0}
        st_ = sstate[qt]
        c_end = min(max(2, (qt + 1) // 2), qt + 1) if part == 0 else qt + 1
        for c in range(st_["c"], c_end):
            extra = [(esel[:, c * 128:(c + 1) * 128], mbT4[i][:, :], [esel, mbT4[i]])]
            if c == qt:
                extra.append((idb[:, :], BD[:, 0, :], [idb, BD]))
            elif c == qt - 1:
                extra.append((idb[:, :], BD[:, 1, :], [idb, BD]))
            eb = score_tile((kin["ks"][:, c * 128:(c + 1) * 128], kin["ks"]), qt, extra)
            st_["pend"].append((OS, 129, eb, vs_ext[:, c, :], [vs_ext], not st_["started"]))
            st_["started"] = True
            if len(st_["pend"]) > DEPTH:
                pv(*st_["pend"].pop(0))
        st_["c"] = c_end
        if part == 1:
            while st_["pend"]:
                pv(*st_["pend"].pop(0))
            finish(OS, 129, 128, qt, 1, False, sm[i])
            del sstate[qt]

    def W_tile(qt):
        k.tag = "W"
        i = qt % 2
        pend = []
        started = False
        for c in range(max(0, qt - 4), qt + 1):
            jj = qt - c
            extra = []
            if jj == 0:
                extra.append((idb[:, :], BD[:, 0, :], [idb, BD]))
            elif jj == 1:
                extra.append((idb[:, :], BD[:, 1, :], [idb, BD]))
            elif jj == 4:
                extra.append((idb[:, :], bw4[:, :], [idb, bw4]))
            eb = score_tile((kin["kw"][:, c * 128:(c + 1) * 128], kin["kw"]), qt, extra)
            pend.append((OA, 193, eb, vw_ext[:, c, :], [vw_ext], len(pend) == 0 and not started))
            started = True
            if len(pend) > DEPTH:
                pv(*pend.pop(0))
        while pend:
            pv(*pend.pop(0))
        finish(OA, 193, 128, qt, 2, False, sm[i])

    def F_tile(qt):
        k.tag = "F"
        i = qt % 2
        bz, s_, y_, yt_ = bzb[i], sg[i], yb[i], ybT[i]
        k.dma("sp", bz[:], I["bz"][qt * 128:(qt + 1) * 128, :], w=[bz])
        k.op("act", lambda e: e.activation(out=s_[:], in_=bz[:], func=AF.Exp, scale=-1.0), r=[bz], w=[s_])
        k.op("pool", lambda e: e.tensor_scalar(out=s_[:], in0=s_[:], scalar1=1.0, scalar2=None, op0=ALU.add), r=[s_], w=[s_])
        k.op("dve", lambda e: e.reciprocal(out=s_[:], in_=s_[:]), r=[s_], w=[s_])
        k.op("pool", lambda e: e.tensor_tensor(out=s_[:], in0=s_[:], in1=bz[:], op=ALU.mult), r=[s_, bz], w=[s_])
        k.op("pool", lambda e: e.tensor_tensor(out=y_[:], in0=s_[:], in1=oacc[i][:], op=ALU.mult), r=[s_, oacc[i]], w=[y_])

    def F2_tile(qt):
        k.tag = "F2"
        i = qt % 2
        y_, yt_ = yb[i], ybT[i]
        for h in range(4):
            k.op("pe", lambda e, h=h: e.transpose(PT[:, 512 + h * 128:512 + (h + 1) * 128], y_[:, h * 128:(h + 1) * 128], idb[:, :]),
                 r=[y_, idb], w=[PT])
        k.op("act", lambda e: e.copy(out=yt_[:], in_=PT[:, 512:1024]), r=[PT], w=[yt_])
        k.dma("sp", yT_d[512:1024, qt * 128:(qt + 1) * 128].rearrange("(h d) q -> d h q", d=128),
              yt_[:].rearrange("d (h q) -> d h q", h=4), r=[yt_], is_output=True)

    nqt = NQT if stage >= 99 else int(stage - 3)
    for it_ in range(nqt + 2):
        if it_ < nqt:
            Cqk_tile(it_)
        if 0 <= it_ - 1 < nqt:
            W_tile(it_ - 1)
            S_tile(it_ - 1, 0)
        if it_ < nqt:
            C_tile(it_)
        if 0 <= it_ - 1 < nqt:
            S_tile(it_ - 1, 1)
        if 0 <= it_ - 2 < nqt:
            F2_tile(it_ - 2)
        if it_ < nqt:
            C2_tile(it_)
        if 0 <= it_ - 1 < nqt:
            F_tile(it_ - 1)


B_INPUTS = [("cin", [4, 512, 4096], BF16), ("qT", [4, 128, 4096], BF16), ("kcT", [128, 4096], BF16), ("vcT", [128, 4096], BF16),
            ("ksT", [128, 4096], BF16), ("kwT", [128, 4096], BF16), ("vs", [4096, 128], BF16), ("vw", [4096, 128], BF16),
            ("gates", [4096, 12], BF16), ("bz", [4096, 512], BF16), ("conv_w", [512, 3], F32), ("cmp_pos", [2, 32, 128], F32),
            ("cmp_w1", [2, 4096, 128], F32), ("cmp_w2", [2, 128, 128], F32), ("table", [32, 4], F32), ("table31", [32, 4], F32)]


def build_B(stage=99):
    nc = bass.Bass("TRN2", target_bir_lowering=False)
    I = {}
    for n, shp, dt in B_INPUTS:
        if SIMDBG and n in ("cmp_w1", "cmp_w2"):
            dt = BF16
        I[n] = nc.dram_tensor(n, shp, dt, kind="ExternalInput").ap()
    for n, shp in CONST_SHAPES.items():
        I[n] = nc.dram_tensor(n, shp, (BF16 if (SIMDBG and n in ("jflip", "onehot", "esel", "ovx", "bw4")) else F32), kind="ExternalInput").ap()
    yT = nc.dram_tensor("yT", [1024, 4096], BF16, kind="ExternalOutput").ap()
    dbg = nc.dram_tensor("dbg", [128, 4096], F32, kind="ExternalOutput").ap()
    k = KB(nc)
    emit_B(k, nc, I, yT, stage, dbg)
    k.finish()
    return nc


def emit_conv_tok(k, hfm_own, hfm_par, halo_p, conv_w, mh, yS, mid=None):
    TP = 1024
    NT = 2048
    NB = 4
    cw = k.sbuf("cwt", [128, 8, 3], F32)
    k.dma("sp", cw[:], conv_w.rearrange("(c p) k -> p c k", p=128), w=[cw])
    mht = k.sbuf("mht", [128, 1], F32)
    k.dma("sp", mht[:], mh[:, :], w=[mht])
    cb_ = {n: [k.sbuf("ct_%s%d" % (n, i), [128, TP + 2], BF16) for i in range(NB)] for n in ("h", "b", "c", "z")}
    pbuf = [k.sbuf("ct_p%d" % i, [128, TP + 2], F32) for i in range(NB)]
    abuf = [k.sbuf("ct_a%d" % i, [128, TP], F32) for i in range(NB)]
    zbuf = [k.sbuf("ct_s%d" % i, [128, TP], F32) for i in range(NB)]
    ybuf = [k.sbuf("ct_y%d" % i, [128, TP], BF16) for i in range(NB)]
    it = 0
    order = [(c8, tp) for tp in (1, 0) for c8 in range(8)]
    for (c8, tp) in order:
        if it == 8 and mid is not None:
            mid()
        T = hfm_own if c8 < 4 else hfm_par
        cl = c8 % 4
        hc8 = (c8 + 4) % 8
        if True:
            i = it % NB
            it += 1
            t0 = tp * TP
            bh, bb, bc, bz = cb_["h"][i], cb_["b"][i], cb_["c"][i], cb_["z"][i]
            rj = [slice(j * 512 + cl * 128, j * 512 + (cl + 1) * 128) for j in range(4)]
            if tp == 0:
                k.dma("sp", bh[:, 0:2], halo_p[hc8 * 128:(hc8 + 1) * 128, 0:2], r=["halo_p"], w=[bh])
                k.dma("sp", bc[:, 0:2], halo_p[1024 + hc8 * 128:1024 + (hc8 + 1) * 128, 0:2], r=["halo_p"], w=[bc])
                k.dma("sp", bh[:, 2:], T[rj[0], 0:TP], w=[bh])
                k.dma("sp", bc[:, 2:], T[rj[2], 0:TP], w=[bc])
            else:
                k.dma("sp", bh[:, :], T[rj[0], t0 - 2:t0 + TP], w=[bh])
                k.dma("sp", bc[:, :], T[rj[2], t0 - 2:t0 + TP], w=[bc])
            k.dma("act", bb[:, 0:TP], T[rj[1], t0:t0 + TP], w=[bb])
            k.dma("act", bz[:, 0:TP], T[rj[3], t0:t0 + TP], w=[bz])
            p, a, sz, y = pbuf[i], abuf[i], zbuf[i], ybuf[i]
            k.op("dve", lambda e, p=p, bc=bc, bh=bh: e.tensor_tensor(out=p[:], in0=bc[:], in1=bh[:], op=ALU.mult), r=[bc, bh], w=[p])
            if tp == 0:
                k.op("dve", lambda e, p=p: e.tensor_scalar(out=p[:, 0:2], in0=p[:, 0:2], scalar1=mht[:, 0:1], scalar2=None, op0=ALU.mult),
                     r=[p, mht], w=[p])
            k.op("dve", lambda e, a=a, p=p, c8=c8: e.tensor_scalar(out=a[:], in0=p[:, 2:TP + 2], scalar1=cw[:, c8, 2:3], scalar2=None, op0=ALU.mult), r=[p, cw], w=[a])
            k.op("dve", lambda e, a=a, p=p, c8=c8: e.scalar_tensor_tensor(out=a[:], in0=p[:, 1:TP + 1], scalar=cw[:, c8, 1:2], in1=a[:], op0=ALU.mult, op1=ALU.add), r=[p, cw, a], w=[a])
            k.op("dve", lambda e, a=a, p=p, c8=c8: e.scalar_tensor_tensor(out=a[:], in0=p[:, 0:TP], scalar=cw[:, c8, 0:1], in1=a[:], op0=ALU.mult, op1=ALU.add), r=[p, cw, a], w=[a])
            k.op("act", lambda e, sz=sz, bz=bz: e.activation(out=sz[:], in_=bz[:, 0:TP], func=AF.Silu), r=[bz], w=[sz])
            k.op("pool", lambda e, a=a, bb=bb: e.tensor_tensor(out=a[:], in0=a[:], in1=bb[:, 0:TP], op=ALU.mult), r=[a, bb], w=[a])
            k.op("pool", lambda e, a=a, sz=sz, y=y: e.tensor_tensor(out=y[:], in0=a[:], in1=sz[:], op=ALU.mult), r=[a, sz], w=[y])
            k.dma("sp", yS[c8 * 128:(c8 + 1) * 128, t0:t0 + TP], y[:], r=[y], is_output=True)


from concourse.bass_utils import run_bass_kernel_spmd

I32 = mybir.dt.int32
PAIRS = [[0, 1], [2, 3], [4, 5], [6, 7]]


def build_fused(nlayers=4):
    nc = bass.Bass("TRN2", target_bir_lowering=False)

    def din(name, shape, dt=F32):
        return nc.dram_tensor(name, list(shape), dt, kind="ExternalInput").ap()

    def scr(name, shape, dt):
        return nc.dram_tensor(name, list(shape), dt).ap()

    x_in = din("x", [2048, 2048])
    sel = din("sel", [1, 8], I32)
    ident = din("ident", [128, 128])
    triu = din("triu", [128, 128])
    W = {}
    for i in range(2):
        W["ev_w_in%d" % i] = din("ev_w_in%d" % i, [2048, EVEN_IN])
        W["ev_w_out%d" % i] = din("ev_w_out%d" % i, [2048, 2048])
        W["conv_w%d" % i] = din("conv_w%d" % i, [1024, 3])
        W["cmp_pos%d" % i] = din("cmp_pos%d" % i, [2, 32, 128])
        W["cmp_w1%d" % i] = din("cmp_w1%d" % i, [2, 4096, 128])
        W["cmp_w2%d" % i] = din("cmp_w2%d" % i, [2, 128, 128])
        W["od_w_in%d" % i] = din("od_w_in%d" % i, [2048, 6144])
        W["od_w_out%d" % i] = din("od_w_out%d" % i, [2048, 2048])
        W["og%d" % i] = din("og%d" % i, [2048])
        W["ob%d" % i] = din("ob%d" % i, [2048])
        W["sgw_t%d" % i] = din("sgw_t%d" % i, [8, 128, 128])
        W["sgb%d" % i] = din("sgb%d" % i, [1024])
    for l in range(4):
        W["ln_g%d" % l] = din("ln_g%d" % l, [2048])
        W["ln_b%d" % l] = din("ln_b%d" % l, [2048])
    mh = din("mh", [128, 1])
    table = din("table", [32, 4])
    table31 = din("table31", [32, 4])
    CB = {n: din("c_" + n, shp) for n, shp in CONST_SHAPES.items() if n != "ident"}
    out = nc.dram_tensor("out", [2048, 2048], F32, kind="ExternalOutput").ap()

    hfm_own = scr("hfm_own", [GR, 2048], BF16)
    hfm_par = scr("hfm_par", [GR, 2048], BF16)
    htm_own = scr("htm_own", [2048, GC], BF16)
    htm_par = scr("htm_par", [2048, GC], BF16)
    hg_fm = scr("hg_fm", [2 * 1024, 2048], BF16)
    halo_loc = scr("halo_loc", [2048, 2], BF16)
    halo_g = scr("halo_g", [4096, 2], BF16)
    halo_p = scr("halo_p", [2048, 2], BF16)
    hg_tm = scr("hg_tm", [4096, GC], BF16)
    hB_fm = scr("hB_fm", [1024, 4096], BF16)
    hB_tm = scr("hB_tm", [4096, GC], BF16)
    yloc = scr("yloc", [1024, 4096], BF16)
    yg = scr("yg", [1024, 4096], BF16)
    yS = scr("yS", [2048, 2048], BF16)
    xa = scr("xa", [2048, 2048], F32)
    xb_ = scr("xb", [2048, 2048], F32)
    xc = scr("xc", [2048, 2048], F32)

    k = KB(nc)

    MULTS = ("own", "oth", "slot", 2048)

    def load_sel():
        if "gown" in k.vals:
            return
        selt = k.sbuf("selt", [1, 8], I32)
        k.dma("sp", selt[:], sel[0:1, 0:8], w=[selt])

        def ld(e):
            for j, m in enumerate(MULTS):
                reg = e.alloc_register("selreg%d_%s" % (k.phase_no, m))
                e.reg_load(reg, selt[0:1, j:j + 1])
                k.vals["g%s" % m] = e.snap(reg, min_val=0, max_val=(1 if m == "slot" else 2048))
            return e.nop()
        k.op("sp", ld, r=[selt], w=[])

    def G(m):
        return k.vals["g%s" % m]

    x_cur = x_in
    for layer in range(nlayers):
        i = layer // 2
        if layer % 2 == 0:
            emit_A(k, x_cur, W["ev_w_in%d" % i], ident, (hfm_own, hfm_par), (htm_own, htm_par), halo=halo_loc)
            k.end_phase()
            k.collective("AllGather", PAIRS, halo_loc[:, :], halo_g[:, :], w=["halo_g"], serialize=False)
            for c_ in range(2):
                k.collective("AllGather", PAIRS, hfm_par[2048 + c_ * 512:2048 + (c_ + 1) * 512, :], hg_fm[c_ * 1024:(c_ + 1) * 1024, :],
                             w=[("hg_fm", c_)], serialize=False)
            for c_ in range(2):
                k.collective("AllGather", PAIRS, htm_par[c_ * 1024:(c_ + 1) * 1024, :], hg_tm[c_ * 2048:(c_ + 1) * 2048, :],
                             w=[("hg_tm", c_)], serialize=False)
            load_sel()
            halo_g3 = halo_g.rearrange("(s r) t -> s r t", s=2)
            k.dma("sp", halo_p[:, :], (lambda: halo_g3[bass.ds(G("slot"), 1), :, :].rearrange("s r t -> (s r) t")),
                  r=["halo_g", ("hg_fm", 0), ("hg_fm", 1), ("hg_tm", 0), ("hg_tm", 1)], w=["halo_p"])
            hg_fm4 = hg_fm.rearrange("(c s r) t -> c s r t", c=2, s=2)
            hg_tm4 = hg_tm.rearrange("(c s r) n -> c s r n", c=2, s=2)
            k.dma("sp", (lambda: hB_fm[:, bass.ds(G("own"), 2048)]), hfm_own[2048:3072, :])
            k.dma("sp", (lambda: hB_tm[bass.ds(G("own"), 2048), :]), htm_own[:, :])

            def partner_relayout():
                allk = ["halo_g", ("hg_fm", 0), ("hg_fm", 1), ("hg_tm", 0), ("hg_tm", 1)]
                k.dma("sp", (lambda: hB_fm[:, bass.ds(G("oth"), 2048)].rearrange("(c r) t -> c r t", c=2)),
                      (lambda: hg_fm4[:, bass.ds(G("slot"), 1), :, :].rearrange("c s r t -> c (s r) t")), r=allk)
                k.dma("sp", (lambda: hB_tm[bass.ds(G("oth"), 2048), :].rearrange("(c r) n -> c r n", c=2)),
                      (lambda: hg_tm4[:, bass.ds(G("slot"), 1), :, :].rearrange("c s r n -> c (s r) n")), r=allk)
            emit_conv_tok(k, hfm_own, hfm_par, halo_p, W["conv_w%d" % i], mh, yS, mid=partner_relayout)
            k.end_phase()
            IB = dict(CB)
            IB["qT"] = hB_fm[0:512, :].rearrange("(h d) t -> h d t", h=4)
            IB["kcT"] = hB_fm[512:640, :]
            IB["vcT"] = hB_fm[640:768, :]
            IB["ksT"] = hB_fm[768:896, :]
            IB["kwT"] = hB_fm[896:1024, :]
            IB["vs"] = hB_tm[:, 0:128]
            IB["vw"] = hB_tm[:, 128:256]
            IB["gates"] = hB_tm[:, 256:268]
            IB["bz"] = hB_tm[:, 268:780]
            IB["ident"] = ident
            IB["table"] = table
            IB["table31"] = table31
            IB["cmp_pos"] = W["cmp_pos%d" % i]
            IB["cmp_w1"] = W["cmp_w1%d" % i]
            IB["cmp_w2"] = W["cmp_w2%d" % i]
            emit_B(k, nc, IB, yloc, lut_name="lutd%d" % i)
            k.end_phase()
            for c_ in range(2):
                k.collective("AllGather", PAIRS, yloc[512 + c_ * 256:512 + (c_ + 1) * 256, :], yg[c_ * 512:(c_ + 1) * 512, :], w=["yg"])
            load_sel()
            yg4 = yg.rearrange("(c s r) t -> c s r t", c=2, s=2)
            for s_ in range(2):
                k.dma("sp", yS[1024 + s_ * 512:1024 + (s_ + 1) * 512, :].rearrange("(c r) t -> c r t", c=2),
                      (lambda s_=s_: yg4[:, s_, :, bass.ds(G(2048), 2048)]), r=["yg"])
            k.end_phase()
            x_next = out if layer == nlayers - 1 else (xa if layer == 0 else xc)
            emit_C(k, yS, x_cur, W["ev_w_out%d" % i], W["ln_g%d" % layer], W["ln_b%d" % layer], x_next)
            if layer < nlayers - 1:
                k.end_phase()
            x_cur = x_next
        else:
            x_next = out if layer == nlayers - 1 else xb_
            emit_D(k, x_cur, W["od_w_in%d" % i], W["og%d" % i], W["ob%d" % i], W["sgw_t%d" % i], W["sgb%d" % i], triu,
                   W["od_w_out%d" % i], W["ln_g%d" % layer], W["ln_b%d" % layer], ident, x_next)
            if layer < nlayers - 1:
                k.end_phase()
            x_cur = x_next
    k.finish()
    return nc


_NC = {}
NLAYERS = 4


def kernel(x, rel_bias_table, ln_g, ln_b, ev_w_in, ev_conv_w, ev_cmp_pos, ev_cmp_w1, ev_cmp_w2, ev_w_out,
           od_w_in, od_ln_g, od_ln_b, od_sgu_w, od_sgu_b, od_w_out):
    f32 = np.float32
    A = lambda a: np.ascontiguousarray(np.asarray(a, dtype=f32))
    x = A(x).reshape(8, 2048, 2048)
    rel = A(rel_bias_table)
    consts = host_consts_B()
    common = {"ident": consts["ident"], "triu": np.triu(np.ones((128, 128), f32))}
    for n, v in consts.items():
        if n != "ident":
            common["c_" + n] = v
    gperm = np.arange(EVEN_IN)
    gidx = np.arange(24).reshape(3, 2, 4).transpose(1, 0, 2).reshape(-1)
    gperm[6656:6680] = 6656 + gidx
    operm = np.concatenate([np.arange(0, 512), np.arange(1024, 1536), np.arange(512, 1024), np.arange(1536, 2048)])
    convw = {}
    wout = {}
    win = {}
    swap = np.arange(EVEN_IN)
    def _sw(a0, b0, n):
        swap[a0:a0 + n] = np.arange(b0, b0 + n)
        swap[b0:b0 + n] = np.arange(a0, a0 + n)
    for j in range(4):
        _sw(j * 1024, j * 1024 + 512, 512)
    _sw(4096, 4608, 512)
    for base in (5120, 5376, 5632, 5888, 6144, 6400):
        _sw(base, base + 128, 128)
    _sw(6656, 6668, 12)
    _sw(6680, 7192, 512)
    for i in range(2):
        w_can = np.asarray(ev_w_in[i], dtype=f32)[:, gperm]
        win[i] = [A(w_can), A(w_can[:, swap])]
        wo_can = np.asarray(ev_w_out[i], dtype=f32)
        wout[i] = [A(wo_can[np.concatenate([np.arange(0, 512), np.arange(512, 1024), np.arange(1024, 2048)])]),
                   A(wo_can[np.concatenate([np.arange(512, 1024), np.arange(0, 512), np.arange(1024, 2048)])])]
        cw = np.asarray(ev_conv_w[i], dtype=f32)
        cwt = cw.T
        convw[i] = [A(cwt), A(np.concatenate([cwt[512:1024], cwt[0:512]], axis=0))]
        common["cmp_pos%d" % i] = A(ev_cmp_pos[i])
        common["cmp_w1%d" % i] = A(ev_cmp_w1[i])
        common["cmp_w2%d" % i] = A(ev_cmp_w2[i])
        common["od_w_in%d" % i] = A(od_w_in[i])
        common["od_w_out%d" % i] = A(od_w_out[i])
        common["og%d" % i] = A(od_ln_g[i])
        common["ob%d" % i] = A(od_ln_b[i])
        common["sgw_t%d" % i] = A(np.asarray(od_sgu_w[i], dtype=f32).transpose(0, 2, 1))
        common["sgb%d" % i] = A(np.asarray(od_sgu_b[i], dtype=f32).reshape(-1))
    for l in range(4):
        common["ln_g%d" % l] = A(ln_g[l])
        common["ln_b%d" % l] = A(ln_b[l])
    if "nc" not in _NC:
        _NC["nc"] = build_fused(NLAYERS)
    in_maps = []
    for c in range(8):
        d = dict(common)
        d["x"] = np.ascontiguousarray(x[c])
        g_ = c % 2
        d["sel"] = np.array([[g_ * 2048, (1 - g_) * 2048, 1 - g_, g_ * 2048, 0, 0, 0, 0]], dtype=np.int32)
        for i in range(2):
            d["conv_w%d" % i] = convw[i][g_]
            d["ev_w_out%d" % i] = wout[i][g_]
            d["ev_w_in%d" % i] = win[i][g_]
        d["mh"] = np.full((128, 1), float(g_), dtype=f32)
        d["table"] = A(rel[:, 4 * g_:4 * g_ + 4])
        d["table31"] = A(np.tile(rel[31:32, 4 * g_:4 * g_ + 4], (32, 1)))
        in_maps.append(d)
    res = run_bass_kernel_spmd(_NC["nc"], in_maps, core_ids=list(range(8)))
    return np.stack([np.asarray(res.results[c]["out"]) for c in range(8)]).reshape(4, 4096, 2048).astype(f32)
```
